# Optimizing a Trainium2 kernel written in Bass

```python
import math
import jax, jax.numpy as jnp
from jax import lax
import numpy as np

D_MODEL = 1024
BATCH = 4
SEQ = 4096
DEPTH = 2
DEC_BATCH = 1
DEC_SEQ = 16384
PAST_LEN = 128

GRID_W = 64
N_MEM = 256
EPS = 1e-6
ROPE_THETA = 10000.0
Q_BLOCK = 128
NEG_BIG = -1e30
GATE_CLIP = 30.0

A_HEADS = 8
A_KV_HEADS = 2
A_HEAD_DIM = 128
A_Q_WIDTH = A_HEADS * A_HEAD_DIM
A_KV_WIDTH = A_KV_HEADS * A_HEAD_DIM

B_PATTERNS = ((128, 1), (512, 4), (2048, 16))
B_GROUPS = 3
B_HEADS_PER_GROUP = 4
B_HEAD_DIM = 64
B_WIDTH = B_GROUPS * B_HEADS_PER_GROUP * B_HEAD_DIM
B_OUT = B_HEADS_PER_GROUP * B_HEAD_DIM

C_HEADS = 8
C_HEAD_DIM = 128
C_WIDTH = C_HEADS * C_HEAD_DIM
C_CHUNK = 64

X_HEADS = 4
X_HEAD_DIM = D_MODEL // X_HEADS

D_FF = 2816
CONV_W = 3

N_BRANCH = 3
IN_WIDTHS = (A_Q_WIDTH, A_KV_WIDTH, A_KV_WIDTH, B_WIDTH, B_WIDTH, B_WIDTH,
             C_WIDTH, C_WIDTH, C_WIDTH, C_WIDTH, C_WIDTH, N_BRANCH * D_MODEL)
IN_WIDTH = A_Q_WIDTH + 2 * A_KV_WIDTH + 3 * B_WIDTH + 5 * C_WIDTH + N_BRANCH * D_MODEL

kernel_name = 'hybrid_gated_parallel_encoder'


def rms_norm(x, g):
    xf = x.astype(jnp.float32)
    y = xf * lax.rsqrt(jnp.mean(xf * xf, axis=-1, keepdims=True) + EPS)
    return (y * g.astype(jnp.float32)).astype(x.dtype)


def rope_tables(pos, dim):
    inv = jnp.power(ROPE_THETA, -(jnp.arange(0, dim, 2, dtype=jnp.float32) / dim))
    ang = pos.astype(jnp.float32)[:, None] * inv[None, :]
    return jnp.cos(ang), jnp.sin(ang)


def apply_rope(x, cos, sin):
    shape = (cos.shape[0],) + (1,) * (x.ndim - 3) + (cos.shape[1],)
    c = cos.reshape(shape)
    s = sin.reshape(shape)
    x1, x2 = jnp.split(x.astype(jnp.float32), 2, axis=-1)
    return jnp.concatenate([x1 * c - x2 * s, x2 * c + x1 * s], axis=-1).astype(x.dtype)


def gqa_axial_attention(aq, ak, av, gq, gk):
    Bsz, S, _ = aq.shape
    q = rms_norm(aq.reshape(Bsz, S, A_HEADS, A_HEAD_DIM), gq)
    k = rms_norm(ak.reshape(Bsz, S, A_KV_HEADS, A_HEAD_DIM), gk)
    v = av.reshape(Bsz, S, A_KV_HEADS, A_HEAD_DIM)
    rows = S // GRID_W
    row = jnp.repeat(jnp.arange(rows), GRID_W)
    col = jnp.tile(jnp.arange(GRID_W), rows)
    half = A_HEAD_DIM // 2
    cr, sr = rope_tables(row, half)
    cc, sc = rope_tables(col, half)

    def axial(t):
        return jnp.concatenate([apply_rope(t[..., :half], cr, sr),
                                apply_rope(t[..., half:], cc, sc)], axis=-1)

    q, k = axial(q), axial(k)
    G = A_HEADS // A_KV_HEADS
    nb = S // Q_BLOCK
    qb = q.reshape(Bsz, nb, Q_BLOCK, A_KV_HEADS, G, A_HEAD_DIM).transpose(1, 0, 2, 3, 4, 5)
    scale = A_HEAD_DIM ** -0.5

    def block(qblk):
        s = jnp.einsum('bqkgd,bskd->bkgqs', qblk, k, preferred_element_type=jnp.float32) * scale
        p = jax.nn.softmax(s, axis=-1).astype(v.dtype)
        return jnp.einsum('bkgqs,bskd->bqkgd', p, v)

    o = lax.map(block, qb)
    return o.transpose(1, 0, 2, 3, 4, 5).reshape(Bsz, S, A_Q_WIDTH)


def dilated_window_attention(bq, bk, bv, gq, gk):
    Bsz, S, _ = bq.shape
    shp = (Bsz, S, B_GROUPS, B_HEADS_PER_GROUP, B_HEAD_DIM)
    cos, sin = rope_tables(jnp.arange(S), B_HEAD_DIM)
    q = apply_rope(rms_norm(bq.reshape(shp), gq[:, None, :]), cos, sin)
    k = apply_rope(rms_norm(bk.reshape(shp), gk[:, None, :]), cos, sin)
    v = bv.reshape(shp)
    nb = S // Q_BLOCK
    t = jnp.arange(S)
    scale = B_HEAD_DIM ** -0.5
    outs, lses = [], []
    for g, (window, dil) in enumerate(B_PATTERNS):
        n_side = window // (2 * dil)
        n_keys = 2 * n_side + 1
        offs = dil * jnp.arange(-n_side, n_side + 1)
        pos = t[:, None] + offs[None, :]
        valid = ((pos >= 0) & (pos < S)).reshape(nb, Q_BLOCK, n_keys)
        idx = jnp.clip(pos, 0, S - 1).reshape(nb, Q_BLOCK, n_keys)
        qg = q[:, :, g].reshape(Bsz, nb, Q_BLOCK, B_HEADS_PER_GROUP, B_HEAD_DIM).swapaxes(0, 1)
        kg = k[:, :, g]
        vg = v[:, :, g]

        def block(args):
            qblk, iblk, mblk = args
            kk = jnp.take(kg, iblk, axis=1)
            vv = jnp.take(vg, iblk, axis=1)
            s = jnp.einsum('bqhd,bqkhd->bhqk', qblk, kk, preferred_element_type=jnp.float32) * scale
            s = jnp.where(mblk[None, None], s, NEG_BIG)
            m = jnp.max(s, axis=-1, keepdims=True)
            e = jnp.where(mblk[None, None], jnp.exp(s - m), 0.0)
            den = jnp.sum(e, axis=-1)
            lse = m[..., 0] + jnp.log(den)
            p = (e / den[..., None]).astype(vv.dtype)
            return jnp.einsum('bhqk,bqkhd->bqhd', p, vv), lse

        o, lse = lax.map(block, (qg, idx, valid))
        outs.append(o.swapaxes(0, 1).reshape(Bsz, S, B_HEADS_PER_GROUP, B_HEAD_DIM))
        lses.append(lse.transpose(1, 0, 3, 2).reshape(Bsz, S, B_HEADS_PER_GROUP))
    w = jax.nn.softmax(jnp.stack(lses, axis=0), axis=0)
    o = jnp.einsum('gbsh,gbshd->bshd', w.astype(v.dtype), jnp.stack(outs, axis=0))
    return o.reshape(Bsz, S, B_OUT)


def layer_lower_bound(raw, layer):
    p = jax.nn.softmax(raw.astype(jnp.float32), axis=0)
    return (jnp.cumsum(p, axis=0) - p[0])[layer]


def forget_gate(z, lb):
    z = jnp.clip(z.astype(jnp.float32), -GATE_CLIP, GATE_CLIP)
    f = lb + (1.0 - lb) * jax.nn.sigmoid(z)
    log_f = jnp.log(f)
    k = (1.0 - lb) * jax.nn.sigmoid(-z)
    return log_f, k


def hgrn2_chunk_scan(q, k, i, log_f):
    Bsz, S, H, Dk = q.shape
    Dv = i.shape[-1]
    nc = S // C_CHUNK

    def to_chunks(a):
        return a.reshape(Bsz, nc, C_CHUNK, H, a.shape[-1]).transpose(1, 0, 3, 2, 4)

    mask = jnp.tril(jnp.ones((C_CHUNK, C_CHUNK), dtype=bool))[:, :, None]

    def step(state, inp):
        qc, kc, ic, lfc = inp
        b = jnp.cumsum(lfc, axis=2)
        diff = b[:, :, :, None, :] - b[:, :, None, :, :]
        decay = jnp.where(mask, jnp.exp(jnp.where(mask, diff, 0.0)), 0.0)
        scores = jnp.einsum('bhtc,bhsc,bhtsc->bhts', qc, kc, decay)
        o = (jnp.einsum('bhts,bhsv->bhtv', scores, ic)
             + jnp.einsum('bhtc,bhcv->bhtv', qc * jnp.exp(b), state))
        b_last = b[:, :, -1:, :]
        state = (jnp.exp(b_last[:, :, 0, :])[..., None] * state
                 + jnp.einsum('bhsc,bhsv->bhcv', kc * jnp.exp(b_last - b), ic))
        return state, o

    state0 = jnp.zeros((Bsz, H, Dk, Dv), jnp.float32)
    _, o = lax.scan(step, state0, (to_chunks(q), to_chunks(k), to_chunks(i), to_chunks(log_f)))
    return o.transpose(1, 0, 3, 2, 4).reshape(Bsz, S, H, Dv)


def hgrn2_bidirectional(cq, ci, cff, cfb, cg, lb_fwd_raw, lb_bwd_raw, gnorm, layer):
    Bsz, S, _ = cq.shape
    hs = (Bsz, S, C_HEADS, C_HEAD_DIM)
    q = jax.nn.silu(cq.astype(jnp.float32)).reshape(hs)
    i = ci.astype(jnp.float32).reshape(hs)
    logf_f, k_f = forget_gate(cff, layer_lower_bound(lb_fwd_raw, layer))
    logf_b, k_b = forget_gate(cfb, layer_lower_bound(lb_bwd_raw, layer))
    o_f = hgrn2_chunk_scan(q, k_f.reshape(hs), i, logf_f.reshape(hs))
    fl = lambda a: jnp.flip(a, axis=1)
    o_b = fl(hgrn2_chunk_scan(fl(q), fl(k_b.reshape(hs)), fl(i), fl(logf_b.reshape(hs))))
    o = rms_norm((o_f + o_b).reshape(Bsz, S, C_WIDTH), gnorm) * jax.nn.silu(cg.astype(jnp.float32))
    return o.astype(cq.dtype)


def parallel_mixer(u, p, layer):
    splits = np.cumsum(np.array(IN_WIDTHS))[:-1].tolist()
    proj = u @ p['w_in'][layer]
    aq, ak, av, bq, bk, bv, cq, ci, cff, cfb, cg, gates = jnp.split(proj, splits, axis=-1)
    y_a = gqa_axial_attention(aq, ak, av, p['a_gq'][layer], p['a_gk'][layer]) @ p['w_br_a'][layer]
    y_b = dilated_window_attention(bq, bk, bv, p['b_gq'][layer], p['b_gk'][layer]) @ p['w_br_b'][layer]
    y_c = hgrn2_bidirectional(cq, ci, cff, cfb, cg, p['c_lb_fwd'], p['c_lb_bwd'],
                              p['c_gnorm'][layer], layer) @ p['w_br_c'][layer]
    g_a, g_b, g_c = jnp.split(jax.nn.sigmoid(gates), N_BRANCH, axis=-1)
    merged = g_a * y_a + g_b * y_b + g_c * y_c
    return merged @ p['w_mix_out'][layer]


def memory_cross_attention(u, mem_n, p, layer):
    Bsz, S, _ = u.shape
    M = mem_n.shape[1]
    q = rms_norm((u @ p['x_wq'][layer]).reshape(Bsz, S, X_HEADS, X_HEAD_DIM), p['x_gq'][layer])
    k, v = jnp.split(mem_n @ p['x_wkv'][layer], 2, axis=-1)
    k = rms_norm(k.reshape(Bsz, M, X_HEADS, X_HEAD_DIM), p['x_gk'][layer])
    v = v.reshape(Bsz, M, X_HEADS, X_HEAD_DIM)
    s = jnp.einsum('bshd,bmhd->bhsm', q, k, preferred_element_type=jnp.float32) * (X_HEAD_DIM ** -0.5)
    pr = jax.nn.softmax(s, axis=-1).astype(v.dtype)
    o = jnp.einsum('bhsm,bmhd->bshd', pr, v).reshape(Bsz, S, D_MODEL)
    return o @ p['x_wo'][layer]


def conv_ffn(u, p, layer):
    h = u @ p['f_wup'][layer]
    c = h.shape[-1]
    rhs = p['f_conv'][layer][:, None, :].astype(h.dtype)
    h = lax.conv_general_dilated(h, rhs, window_strides=(1,),
                                 padding=((CONV_W // 2, CONV_W // 2),),
                                 dimension_numbers=('NWC', 'WIO', 'NWC'),
                                 feature_group_count=c) + p['f_conv_b'][layer]
    a, g = jnp.split(h, 2, axis=-1)
    return (a * jax.nn.silu(g)) @ p['f_wdown'][layer]


def run_trunk(x, mem, p):
    for layer in range(DEPTH):
        x = x + parallel_mixer(rms_norm(x, p['g_mix'][layer]), p, layer)
        x = x + memory_cross_attention(rms_norm(x, p['g_cross'][layer]),
                                       rms_norm(mem, p['g_mem'][layer]), p, layer)
        x = x + conv_ffn(rms_norm(x, p['g_ffn'][layer]), p, layer)
    return x


def setup_inputs(seed: int = 0) -> dict:
    key = jax.random.key(seed)
    ks = iter(jax.random.split(key, 40))

    def nrm(shape, scale):
        return jax.random.normal(next(ks), shape, jnp.float32) * scale

    def gain(shape):
        return 1.0 + nrm(shape, 0.02)

    D = D_MODEL
    return {
        'x_prompt': nrm((BATCH, SEQ, D), 1.0),
        'x_sample': nrm((DEC_BATCH, DEC_SEQ, D), 1.0),
        'mem_prompt': nrm((BATCH, N_MEM, D), 1.0),
        'mem_sample': nrm((DEC_BATCH, N_MEM, D), 1.0),
        'g_mix': gain((DEPTH, D)),
        'w_in': nrm((DEPTH, D, IN_WIDTH), D ** -0.5),
        'a_gq': gain((DEPTH, A_HEAD_DIM)),
        'a_gk': gain((DEPTH, A_HEAD_DIM)),
        'b_gq': gain((DEPTH, B_GROUPS, B_HEAD_DIM)),
        'b_gk': gain((DEPTH, B_GROUPS, B_HEAD_DIM)),
        'c_lb_fwd': nrm((DEPTH, C_WIDTH), 0.5),
        'c_lb_bwd': nrm((DEPTH, C_WIDTH), 0.5),
        'c_gnorm': gain((DEPTH, C_WIDTH)),
        'w_br_a': nrm((DEPTH, A_Q_WIDTH, D), A_Q_WIDTH ** -0.5),
        'w_br_b': nrm((DEPTH, B_OUT, D), B_OUT ** -0.5),
        'w_br_c': nrm((DEPTH, C_WIDTH, D), C_WIDTH ** -0.5),
        'w_mix_out': nrm((DEPTH, D, D), D ** -0.5),
        'g_cross': gain((DEPTH, D)),
        'g_mem': gain((DEPTH, D)),
        'x_wq': nrm((DEPTH, D, D), D ** -0.5),
        'x_wkv': nrm((DEPTH, D, 2 * D), D ** -0.5),
        'x_gq': gain((DEPTH, X_HEAD_DIM)),
        'x_gk': gain((DEPTH, X_HEAD_DIM)),
        'x_wo': nrm((DEPTH, D, D), D ** -0.5),
        'g_ffn': gain((DEPTH, D)),
        'f_wup': nrm((DEPTH, D, 2 * D_FF), D ** -0.5),
        'f_conv': nrm((DEPTH, CONV_W, 2 * D_FF), CONV_W ** -0.5),
        'f_conv_b': nrm((DEPTH, 2 * D_FF), 0.01),
        'f_wdown': nrm((DEPTH, D_FF, D), D_FF ** -0.5),
    }


def reference(x_prompt, x_sample, mem_prompt, mem_sample, g_mix, w_in, a_gq, a_gk, b_gq, b_gk,
              c_lb_fwd, c_lb_bwd, c_gnorm, w_br_a, w_br_b, w_br_c, w_mix_out, g_cross, g_mem,
              x_wq, x_wkv, x_gq, x_gk, x_wo, g_ffn, f_wup, f_conv, f_conv_b, f_wdown):
    p = dict(g_mix=g_mix, w_in=w_in, a_gq=a_gq, a_gk=a_gk, b_gq=b_gq, b_gk=b_gk,
             c_lb_fwd=c_lb_fwd, c_lb_bwd=c_lb_bwd, c_gnorm=c_gnorm, w_br_a=w_br_a, w_br_b=w_br_b,
             w_br_c=w_br_c, w_mix_out=w_mix_out, g_cross=g_cross, g_mem=g_mem, x_wq=x_wq,
             x_wkv=x_wkv, x_gq=x_gq, x_gk=x_gk, x_wo=x_wo, g_ffn=g_ffn, f_wup=f_wup,
             f_conv=f_conv, f_conv_b=f_conv_b, f_wdown=f_wdown)
    y_prompt = run_trunk(x_prompt, mem_prompt, p)
    y_sample = run_trunk(x_sample, mem_sample, p)
    return (y_prompt, y_sample)
```

```python
import numpy as np
import ml_dtypes
from contextlib import ExitStack
import concourse.bass as bass
import concourse.mybir as mybir
from concourse.bass_utils import run_bass_kernel_spmd

F32 = mybir.dt.float32
BF16 = mybir.dt.bfloat16
AF = mybir.ActivationFunctionType
ALU = mybir.AluOpType
AX = mybir.AxisListType

NCORES = 8
D = 1024
NL = 2048
NTOK = 2 * NL
NT = NTOK // 128
INW = 12032
DFF = 2816
EPS = 1e-6
SEG_S = (4096, 16384)
PAIRS = [[0, 1], [2, 3], [4, 5], [6, 7]]
ALL8 = [list(range(8))]

TM_AQ, TM_AK, TM_AV, TM_BQ, TM_BK, TM_BV, TM_CI, TM_CG, TM_GT = 0, 1024, 1280, 1536, 2304, 3072, 3840, 4864, 5888
TMW = 8960


class Tile:
    def __init__(self, h, name):
        self.h = h; self.name = name
        self.writers = {}; self.readers = {}; self.dsem = None

    def __getitem__(self, idx):
        return self.h[idx]


class FW:
    def __init__(self, nc, es):
        self.nc = nc
        self.engs = {'pe': nc.tensor, 'act': nc.scalar, 'dve': nc.vector, 'pool': nc.gpsimd, 'sp': nc.sync}
        self.sems = {}; self.cnt = {}
        self.seen = {e: {} for e in self.engs}
        for e in self.engs:
            self.sems[e] = es.enter_context(nc.semaphore("s_" + e)); self.cnt[e] = 0
        self.ndpool = 72
        for i in range(self.ndpool):
            k = "d%d" % i
            self.sems[k] = es.enter_context(nc.semaphore("s_" + k)); self.cnt[k] = 0
        self.nd = 0
        self.ninst = 0; self.nwaits = 0
        self.uid = 0

    def sbuf(self, es, name, shape, dt):
        self.uid += 1
        return Tile(es.enter_context(self.nc.sbuf_tensor("%s_%d" % (name, self.uid), list(shape), dt)), name)

    def psum(self, es, name, shape, dt=F32):
        self.uid += 1
        return Tile(es.enter_context(self.nc.psum_tensor("%s_%d" % (name, self.uid), list(shape), dt)), name)

    def dram(self, name, shape, dt, kind=None):
        if kind is None and name in DEBUG_OUT:
            kind = "ExternalOutput"
        if kind is None:
            h = self.nc.dram_tensor(name, list(shape), dt)
        else:
            h = self.nc.dram_tensor(name, list(shape), dt, kind=kind)
        return Tile(h, name)

    def _dsem(self, t):
        if t.dsem is None:
            t.dsem = "d%d" % (self.nd % self.ndpool); self.nd += 1
        return t.dsem

    def _wait(self, e, key, c):
        if c <= 0 or self.seen[e].get(key, 0) >= c:
            return
        if e == 'pe' and key == 'pe':
            return
        self.engs[e].wait_ge(self.sems[key], c)
        self.seen[e][key] = c
        self.nwaits += 1

    def _deps(self, e, reads, writes):
        for t in reads:
            for k, c in t.writers.items():
                self._wait(e, k, self.cnt[k] if k[0] == 'd' else c)
        for t in writes:
            for k, c in t.writers.items():
                self._wait(e, k, self.cnt[k] if k[0] == 'd' else c)
            for k, c in t.readers.items():
                self._wait(e, k, self.cnt[k] if k[0] == 'd' else c)

    def _record(self, key, c, reads, writes):
        for t in reads:
            if t not in writes:
                t.readers[key] = c
        for t in writes:
            t.writers = {key: c}; t.readers = {}

    def op(self, e, fn, reads=(), writes=(), inc=True):
        self._deps(e, reads, writes)
        ins = fn(self.engs[e])
        self.ninst += 1
        inc = True
        if inc:
            self.cnt[e] += 1
            ins.then_inc(self.sems[e], 1)
            self._record(e, self.cnt[e], reads, writes)
        else:
            self._record(e, self.cnt[e] + 1, reads, writes)
        return ins

    def dma(self, q, out_ap, in_ap, reads=(), writes=(), **kw):
        self._deps(q, reads, writes)
        owner = (list(writes) + list(reads))[0]
        k = self._dsem(owner)
        ins = self.engs[q].dma_start(out=out_ap, in_=in_ap, **kw)
        self.cnt[k] += 16
        ins.then_inc(self.sems[k], 16)
        self.ninst += 1
        self._record(k, self.cnt[k], reads, writes)
        return ins

    def collective(self, kind, groups, tin, tout):
        e = 'pool'
        self._deps(e, [tin], [tout])
        self.cnt[e] += 1
        self.nc.gpsimd.collective_compute(kind, ALU.bypass, replica_groups=groups,
                                          ins=[tin.h.ap().opt()], outs=[tout.h.ap().opt()]).then_inc(self.sems[e], 1)
        self._record(e, self.cnt[e], [tin], [tout])

    def barrier(self):
        for e in self.engs:
            for k, c in self.cnt.items():
                self._wait(e, k, c)


class Ring:
    def __init__(self, tiles):
        self.t = tiles; self.i = 0

    def next(self):
        t = self.t[self.i % len(self.t)]; self.i += 1
        return t


def bc(ap, shape):
    return ap.broadcast_to(list(shape))


def build():
    nc = bass.Bass("TRN2", target_bir_lowering=False)
    top = ExitStack()
    f = FW(nc, top)
    I = {}

    def inp(name, shape, dt=F32):
        I[name] = f.dram(name, shape, dt, kind="ExternalInput")
        return I[name]

    x_in = inp("x", [NTOK, D])
    mem_in = inp("mem", [2 * 256, D])
    tabs = inp("tabs", [NTOK, 384])
    for nm, shp in (("g_mix", [2, D]), ("w_in", [2, D, INW]), ("a_gq", [2, 128]), ("a_gk", [2, 128]),
                    ("b_gq", [2, 192]), ("b_gk", [2, 192]), ("c_lb_fwd", [2, D]), ("c_lb_bwd", [2, D]),
                    ("c_gnorm", [2, D]), ("w_br_a", [2, D, D]), ("w_br_b", [2, 256, D]), ("w_br_c", [2, D, D]),
                    ("w_mix_out", [2, D, D]), ("g_cross", [2, D]), ("g_mem", [2, D]), ("x_wq", [2, D, D]),
                    ("x_wkv", [2, D, 2 * D]), ("x_gq", [2, 256]), ("x_gk", [2, 256]), ("x_wo", [2, D, D]),
                    ("g_ffn", [2, D]), ("f_wup", [2, D, 2 * DFF]), ("f_conv", [2, 3, 2 * DFF]),
                    ("f_conv_b", [2, 2 * DFF]), ("f_wdown", [2, DFF, D])):
        inp(nm, shp)
    ident_in = inp("ident", [128, 128], BF16)
    selI_in = inp("selI", [128, 32 * 128], BF16)
    bmask_in = inp("bmask", [128, 34 * 512], BF16)
    bvalid_in = inp("bvalid", [128, 2 * 32])
    cmask_in = inp("cmask", [128, 2 * 128], BF16)
    cscan_in = inp("cscan", [128, 16])
    chunkm_in = inp("chunkm", [128, NL])
    fsel_in = inp("fsel", [32, 4], BF16)
    y_out = f.dram("y", [NTOK, D], F32, kind="ExternalOutput")

    xres = f.dram("xres", [NTOK, D], F32)
    projTM = f.dram("projTM", [NTOK, TMW], BF16)
    projFM = f.dram("projFM", [24 * 128, NTOK], F32)
    qTA = f.dram("qTA", [8 * 128, NTOK], BF16)
    akin = [f.dram("akin%d" % s, [256, NL], BF16) for s in range(2)]
    avin = [f.dram("avin%d" % s, [NL, 256], BF16) for s in range(2)]
    akout = [f.dram("akout0", [2 * 256, NL], BF16), f.dram("akout1", [8 * 256, NL], BF16)]
    avout = [f.dram("avout0", [2 * NL, 256], BF16), f.dram("avout1", [8 * NL, 256], BF16)]
    oA = f.dram("oA", [NTOK, D], BF16)
    qTB = f.dram("qTB", [768, NTOK], BF16)
    bkin = f.dram("bkin", [2 * 768, NL], BF16)
    bvin = f.dram("bvin", [2 * NL, 768], BF16)
    bkout = f.dram("bkout", [8 * 2 * 768, NL], BF16)
    bvout = f.dram("bvout", [8 * 2 * NL, 768], BF16)
    extK = f.dram("extK", [2 * 768, 4096], BF16)
    extV = f.dram("extV", [2 * 4096, 768], BF16)
    oB = f.dram("oB", [NTOK, 256], BF16)
    cqT = f.dram("cqT", [2 * 1024, NTOK], BF16)
    ckT = f.dram("ckT", [2 * 1024, NTOK], BF16)
    ckTM = f.dram("ckTM", [2 * NTOK, 1024], BF16)
    cer = f.dram("cer", [256, 512], F32)
    cel = f.dram("cel", [256, 512], F32)
    cKI = f.dram("cKI", [2 * 2 * 32 * 128, 1024], F32)
    csin = f.dram("csin", [128, 4128], F32)
    csout = f.dram("csout", [1024, 4128], F32)
    coC = f.dram("coC", [2 * NTOK, 1024], F32)
    xmid = f.dram("xmid", [NTOK, D], F32)
    u3T = f.dram("u3T", [D, NTOK], BF16)
    fgin = f.dram("fgin", [4, D], BF16)
    fgout = f.dram("fgout", [32, D], BF16)
    hact = f.dram("hact", [DFF, NTOK], BF16)

    ident = f.sbuf(top, "ident", [128, 128], BF16)
    f.dma('sp', ident[:], ident_in[:, :], writes=[ident])

    zeroW = f.sbuf(top, "zeroW", [128, 128], BF16)
    f.op('dve', lambda e: e.memset(zeroW[:], 0.0), writes=[zeroW])

    def rms_rows(es, xt, width, rs, sq):
        f.op('dve', lambda e: e.tensor_tensor(sq[:, 0:width], xt[:, 0:width], xt[:, 0:width], ALU.mult),
             reads=[xt], writes=[sq])
        f.op('dve', lambda e: e.reduce_sum(rs[:, 0:1], sq[:, 0:width], AX.X), reads=[sq], writes=[rs])
        f.op('dve', lambda e: e.tensor_scalar(rs[:, 0:1], rs[:, 0:1], 1.0 / width, EPS, ALU.mult, ALU.add),
             reads=[rs], writes=[rs])
        f.op('act', lambda e: e.activation(out=rs[:, 0:1], in_=rs[:, 0:1], func=AF.Sqrt), reads=[rs], writes=[rs])
        f.op('dve', lambda e: e.reciprocal(rs[:, 0:1], rs[:, 0:1]), reads=[rs], writes=[rs])

    def transpose_blocks(src, nblk, psr, dst_fn, dst_tiles, src_off=0):
        k0 = 0
        gi = 0
        while k0 < nblk:
            n = min(4, nblk - k0)
            ps = psr.next()
            for k in range(n):
                f.op('pe', lambda e, k=k: e.matmul(ps[:, k, :], src[:, src_off + (k0 + k) * 128: src_off + (k0 + k + 1) * 128], ident[:],
                                                   start=True, stop=True),
                     reads=[src, ident], writes=[ps])
            if gi % 2 == 0:
                f.op('act', lambda e: e.copy(dst_fn(k0, n), ps[:, 0:n, :]), reads=[ps], writes=dst_tiles)
            else:
                f.op('dve', lambda e: e.tensor_copy(dst_fn(k0, n), ps[:, 0:n, :]), reads=[ps], writes=dst_tiles)
            k0 += n; gi += 1

    def load_bcast_row(es, name, src_ap_row, width, q='sp'):
        t = f.sbuf(es, name, [128, width], F32)
        f.dma(q, t[:], src_ap_row.partition_broadcast(128), writes=[t])
        return t

    for layer in range(2):
        xsrc = x_in if layer == 0 else xres

        with ExitStack() as es:
            uT = f.sbuf(es, "uT", [128, 8, NTOK], BF16)
            with ExitStack() as es2:
                gmix = load_bcast_row(es2, "gmix", I["g_mix"][layer:layer + 1, :], D)
                xin = Ring([f.sbuf(es2, "xin", [128, D], F32) for _ in range(2)])
                sq = f.sbuf(es2, "sq", [128, D], F32)
                rsr = Ring([f.sbuf(es2, "rs", [128, 1], F32) for _ in range(2)])
                unr = Ring([f.sbuf(es2, "un", [128, D], BF16) for _ in range(2)])
                pst = Ring([f.psum(es2, "pst", [128, 4, 128], F32) for _ in range(4)])
                for tt in range(NT):
                    xt = xin.next(); rs = rsr.next(); un = unr.next()
                    f.dma('sp', xt[:], xsrc[tt * 128:(tt + 1) * 128, :], writes=[xt])
                    rms_rows(es2, xt, D, rs, sq)
                    f.op('dve', lambda e: e.scalar_tensor_tensor(un[:], xt[:], rs[:, 0:1], gmix[:], ALU.mult, ALU.mult),
                         reads=[xt, rs, gmix], writes=[un])
                    transpose_blocks(un, 8, pst, lambda k0, n, tt=tt: uT[:, k0:k0 + n, tt * 128:(tt + 1) * 128], [uT])
                f.barrier()
            with ExitStack() as es2:
                w32 = Ring([f.sbuf(es2, "w32", [128, 8, 512], F32) for _ in range(2)])
                wbf = Ring([f.sbuf(es2, "wbf", [128, 8, 512], BF16) for _ in range(2)])
                psm = Ring([f.psum(es2, "psm", [128, 512], F32) for _ in range(4)])
                stg = Ring([f.sbuf(es2, "stg", [128, 512], BF16) for _ in range(4)])
                stgf = Ring([f.sbuf(es2, "stgf", [128, 512], F32) for _ in range(4)])
                wv = I["w_in"].h.ap()[layer].rearrange("(kc p) n -> p kc n", p=128)
                tm_blocks = []
                for (c0, n, t0) in ((0, 1024, TM_AQ), (1024, 512, TM_AK), (1536, 2304, TM_BQ), (4864, 1024, TM_CI),
                                    (7936, 4096, TM_CG)):
                    o = 0
                    while o < n:
                        w = min(512, n - o)
                        tm_blocks.append(('tm', c0 + o, w, t0 + o)); o += w
                fm_blocks = []
                for (c0, ch0) in ((3840, 0), (4352, 4), (5888, 8), (6400, 12), (6912, 16), (7424, 20)):
                    fm_blocks.append(('fm', c0, 512, ch0))
                evi = 0
                for (kind, c0, w, dst) in tm_blocks + fm_blocks:
                    wa = w32.next(); wb = wbf.next()
                    f.dma('sp', wa[:, :, 0:w], wv[:, :, c0:c0 + w], writes=[wa])
                    f.op('pool', lambda e: e.tensor_copy(wb[:, :, 0:w], wa[:, :, 0:w]), reads=[wa], writes=[wb])
                    if kind == 'tm':
                        for tt in range(NT):
                            ps = psm.next(); st = stg.next()
                            for kc in range(8):
                                f.op('pe', lambda e, kc=kc: e.matmul(ps[:, 0:w], uT[:, kc, tt * 128:(tt + 1) * 128], wb[:, kc, 0:w],
                                                                     start=(kc == 0), stop=(kc == 7)),
                                     reads=[uT, wb], writes=[ps], inc=(kc == 7))
                            if evi % 2 == 0:
                                f.op('act', lambda e: e.copy(st[:, 0:w], ps[:, 0:w]), reads=[ps], writes=[st])
                            else:
                                f.op('dve', lambda e: e.tensor_copy(st[:, 0:w], ps[:, 0:w]), reads=[ps], writes=[st])
                            evi += 1
                            f.dma('pool', projTM[tt * 128:(tt + 1) * 128, dst:dst + w], st[:, 0:w], reads=[st])
                    else:
                        for cc in range(4):
                            for tg in range(NTOK // 512):
                                ps = psm.next(); st = stgf.next()
                                for kc in range(8):
                                    f.op('pe', lambda e, kc=kc: e.matmul(ps[:], wb[:, kc, cc * 128:(cc + 1) * 128], uT[:, kc, tg * 512:(tg + 1) * 512],
                                                                         start=(kc == 0), stop=(kc == 7)),
                                         reads=[uT, wb], writes=[ps], inc=(kc == 7))
                                if evi % 2 == 0:
                                    f.op('act', lambda e: e.copy(st[:], ps[:]), reads=[ps], writes=[st])
                                else:
                                    f.op('dve', lambda e: e.tensor_copy(st[:], ps[:]), reads=[ps], writes=[st])
                                evi += 1
                                f.dma('pool', projFM[(dst + cc) * 128:(dst + cc + 1) * 128, tg * 512:(tg + 1) * 512], st[:], reads=[st])
                f.barrier()
            f.barrier()
        if layer == 0 and BUILD_UPTO == 1:
            break

        with ExitStack() as es:
            gA = f.sbuf(es, "gA", [128, 10, 128], F32)
            gB = f.sbuf(es, "gB", [128, 24, 64], F32)
            for h in range(10):
                srcg = I["a_gq"] if h < 8 else I["a_gk"]
                f.dma('sp', gA[:, h, :], srcg[layer:layer + 1, :].partition_broadcast(128), writes=[gA])
            for h in range(24):
                srcg = I["b_gq"] if h < 12 else I["b_gk"]
                g = (h % 12) // 4
                f.dma('sp', gB[:, h, :], srcg[layer:layer + 1, g * 64:(g + 1) * 64].partition_broadcast(128), writes=[gB])
            f.op('dve', lambda e: e.tensor_scalar(gA[:, 0:8, :], gA[:, 0:8, :], 128.0 ** -0.5, None, ALU.mult), reads=[gA], writes=[gA])
            f.op('dve', lambda e: e.tensor_scalar(gB[:, 0:12, :], gB[:, 0:12, :], 64.0 ** -0.5, None, ALU.mult), reads=[gB], writes=[gB])
            par = Ring([f.sbuf(es, "pa", [128, 3840], BF16) for _ in range(2)])
            tbr = Ring([f.sbuf(es, "tb", [128, 384], F32) for _ in range(2)])
            sqA = f.sbuf(es, "sqA", [128, 10, 128], F32)
            xnA = f.sbuf(es, "xnA", [128, 10, 128], F32)
            t2A = f.sbuf(es, "t2A", [128, 10, 128], F32)
            rsA = f.sbuf(es, "rsA", [128, 10], F32)
            roA = Ring([f.sbuf(es, "roA", [128, 1280], BF16) for _ in range(2)])
            sqB = f.sbuf(es, "sqB", [128, 24, 64], F32)
            xnB = f.sbuf(es, "xnB", [128, 24, 64], F32)
            t2B = f.sbuf(es, "t2B", [128, 24, 64], F32)
            rsB = f.sbuf(es, "rsB", [128, 24], F32)
            roB = Ring([f.sbuf(es, "roB", [128, 1536], BF16) for _ in range(2)])
            stA = Ring([f.sbuf(es, "stA", [128, 10, 128], BF16) for _ in range(2)])
            stB = Ring([f.sbuf(es, "stB", [128, 12, 128], BF16) for _ in range(2)])
            pst = Ring([f.psum(es, "pst2", [128, 4, 128], F32) for _ in range(4)])

            def normrope(E, src3, nh, hd, sq, xn, t2, rs, gt, ctab, stab, ro):
                f.op(E, lambda e: e.tensor_tensor(sq[:], src3, src3, ALU.mult), reads=[pa], writes=[sq])
                f.op('dve', lambda e: e.reduce_sum(rs[:], sq[:], AX.X), reads=[sq], writes=[rs])
                f.op('dve', lambda e: e.tensor_scalar(rs[:], rs[:], 1.0 / hd, EPS, ALU.mult, ALU.add), reads=[rs], writes=[rs])
                f.op('act', lambda e: e.activation(out=rs[:], in_=rs[:], func=AF.Sqrt), reads=[rs], writes=[rs])
                f.op('dve', lambda e: e.reciprocal(rs[:], rs[:]), reads=[rs], writes=[rs])
                f.op(E, lambda e: e.tensor_tensor(xn[:], src3, rs[:].unsqueeze(2).broadcast_to([128, nh, hd]), ALU.mult),
                     reads=[pa, rs], writes=[xn])
                f.op(E, lambda e: e.tensor_tensor(xn[:], xn[:], gt[:], ALU.mult), reads=[xn, gt], writes=[xn])
                hh = hd // 2 if hd == 64 else 32
                nb = hd // (2 * hh)
                xv = xn[:].rearrange("p h (b s j) -> p (h b) s j", b=nb, s=2)
                tv = t2[:].rearrange("p h (b s j) -> p (h b) s j", b=nb, s=2)
                if nb == 1:
                    sv = stab.rearrange("p (s j) -> p s j", s=2)
                    s0 = sv[:, 0:1, :].broadcast_to([128, nh, hh]); s1 = sv[:, 1:2, :].broadcast_to([128, nh, hh])
                    f.op(E, lambda e: e.tensor_tensor(tv[:, :, 0, :], xv[:, :, 1, :], s0, ALU.mult), reads=[xn, tb], writes=[t2])
                    f.op(E, lambda e: e.tensor_tensor(tv[:, :, 1, :], xv[:, :, 0, :], s1, ALU.mult), reads=[xn, tb], writes=[t2])
                else:
                    xv4 = xn[:].rearrange("p h (b s j) -> p h b s j", b=nb, s=2)
                    tv4 = t2[:].rearrange("p h (b s j) -> p h b s j", b=nb, s=2)
                    sv = stab.rearrange("p (b s j) -> p b s j", b=nb, s=2)
                    for bb in range(nb):
                        s0 = sv[:, bb, 0:1, :].broadcast_to([128, nh, hh]); s1 = sv[:, bb, 1:2, :].broadcast_to([128, nh, hh])
                        f.op(E, lambda e: e.tensor_tensor(tv4[:, :, bb, 0, :], xv4[:, :, bb, 1, :], s0, ALU.mult), reads=[xn, tb], writes=[t2])
                        f.op(E, lambda e: e.tensor_tensor(tv4[:, :, bb, 1, :], xv4[:, :, bb, 0, :], s1, ALU.mult), reads=[xn, tb], writes=[t2])
                f.op(E, lambda e: e.tensor_tensor(xn[:], xn[:], ctab.unsqueeze(1).broadcast_to([128, nh, hd]), ALU.mult),
                     reads=[xn, tb], writes=[xn])
                f.op(E, lambda e: e.tensor_tensor(ro[:].rearrange("p (h d) -> p h d", d=hd), xn[:], t2[:], ALU.add),
                     reads=[xn, t2], writes=[ro])

            for tt in range(NT):
                seg, lt = tt // 16, tt % 16
                pa = par.next(); tb = tbr.next(); ra = roA.next(); rb = roB.next(); sa = stA.next(); sb = stB.next()
                f.dma('sp', pa[:], projTM[tt * 128:(tt + 1) * 128, 0:3840], writes=[pa])
                f.dma('sp', tb[:], tabs[tt * 128:(tt + 1) * 128, :], writes=[tb])
                normrope('dve', pa[:, 0:1280].rearrange("p (h d) -> p h d", d=128), 10, 128, sqA, xnA, t2A, rsA, gA,
                         tb[:, 0:128], tb[:, 128:256], ra)
                normrope('pool', pa[:, 1536:3072].rearrange("p (h d) -> p h d", d=64), 24, 64, sqB, xnB, t2B, rsB, gB,
                         tb[:, 256:320], tb[:, 320:384], rb)
                transpose_blocks(ra, 10, pst, lambda k0, n: sa[:, k0:k0 + n, :], [sa])
                transpose_blocks(rb, 12, pst, lambda k0, n: sb[:, k0:k0 + n, :], [sb])
                tc = slice(tt * 128, (tt + 1) * 128); lc = slice(lt * 128, (lt + 1) * 128)
                f.dma('act', qTA.h.ap().rearrange("(h d) t -> d h t", d=128)[:, :, tc], sa[:, 0:8, :], reads=[sa])
                f.dma('act', akin[seg].h.ap().rearrange("(h d) t -> d h t", d=128)[:, :, lc], sa[:, 8:10, :], reads=[sa])
                f.dma('act', avin[seg][lc, :], pa[:, 1280:1536], reads=[pa])
                f.dma('act', qTB.h.ap().rearrange("(j p) t -> p j t", p=128)[:, :, tc], sb[:, 0:6, :], reads=[sb])
                f.dma('act', bkin.h.ap()[seg * 768:(seg + 1) * 768].rearrange("(j p) t -> p j t", p=128)[:, :, lc], sb[:, 6:12, :], reads=[sb])
                f.dma('act', bvin[seg * NL + lt * 128: seg * NL + (lt + 1) * 128, :], pa[:, 3072:3840], reads=[pa])
            f.barrier()
        f.collective("AllGather", PAIRS, akin[0], akout[0])
        f.collective("AllGather", PAIRS, avin[0], avout[0])
        f.collective("AllGather", ALL8, akin[1], akout[1])
        f.collective("AllGather", ALL8, avin[1], avout[1])
        f.collective("AllGather", ALL8, bkin, bkout)
        f.collective("AllGather", ALL8, bvin, bvout)
        f.barrier()
        if layer == 0 and BUILD_UPTO == 3:
            break

        with ExitStack() as es:
            kT = f.sbuf(es, "kT", [128, 16384], BF16)
            vaug = f.sbuf(es, "vaug", [128, 128, 129], BF16)
            f.op('dve', lambda e: e.memset(vaug[:, :, 128:129], 1.0), writes=[vaug])
            qtr = Ring([f.sbuf(es, "qt", [128, 512], BF16) for _ in range(2)])
            psS = Ring([f.psum(es, "psS", [128, 2, 512], F32) for _ in range(3)])
            acc = f.psum(es, "acc", [128, 4, 256], F32)
            pTr = Ring([f.sbuf(es, "pT", [128, 2, 512], BF16) for _ in range(3)])
            rcr = Ring([f.sbuf(es, "rc", [128, 4], F32) for _ in range(2)])
            ostr = Ring([f.sbuf(es, "ost", [128, 4, 128], BF16) for _ in range(2)])
            for seg in range(2):
                S = SEG_S[seg]; nkt = S // 128; R = S // NL
                for kvh in range(2):
                    f.dma('sp', kT[:, 0:S].rearrange("p (r t) -> p r t", r=R),
                          akout[seg].h.ap().rearrange("(r hd) t -> hd r t", hd=256)[kvh * 128:(kvh + 1) * 128], writes=[kT])
                    vsrc = avout[seg].h.ap().rearrange("(j p) c -> p j c", p=128)
                    for j0 in range(0, nkt, 16):
                        f.dma('sp', vaug[:, j0:j0 + 16, 0:128], vsrc[:, j0:j0 + 16, kvh * 128:(kvh + 1) * 128], writes=[vaug])
                    for g in range(4):
                        h = kvh * 4 + g
                        for qb in range(4):
                            qt = qtr.next()
                            q0 = seg * NL + qb * 512
                            f.dma('sp', qt[:], qTA[h * 128:(h + 1) * 128, q0:q0 + 512], writes=[qt])
                            ngrp = nkt // 2
                            for bnk in range(2):
                                f.op('pe', lambda e: e.matmul(acc[:, 2 * bnk:2 * bnk + 2, :].rearrange("p a b -> p (a b)"), zeroW[:], qt[:], start=True, stop=False),
                                     reads=[zeroW, qt], writes=[acc])
                            for kg in range(ngrp):
                                ps = psS.next(); pT = pTr.next()
                                for j in range(2):
                                    kt = 2 * kg + j
                                    f.op('pe', lambda e: e.matmul(ps[:, j, :], kT[:, kt * 128:(kt + 1) * 128], qt[:], start=True, stop=True),
                                         reads=[kT, qt], writes=[ps])
                                f.op('act', lambda e: e.activation(out=pT[:], in_=ps[:], func=AF.Exp), reads=[ps], writes=[pT])
                                for j in range(2):
                                    kt = 2 * kg + j
                                    for qs in range(4):
                                        f.op('pe', lambda e: e.matmul(acc[:, qs, 0:129], pT[:, j, qs * 128:(qs + 1) * 128], vaug[:, kt, :],
                                                                      start=False, stop=(kt == nkt - 1)),
                                             reads=[pT, vaug], writes=[acc])
                            rc = rcr.next(); ost = ostr.next()
                            f.op('dve', lambda e: e.reciprocal(rc[:], acc[:, :, 128]), reads=[acc], writes=[rc])
                            f.op('dve', lambda e: e.tensor_tensor(ost[:], acc[:, :, 0:128], rc[:].unsqueeze(2).broadcast_to([128, 4, 128]), ALU.mult),
                                 reads=[acc, rc], writes=[ost])
                            f.dma('act', oA.h.ap()[q0:q0 + 512, h * 128:(h + 1) * 128].rearrange("(qs p) d -> p qs d", p=128), ost[:], reads=[ost])
            f.barrier()
        if layer == 0 and BUILD_UPTO == 4:
            break

        with ExitStack() as es:
            selI = f.sbuf(es, "selI", [128, 32, 128], BF16)
            f.dma('sp', selI[:], selI_in.h.ap().rearrange("p (i m) -> p i m", m=128), writes=[selI])
            candK = Ring([f.sbuf(es, "candK", [128, 1024], BF16) for _ in range(4)])
            candV = Ring([f.sbuf(es, "candV", [128, 768], BF16) for _ in range(4)])
            psE = Ring([f.psum(es, "psE", [128, 2, 512], F32) for _ in range(3)])
            steK = Ring([f.sbuf(es, "steK", [128, 2, 512], BF16) for _ in range(2)])
            steV = Ring([f.sbuf(es, "steV", [128, 2, 384], BF16) for _ in range(2)])
            for seg in range(2):
                f.dma('sp', extK[seg * 768:(seg + 1) * 768, 1024:3072], bkin[seg * 768:(seg + 1) * 768, :], reads=[bkin], writes=[extK])
                f.dma('sp', extV[seg * 4096 + 1024: seg * 4096 + 3072, :], bvin[seg * NL:(seg + 1) * NL, :], reads=[bvin], writes=[extV])
                for side in range(2):
                    scol = 1024 if side == 0 else 0
                    ecol = 0 if side == 0 else 3072
                    si = (seg * 2 + side) * 8
                    for hp in range(6):
                        ps = psE.next()
                        for r in range(8):
                            cd = candK.next()
                            r0 = r * 1536 + seg * 768 + hp * 128
                            f.dma('sp', cd[:], bkout[r0:r0 + 128, scol:scol + 1024], writes=[cd])
                            for j in range(2):
                                f.op('pe', lambda e: e.matmul(ps[:, j, :], selI[:, si + r, :], cd[:, j * 512:(j + 1) * 512], start=(r == 0), stop=(r == 7)),
                                     reads=[selI, cd], writes=[ps])
                        st = steK.next()
                        f.op('act', lambda e: e.copy(st[:], ps[:]), reads=[ps], writes=[st])
                        f.dma('act', extK[seg * 768 + hp * 128: seg * 768 + (hp + 1) * 128, ecol:ecol + 1024].rearrange("p (j n) -> p j n", j=2), st[:], reads=[st])
                    for kt in range(8):
                        ps = psE.next()
                        for r in range(8):
                            cv = candV.next()
                            r0 = r * 4096 + seg * 2048 + scol + kt * 128
                            f.dma('sp', cv[:], bvout[r0:r0 + 128, :], writes=[cv])
                            for j in range(2):
                                f.op('pe', lambda e: e.matmul(ps[:, j, 0:384], selI[:, si + r, :], cv[:, j * 384:(j + 1) * 384], start=(r == 0), stop=(r == 7)),
                                     reads=[selI, cv], writes=[ps])
                        st = steV.next()
                        f.op('dve', lambda e: e.tensor_copy(st[:], ps[:, :, 0:384]), reads=[ps], writes=[st])
                        e0 = seg * 4096 + ecol + kt * 128
                        f.dma('act', extV[e0:e0 + 128, :].rearrange("p (j n) -> p j n", j=2), st[:], reads=[st])
            f.barrier()
        with ExitStack() as es:
            bmask = f.sbuf(es, "bmask", [128, 34, 512], BF16)
            f.dma('sp', bmask[:], bmask_in.h.ap().rearrange("p (i m) -> p i m", m=512), writes=[bmask])
            bval = f.sbuf(es, "bval", [128, 64], F32)
            f.dma('sp', bval[:], bvalid_in[:, :], writes=[bval])
            kTe = f.sbuf(es, "kTe", [128, 6, 4096], BF16)
            vext = f.sbuf(es, "vext", [128, 32, 12, 65], BF16)
            vraw = Ring([f.sbuf(es, "vraw", [128, 8, 768], BF16) for _ in range(2)])
            qtbr = Ring([f.sbuf(es, "qtb", [128, 6, 512], BF16) for _ in range(2)])
            psB = Ring([f.psum(es, "psB", [128, 512], F32) for _ in range(4)])
            accB = f.psum(es, "accB", [128, 4, 128], F32)
            pTb = Ring([f.sbuf(es, "pTb", [128, 512], BF16) for _ in range(3)])
            pmb = Ring([f.sbuf(es, "pmb", [128, 512], BF16) for _ in range(3)])
            rcb = Ring([f.sbuf(es, "rcb", [128, 4], F32) for _ in range(2)])
            ostb = Ring([f.sbuf(es, "ostb", [128, 4, 64], BF16) for _ in range(2)])
            for seg in range(2):
                f.dma('sp', kTe[:], extK.h.ap()[seg * 768:(seg + 1) * 768].rearrange("(j p) t -> p j t", p=128), writes=[kTe])
                for ch in range(4):
                    vr = vraw.next()
                    f.dma('sp', vr[:], extV.h.ap()[seg * 4096 + ch * 1024: seg * 4096 + (ch + 1) * 1024].rearrange("(j p) c -> p j c", p=128), writes=[vr])
                    f.op('pool', lambda e: e.tensor_copy(vext[:, ch * 8:(ch + 1) * 8, :, 0:64], vr[:].rearrange("p j (h d) -> p j h d", d=64)),
                         reads=[vr], writes=[vext])
                f.op('dve', lambda e: e.tensor_copy(vext[:, :, :, 64], bval[:, seg * 32:(seg + 1) * 32].unsqueeze(2).broadcast_to([128, 32, 12])),
                     reads=[bval], writes=[vext])
                for qb in range(4):
                    q0 = seg * NL + qb * 512
                    qtb = qtbr.next()
                    f.dma('sp', qtb[:], qTB.h.ap().rearrange("(j p) t -> p j t", p=128)[:, :, q0:q0 + 512], writes=[qtb])
                    for h in range(4):
                        tl = [(0, kt, kt - (7 + 4 * qb)) for kt in range(7 + 4 * qb, 13 + 4 * qb)]
                        tl += [(1, kt, 6 + kt - (6 + 4 * qb)) for kt in range(6 + 4 * qb, 14 + 4 * qb)]
                        tl += [(2, kt, 14 + kt - 4 * qb) for kt in range(4 * qb, 4 * qb + 20)]
                        f.op('pe', lambda e: e.matmul(accB[:].rearrange("p a b -> p (a b)"), zeroW[:], qtb[:, 0, :], start=True, stop=False),
                             reads=[zeroW, qtb], writes=[accB])
                        for idx, (g, kt, mi) in enumerate(tl):
                            hh = g * 4 + h; hp = hh // 2; b0 = (hh % 2) * 64
                            ps = psB.next(); pT = pTb.next(); pm = pmb.next()
                            f.op('pe', lambda e: e.matmul(ps[:], kTe[b0:b0 + 64, hp, kt * 128:(kt + 1) * 128], qtb[b0:b0 + 64, hp, :], start=True, stop=True),
                                 reads=[kTe, qtb], writes=[ps])
                            f.op('act', lambda e: e.activation(out=pT[:], in_=ps[:], func=AF.Exp), reads=[ps], writes=[pT])
                            f.op('dve', lambda e: e.tensor_tensor(pm[:], pT[:], bmask[:, mi, :], ALU.mult), reads=[pT, bmask], writes=[pm])
                            for qs in range(4):
                                f.op('pe', lambda e: e.matmul(accB[:, qs, 0:65], pm[:, qs * 128:(qs + 1) * 128], vext[:, kt, hh, :],
                                                              start=False, stop=(idx == len(tl) - 1)),
                                     reads=[pm, vext], writes=[accB])
                        rc = rcb.next(); ost = ostb.next()
                        f.op('dve', lambda e: e.reciprocal(rc[:], accB[:, :, 64]), reads=[accB], writes=[rc])
                        f.op('dve', lambda e: e.tensor_tensor(ost[:], accB[:, :, 0:64], rc[:].unsqueeze(2).broadcast_to([128, 4, 64]), ALU.mult),
                             reads=[accB, rc], writes=[ost])
                        f.dma('act', oB.h.ap()[q0:q0 + 512, h * 64:(h + 1) * 64].rearrange("(qs p) d -> p qs d", p=128), ost[:], reads=[ost])
            f.barrier()
        if layer == 0 and BUILD_UPTO == 5:
            break

        with ExitStack() as es:
            lbt = f.sbuf(es, "lbt", [128, 2, 8], F32)
            oml = f.sbuf(es, "oml", [128, 2, 8], F32)
            if layer == 0:
                f.op('dve', lambda e: e.memset(lbt[:], 0.0), writes=[lbt])
                f.op('dve', lambda e: e.memset(oml[:], 1.0), writes=[oml])
            else:
                raw = f.sbuf(es, "raw", [128, 2, 2, 8], F32)
                for dr, nm in enumerate(("c_lb_fwd", "c_lb_bwd")):
                    for l2 in range(2):
                        f.dma('sp', raw[:, dr, l2, :], I[nm].h.ap()[l2].rearrange("(h c) -> c h", c=128), writes=[raw],
                              allow_slow_non_contiguous=True)
                f.op('dve', lambda e: e.tensor_tensor(lbt[:], raw[:, :, 1, :], raw[:, :, 0, :], ALU.subtract), reads=[raw], writes=[lbt])
                f.op('act', lambda e: e.activation(out=lbt[:], in_=lbt[:], func=AF.Sigmoid), reads=[lbt], writes=[lbt])
                f.op('dve', lambda e: e.tensor_scalar(oml[:], lbt[:], -1.0, 1.0, ALU.mult, ALU.add), reads=[lbt], writes=[oml])
            chm = f.sbuf(es, "chm", [128, NL], F32)
            f.dma('sp', chm[:], chunkm_in[:, :], writes=[chm])
            W = NL
            zq = f.sbuf(es, "zq", [128, W], F32); qs = f.sbuf(es, "qs", [128, W], F32)
            zz = f.sbuf(es, "zz", [128, W], F32); sg = f.sbuf(es, "sg", [128, W], F32)
            lf = f.sbuf(es, "lf", [128, W], F32); kk = f.sbuf(es, "kk", [128, W], F32)
            PP = f.sbuf(es, "PP", [128, W], F32); br = f.sbuf(es, "br", [128, W], F32)
            eq = f.sbuf(es, "eq", [128, W], F32); ek = f.sbuf(es, "ek", [128, W], F32)
            qo = Ring([f.sbuf(es, "qo", [128, W], BF16) for _ in range(2)])
            ko = Ring([f.sbuf(es, "ko", [128, W], BF16) for _ in range(2)])
            erl = Ring([f.sbuf(es, "erl", [128, 2, 32], F32) for _ in range(2)])
            att = Ring([f.sbuf(es, "att", [128, 2], F32) for _ in range(2)])
            pst = Ring([f.psum(es, "pst6", [128, 4, 128], F32) for _ in range(4)])
            stT = Ring([f.sbuf(es, "stT", [128, 4, 128], BF16) for _ in range(3)])
            v3 = lambda t: t[:].rearrange("p (j t) -> p j t", t=64)
            for h in range(8):
                for seg in range(2):
                    cs = slice(seg * NL, (seg + 1) * NL)
                    f.dma('sp', zq[:], projFM[h * 128:(h + 1) * 128, cs], writes=[zq])
                    f.op('act', lambda e: e.activation(out=qs[:], in_=zq[:], func=AF.Silu), reads=[zq], writes=[qs])
                    for dr in range(2):
                        f.dma('sp', zz[:], projFM[(8 + 8 * dr + h) * 128:(9 + 8 * dr + h) * 128, cs], writes=[zz])
                        f.op('dve', lambda e: e.tensor_scalar(zz[:], zz[:], -30.0, 30.0, ALU.max, ALU.min), reads=[zz], writes=[zz])
                        f.op('act', lambda e: e.activation(out=sg[:], in_=zz[:], func=AF.Sigmoid), reads=[zz], writes=[sg])
                        f.op('dve', lambda e: e.tensor_scalar(sg[:], sg[:], oml[:, dr, h:h + 1], lbt[:, dr, h:h + 1], ALU.mult, ALU.add),
                             reads=[sg, oml, lbt], writes=[sg])
                        f.op('act', lambda e: e.activation(out=lf[:], in_=sg[:], func=AF.Ln), reads=[sg], writes=[lf])
                        f.op('pool', lambda e: e.tensor_scalar(kk[:], sg[:], -1.0, 1.0, ALU.mult, ALU.add), reads=[sg], writes=[kk])
                        f.op('dve', lambda e: e.tensor_tensor_scan(PP[:], chm[:], lf[:], 0.0, ALU.mult, ALU.add), reads=[chm, lf], writes=[PP])
                        er_ = erl.next(); at_ = att.next()
                        if dr == 0:
                            f.op('dve', lambda e: e.tensor_tensor(v3(br), v3(PP), v3(PP)[:, :, 31:32].broadcast_to([128, 32, 64]), ALU.subtract),
                                 reads=[PP], writes=[br])
                            f.op('act', lambda e: e.activation(out=er_[:, 0, :], in_=v3(PP)[:, :, 31], func=AF.Exp), reads=[PP], writes=[er_])
                            f.op('act', lambda e: e.activation(out=er_[:, 1, :], in_=v3(br)[:, :, 63], func=AF.Exp), reads=[br], writes=[er_])
                        else:
                            f.op('dve', lambda e: e.tensor_tensor(v3(eq), v3(lf), v3(PP), ALU.subtract), reads=[lf, PP], writes=[eq])
                            f.op('dve', lambda e: e.tensor_tensor(v3(eq), v3(eq), v3(PP)[:, :, 63:64].broadcast_to([128, 32, 64]), ALU.add),
                                 reads=[eq, PP], writes=[eq])
                            f.op('dve', lambda e: e.tensor_tensor(v3(br), v3(eq), v3(eq)[:, :, 32:33].broadcast_to([128, 32, 64]), ALU.subtract),
                                 reads=[eq], writes=[br])
                            f.op('act', lambda e: e.activation(out=er_[:, 0, :], in_=v3(eq)[:, :, 32], func=AF.Exp), reads=[eq], writes=[er_])
                            f.op('act', lambda e: e.activation(out=er_[:, 1, :], in_=v3(br)[:, :, 0], func=AF.Exp), reads=[br], writes=[er_])
                        f.op('dve', lambda e: e.reduce_sum(at_[:, 0:1], v3(PP)[:, :, 63], AX.X), reads=[PP], writes=[at_])
                        f.op('act', lambda e: e.activation(out=at_[:, 1:2], in_=at_[:, 0:1], func=AF.Exp), reads=[at_], writes=[at_])
                        ds_ = dr * 2 + seg
                        f.dma('act', csin[:, 4096 + ds_ * 8 + h: 4096 + ds_ * 8 + h + 1], at_[:, 1:2], reads=[at_], allow_slow_non_contiguous=True)
                        f.dma('act', cer[dr * 128:(dr + 1) * 128, h * 64 + seg * 32: h * 64 + seg * 32 + 32], er_[:, 0, :], reads=[er_])
                        f.dma('act', cel[dr * 128:(dr + 1) * 128, h * 64 + seg * 32: h * 64 + seg * 32 + 32], er_[:, 1, :], reads=[er_])
                        f.op('act', lambda e: e.activation(out=eq[:], in_=br[:], func=AF.Exp), reads=[br], writes=[eq])
                        f.op('dve', lambda e: e.tensor_scalar(br[:], br[:], -80.0, None, ALU.max), reads=[br], writes=[br])
                        f.op('act', lambda e: e.activation(out=ek[:], in_=br[:], func=AF.Exp, scale=-1.0), reads=[br], writes=[ek])
                        qo_ = qo.next(); ko_ = ko.next()
                        f.op('pool', lambda e: e.tensor_tensor(qo_[:], qs[:], eq[:], ALU.mult), reads=[qs, eq], writes=[qo_])
                        f.op('dve', lambda e: e.tensor_tensor(ko_[:], kk[:], ek[:], ALU.mult), reads=[kk, ek], writes=[ko_])
                        f.dma('sp', cqT[dr * 1024 + h * 128: dr * 1024 + (h + 1) * 128, cs], qo_[:], reads=[qo_])
                        f.dma('sp', ckT[dr * 1024 + h * 128: dr * 1024 + (h + 1) * 128, cs], ko_[:], reads=[ko_])
                        for g4 in range(4):
                            ps = pst.next(); st = stT.next()
                            for k in range(4):
                                bk_ = g4 * 4 + k
                                f.op('pe', lambda e: e.matmul(ps[:, k, :], ko_[:, bk_ * 128:(bk_ + 1) * 128], ident[:], start=True, stop=True),
                                     reads=[ko_, ident], writes=[ps])
                            f.op('act', lambda e: e.copy(st[:], ps[:]), reads=[ps], writes=[st])
                            r0 = dr * NTOK + seg * NL + g4 * 512
                            f.dma('sp', ckTM.h.ap()[r0:r0 + 512, h * 128:(h + 1) * 128].rearrange("(j p) c -> p j c", p=128), st[:], reads=[st])
            f.barrier()
        if layer == 0 and BUILD_UPTO == 6.1:
            break
        with ExitStack() as es:
            ktm = Ring([f.sbuf(es, "ktm", [128, 1024], BF16) for _ in range(2)])
            cit = Ring([f.sbuf(es, "cit", [128, 1024], BF16) for _ in range(2)])
            psk = Ring([f.psum(es, "psk", [128, 4, 128], F32) for _ in range(4)])
            stki = Ring([f.sbuf(es, "stki", [128, 8, 128], F32) for _ in range(3)])
            for dr in range(2):
                for tt in range(NT):
                    seg, lt = tt // 16, tt % 16
                    km = ktm.next(); ci_ = cit.next()
                    f.dma('sp', km[:], ckTM[dr * NTOK + tt * 128: dr * NTOK + (tt + 1) * 128, :], writes=[km])
                    f.dma('sp', ci_[:], projTM[tt * 128:(tt + 1) * 128, TM_CI:TM_CI + 1024], writes=[ci_])
                    for half in range(2):
                        sk = stki.next()
                        hs = slice(half * 64, (half + 1) * 64)
                        for hg in range(2):
                            ps = psk.next()
                            for h4 in range(4):
                                h = hg * 4 + h4
                                f.op('pe', lambda e: e.matmul(ps[:, h4, :], km[hs, h * 128:(h + 1) * 128], ci_[hs, h * 128:(h + 1) * 128], start=True, stop=True),
                                     reads=[km, ci_], writes=[ps])
                            if hg == 0:
                                f.op('act', lambda e: e.copy(sk[:, 0:4, :], ps[:]), reads=[ps], writes=[sk])
                            else:
                                f.op('dve', lambda e: e.tensor_copy(sk[:, 4:8, :], ps[:]), reads=[ps], writes=[sk])
                        chunk = lt * 2 + half
                        r0 = ((dr * 2 + seg) * 32 + chunk) * 128
                        f.dma('act', cKI[r0:r0 + 128, :], sk[:].rearrange("p h v -> p (h v)"), reads=[sk])
            f.barrier()
        if layer == 0 and BUILD_UPTO == 6.2:
            break
        with ExitStack() as es:
            sins = [f.sbuf(es, "sin%d" % i, [128, 8, 128], F32) for i in range(4)]
            oh = f.sbuf(es, "oh", [128, 16], F32)
            f.dma('sp', oh[:], cscan_in[:, :], writes=[oh])
            cm = f.sbuf(es, "cm", [128, 2, 128], BF16)
            f.dma('sp', cm[:], cmask_in.h.ap().rearrange("p (a b) -> p a b", a=2), writes=[cm])

            def load_erl(es2, dr, seg):
                er = f.sbuf(es2, "er", [128, 8, 32], F32); el = f.sbuf(es2, "el", [128, 8, 32], F32)
                f.dma('sp', er[:], cer.h.ap()[dr * 128:(dr + 1) * 128].rearrange("p (h s j) -> p h s j", h=8, s=2)[:, :, seg, :], writes=[er])
                f.dma('sp', el[:], cel.h.ap()[dr * 128:(dr + 1) * 128].rearrange("p (h s j) -> p h s j", h=8, s=2)[:, :, seg, :], writes=[el])
                return er, el

            def bcj(t, j):
                return t[:, :, j:j + 1].broadcast_to([128, 8, 128])

            with ExitStack() as es2:
                state = f.sbuf(es2, "state", [128, 8, 128], F32)
                tmp = f.sbuf(es2, "tmp", [128, 8, 128], F32)
                kir = Ring([f.sbuf(es2, "ki", [128, 8, 128], F32) for _ in range(4)])
                for dr in range(2):
                    for seg in range(2):
                        with ExitStack() as es3:
                            er, el = load_erl(es3, dr, seg)
                            f.op('dve', lambda e: e.memset(state[:], 0.0), writes=[state])
                            order = range(32) if dr == 0 else range(31, -1, -1)
                            for j in order:
                                ki = kir.next()
                                r0 = ((dr * 2 + seg) * 32 + j) * 128
                                f.dma('sp', ki[:].rearrange("p h v -> p (h v)"), cKI[r0:r0 + 128, :], writes=[ki])
                                f.op('dve', lambda e: e.tensor_tensor(tmp[:], state[:], bcj(er, j), ALU.mult), reads=[state, er], writes=[tmp])
                                f.op('pool', lambda e: e.tensor_tensor(tmp[:], tmp[:], ki[:], ALU.add), reads=[tmp, ki], writes=[tmp])
                                f.op('dve', lambda e: e.tensor_tensor(state[:], tmp[:], bcj(el, j), ALU.mult), reads=[tmp, el], writes=[state])
                            ds_ = dr * 2 + seg
                            f.dma('act', csin[:, ds_ * 1024:(ds_ + 1) * 1024], state[:].rearrange("p h v -> p (h v)"), reads=[state])
                            f.barrier()
                f.barrier()
            f.collective("AllGather", ALL8, csin, csout)
            f.barrier()
            with ExitStack() as es2:
                G = f.sbuf(es2, "G", [128, 8, 1024], F32)
                At = f.sbuf(es2, "At", [128, 8, 8], F32)
                acc = f.sbuf(es2, "accs", [128, 8, 128], F32)
                gv = csout.h.ap().rearrange("(r p) n -> p r n", p=128)
                for dr in range(2):
                    for seg in range(2):
                        ds_ = dr * 2 + seg
                        sin_ = sins[ds_]
                        f.dma('sp', G[:], gv[:, :, ds_ * 1024:(ds_ + 1) * 1024], writes=[G])
                        f.dma('sp', At[:], gv[:, :, 4096 + ds_ * 8: 4096 + ds_ * 8 + 8], writes=[At])
                        f.op('dve', lambda e: e.memset(acc[:], 0.0), writes=[acc])
                        f.op('dve', lambda e: e.memset(sin_[:], 0.0), writes=[sin_])
                        order = range(8) if dr == 0 else range(7, -1, -1)
                        for r in order:
                            if seg == 0 and ((dr == 0 and r % 2 == 0) or (dr == 1 and r % 2 == 1)):
                                f.op('dve', lambda e: e.memset(acc[:], 0.0), writes=[acc])
                            f.op('dve', lambda e: e.scalar_tensor_tensor(sin_[:], acc[:], oh[:, dr * 8 + r: dr * 8 + r + 1], sin_[:], ALU.mult, ALU.add),
                                 reads=[acc, oh, sin_], writes=[sin_])
                            f.op('dve', lambda e: e.tensor_tensor(acc[:], acc[:], At[:, r, :].unsqueeze(2).broadcast_to([128, 8, 128]), ALU.mult),
                                 reads=[acc, At], writes=[acc])
                            f.op('dve', lambda e: e.tensor_tensor(acc[:], acc[:], G[:, r, :].rearrange("p (h v) -> p h v", v=128), ALU.add),
                                 reads=[acc, G], writes=[acc])
                f.barrier()
            with ExitStack() as es2:
                state = f.sbuf(es2, "state2", [128, 8, 128], F32)
                tmp = f.sbuf(es2, "tmp2", [128, 8, 128], F32)
                kir = Ring([f.sbuf(es2, "ki2", [128, 8, 128], F32) for _ in range(4)])
                Ebr = Ring([f.sbuf(es2, "Eb", [128, 8, 128], BF16) for _ in range(3)])
                qT_ = f.sbuf(es2, "cqTs", [128, 8, NL], BF16)
                kT_ = f.sbuf(es2, "ckTs", [128, 8, NL], BF16)
                cia = f.sbuf(es2, "cia", [128, 16, 1024], BF16)
                PTr = Ring([f.sbuf(es2, "PTc", [128, 8, 128], BF16) for _ in range(2)])
                psc = Ring([f.psum(es2, "psc", [128, 4, 128], F32) for _ in range(2)])
                por = Ring([f.psum(es2, "po", [64, 8, 128], F32) for _ in range(2)])
                osr = Ring([f.sbuf(es2, "osc", [64, 8, 128], F32) for _ in range(3)])
                for dr in range(2):
                    for seg in range(2):
                        with ExitStack() as es3:
                            er, el = load_erl(es3, dr, seg)
                            cs = slice(seg * NL, (seg + 1) * NL)
                            f.dma('sp', qT_[:], cqT.h.ap()[dr * 1024:(dr + 1) * 1024, cs].rearrange("(h p) t -> p h t", p=128), writes=[qT_])
                            f.dma('sp', kT_[:], ckT.h.ap()[dr * 1024:(dr + 1) * 1024, cs].rearrange("(h p) t -> p h t", p=128), writes=[kT_])
                            f.dma('sp', cia[:], projTM.h.ap()[seg * NL:(seg + 1) * NL, TM_CI:TM_CI + 1024].rearrange("(j p) c -> p j c", p=128), writes=[cia])
                            f.op('dve', lambda e: e.tensor_copy(state[:], sins[dr * 2 + seg][:]), reads=[sins[dr * 2 + seg]], writes=[state])
                            order = range(32) if dr == 0 else range(31, -1, -1)
                            PT = None
                            for step, j in enumerate(order):
                                lt, half = j // 2, j % 2
                                tc = slice(lt * 128, (lt + 1) * 128)
                                if step % 2 == 0:
                                    PT = PTr.next()
                                    for hg in range(2):
                                        ps = psc.next()
                                        for h4 in range(4):
                                            h = hg * 4 + h4
                                            f.op('pe', lambda e: e.matmul(ps[:, h4, :], kT_[:, h, tc], qT_[:, h, tc], start=True, stop=True),
                                                 reads=[kT_, qT_], writes=[ps])
                                        f.op('dve', lambda e: e.tensor_tensor(PT[:, hg * 4:(hg + 1) * 4, :], ps[:], cm[:, dr:dr + 1, :].broadcast_to([128, 4, 128]), ALU.mult),
                                             reads=[ps, cm], writes=[PT])
                                ki = kir.next(); Eb = Ebr.next()
                                r0 = ((dr * 2 + seg) * 32 + j) * 128
                                f.dma('sp', ki[:].rearrange("p h v -> p (h v)"), cKI[r0:r0 + 128, :], writes=[ki])
                                f.op('dve', lambda e: e.tensor_tensor(tmp[:], state[:], bcj(er, j), ALU.mult), reads=[state, er], writes=[tmp])
                                f.op('act', lambda e: e.copy(Eb[:], tmp[:]), reads=[tmp], writes=[Eb])
                                f.op('pool', lambda e: e.tensor_tensor(tmp[:], tmp[:], ki[:], ALU.add), reads=[tmp, ki, Eb], writes=[tmp])
                                f.op('dve', lambda e: e.tensor_tensor(state[:], tmp[:], bcj(el, j), ALU.mult), reads=[tmp, el], writes=[state])
                                po = por.next(); os_ = osr.next()
                                for bnk in range(2):
                                    f.op('pe', lambda e: e.matmul(po[:, bnk * 4:(bnk + 1) * 4, :].rearrange("p a b -> p (a b)"), zeroW[:, 0:64], cia[:, lt, 0:512], start=True, stop=False),
                                         reads=[zeroW, cia], writes=[po])
                                for h in range(8):
                                    f.op('pe', lambda e: e.matmul(po[:, h, :], PT[:, h, half * 64:(half + 1) * 64], cia[:, lt, h * 128:(h + 1) * 128], start=False, stop=False),
                                         reads=[PT, cia], writes=[po])
                                    f.op('pe', lambda e: e.matmul(po[:, h, :], qT_[:, h, j * 64:(j + 1) * 64], Eb[:, h, :], start=False, stop=True),
                                         reads=[qT_, Eb], writes=[po])
                                if step % 2 == 0:
                                    f.op('act', lambda e: e.copy(os_[:], po[:]), reads=[po], writes=[os_])
                                else:
                                    f.op('dve', lambda e: e.tensor_copy(os_[:], po[:]), reads=[po], writes=[os_])
                                r1 = dr * NTOK + seg * NL + j * 64
                                f.dma('act', coC[r1:r1 + 64, :], os_[:].rearrange("p h v -> p (h v)"), reads=[os_])
                            f.barrier()
                f.barrier()
            f.barrier()
        if layer == 0 and BUILD_UPTO == 6:
            break

        with ExitStack() as es:
            def load_w(name, src2d, kch, ncol, es_=es):
                wt = f.sbuf(es_, name, [128, kch, ncol], BF16)
                with ExitStack() as est:
                    tmpw = Ring([f.sbuf(est, "tmpw", [128, 512], F32) for _ in range(3)])
                    sv = src2d.rearrange("(kc p) n -> p kc n", p=128)
                    i = 0
                    for kc in range(kch):
                        for c0 in range(0, ncol, 512):
                            w = min(512, ncol - c0)
                            tw = tmpw.next()
                            f.dma('sp', tw[:, 0:w], sv[:, kc, c0:c0 + w], writes=[tw])
                            E = ('dve', 'pool')[i % 2]; i += 1
                            f.op(E, lambda e: e.tensor_copy(wt[:, kc, c0:c0 + w], tw[:, 0:w]), reads=[tw], writes=[wt])
                    f.barrier()
                return wt
            wA = load_w("wA", I["w_br_a"].h.ap()[layer], 8, D)
            wBb = load_w("wBb", I["w_br_b"].h.ap()[layer], 2, D)
            wC = load_w("wC", I["w_br_c"].h.ap()[layer], 8, D)
            wMo = load_w("wMo", I["w_mix_out"].h.ap()[layer], 8, D)
            wQ = load_w("wQ", I["x_wq"].h.ap()[layer], 8, D)
            wO = load_w("wO", I["x_wo"].h.ap()[layer], 8, D)
            gcn = load_bcast_row(es, "gcn", I["c_gnorm"][layer:layer + 1, :], D)
            gcr = load_bcast_row(es, "gcr", I["g_cross"][layer:layer + 1, :], D)
            gff = load_bcast_row(es, "gff", I["g_ffn"][layer:layer + 1, :], D)
            gxq = f.sbuf(es, "gxq", [128, 4, 256], F32)
            for h in range(4):
                f.dma('sp', gxq[:, h, :], I["x_gq"][layer:layer + 1, :].partition_broadcast(128), writes=[gxq])
            f.op('dve', lambda e: e.tensor_scalar(gxq[:], gxq[:], 256.0 ** -0.5, None, ALU.mult), reads=[gxq], writes=[gxq])
            kTm = f.sbuf(es, "kTm", [128, 2, 8, 256], BF16)
            vmem = f.sbuf(es, "vmem", [128, 2, 2, 4, 257], BF16)
            f.op('dve', lambda e: e.memset(vmem[:, :, :, :, 256:257], 1.0), writes=[vmem])
            pT4 = Ring([f.psum(es, "pT4", [128, 4, 128], F32) for _ in range(2)])
            pM = Ring([f.psum(es, "pM", [128, 512], F32) for _ in range(2)])
            with ExitStack() as es2:
                gmm = load_bcast_row(es2, "gmm", I["g_mem"][layer:layer + 1, :], D)
                gxk = f.sbuf(es2, "gxk", [128, 4, 256], F32)
                for h in range(4):
                    f.dma('sp', gxk[:, h, :], I["x_gk"][layer:layer + 1, :].partition_broadcast(128), writes=[gxk])
                wKV = load_w("wKV", I["x_wkv"].h.ap()[layer], 8, 2 * D, es_=es2)
                mt = f.sbuf(es2, "mt", [128, D], F32); sqm = f.sbuf(es2, "sqm", [128, D], F32)
                rsm = f.sbuf(es2, "rsm", [128, 4], F32); mnb = f.sbuf(es2, "mnb", [128, D], BF16)
                mT = f.sbuf(es2, "mT", [128, 8, 128], BF16)
                kf = f.sbuf(es2, "kf", [128, 4, 256], F32); kb = f.sbuf(es2, "kb", [128, D], BF16)
                for seg in range(2):
                    for mtile in range(2):
                        f.dma('sp', mt[:], mem_in[seg * 256 + mtile * 128: seg * 256 + (mtile + 1) * 128, :], writes=[mt])
                        rms_rows(es2, mt, D, rsm, sqm)
                        f.op('dve', lambda e: e.scalar_tensor_tensor(mnb[:], mt[:], rsm[:, 0:1], gmm[:], ALU.mult, ALU.mult), reads=[mt, rsm, gmm], writes=[mnb])
                        transpose_blocks(mnb, 8, pT4, lambda k0, n: mT[:, k0:k0 + n, :], [mT])
                        for cb in range(4):
                            ps = pM.next()
                            for kc in range(8):
                                f.op('pe', lambda e: e.matmul(ps[:], mT[:, kc, :], wKV[:, kc, cb * 512:(cb + 1) * 512], start=(kc == 0), stop=(kc == 7)),
                                     reads=[mT, wKV], writes=[ps])
                            if cb < 2:
                                f.op('act', lambda e: e.copy(kf[:, cb * 2:(cb + 1) * 2, :], ps[:].rearrange("p (h d) -> p h d", d=256)), reads=[ps], writes=[kf])
                            else:
                                f.op('act', lambda e: e.copy(vmem[:, seg, mtile, (cb - 2) * 2:(cb - 1) * 2, 0:256], ps[:].rearrange("p (h d) -> p h d", d=256)),
                                     reads=[ps], writes=[vmem])
                        f.op('dve', lambda e: e.tensor_tensor(sqm[:], kf[:].rearrange("p h d -> p (h d)"), kf[:].rearrange("p h d -> p (h d)"), ALU.mult), reads=[kf], writes=[sqm])
                        f.op('dve', lambda e: e.reduce_sum(rsm[:], sqm[:].rearrange("p (h d) -> p h d", d=256), AX.X), reads=[sqm], writes=[rsm])
                        f.op('dve', lambda e: e.tensor_scalar(rsm[:], rsm[:], 1.0 / 256, EPS, ALU.mult, ALU.add), reads=[rsm], writes=[rsm])
                        f.op('act', lambda e: e.activation(out=rsm[:], in_=rsm[:], func=AF.Sqrt), reads=[rsm], writes=[rsm])
                        f.op('dve', lambda e: e.reciprocal(rsm[:], rsm[:]), reads=[rsm], writes=[rsm])
                        f.op('dve', lambda e: e.tensor_tensor(kf[:], kf[:], rsm[:].unsqueeze(2).broadcast_to([128, 4, 256]), ALU.mult), reads=[kf, rsm], writes=[kf])
                        f.op('dve', lambda e: e.tensor_tensor(kb[:].rearrange("p (h d) -> p h d", d=256), kf[:], gxk[:], ALU.mult), reads=[kf, gxk], writes=[kb])
                        transpose_blocks(kb, 8, pT4, lambda k0, n: kTm[:, seg, k0:k0 + n, mtile * 128:(mtile + 1) * 128], [kTm])
                f.barrier()
            with ExitStack() as es2:
                oAt = Ring([f.sbuf(es2, "oAt", [128, D], BF16) for _ in range(1)])
                oBt = Ring([f.sbuf(es2, "oBt", [128, 256], BF16) for _ in range(1)])
                cft = Ring([f.sbuf(es2, "cft", [128, D], F32) for _ in range(1)])
                cbt = Ring([f.sbuf(es2, "cbt", [128, D], F32) for _ in range(1)])
                cgt = Ring([f.sbuf(es2, "cgt", [128, 4096], BF16) for _ in range(1)])
                xt_ = Ring([f.sbuf(es2, "xt7", [128, D], F32) for _ in range(1)])
                sq7 = f.sbuf(es2, "sq7", [128, D], F32)
                rs7 = f.sbuf(es2, "rs7", [128, 4], F32)
                slt = f.sbuf(es2, "slt", [128, D], F32)
                ocb = f.sbuf(es2, "ocb", [128, D], BF16)
                aT = f.sbuf(es2, "aT", [128, 18, 128], BF16)
                gsg = f.sbuf(es2, "gsg", [128, 3, D], BF16)
                mrg = f.sbuf(es2, "mrg", [128, D], F32); mtmp = f.sbuf(es2, "mtmp", [128, 512], F32)
                mrb = f.sbuf(es2, "mrb", [128, D], BF16)
                mTt = f.sbuf(es2, "mTt", [128, 8, 128], BF16)
                x1 = f.sbuf(es2, "x1", [128, D], F32)
                u2 = f.sbuf(es2, "u2", [128, D], BF16); u2T = f.sbuf(es2, "u2T", [128, 8, 128], BF16)
                qf = f.sbuf(es2, "qf", [128, 4, 256], F32); qb_ = f.sbuf(es2, "qb7", [128, D], BF16)
                qTx = f.sbuf(es2, "qTx", [128, 8, 128], BF16)
                psS7 = f.psum(es2, "psS7", [128, 8, 128], F32)
                psO7 = Ring([f.psum(es2, "psO7", [128, 512], F32) for _ in range(2)])
                PT7 = f.sbuf(es2, "PT7", [128, 8, 128], BF16)
                rc7 = f.sbuf(es2, "rc7", [128, 4], F32)
                ox = f.sbuf(es2, "ox", [128, D], BF16); oxT = f.sbuf(es2, "oxT", [128, 8, 128], BF16)
                x2 = Ring([f.sbuf(es2, "x2", [128, D], F32) for _ in range(1)])
                u3 = Ring([f.sbuf(es2, "u3", [128, D], BF16) for _ in range(1)])
                u3s = Ring([f.sbuf(es2, "u3s", [128, 8, 128], BF16) for _ in range(1)])
                for tt in range(NT):
                    seg, lt = tt // 16, tt % 16
                    rows = slice(tt * 128, (tt + 1) * 128)
                    oa = oAt.next(); ob = oBt.next(); cf = cft.next(); cb_ = cbt.next(); cg = cgt.next(); xt = xt_.next()
                    f.dma('sp', oa[:], oA[rows, :], writes=[oa])
                    f.dma('sp', ob[:], oB[rows, :], writes=[ob])
                    f.dma('sp', cf[:], coC[tt * 128:(tt + 1) * 128, :], writes=[cf])
                    f.dma('sp', cb_[:], coC[NTOK + tt * 128: NTOK + (tt + 1) * 128, :], writes=[cb_])
                    f.dma('sp', cg[:], projTM[rows, TM_CG:TM_CG + 4096], writes=[cg])
                    f.dma('sp', xt[:], xsrc[rows, :], writes=[xt])
                    f.op('pool', lambda e: e.tensor_tensor(cf[:], cf[:], cb_[:], ALU.add), reads=[cf, cb_], writes=[cf])
                    rms_rows(es2, cf, D, rs7, sq7)
                    f.op('act', lambda e: e.activation(out=slt[:], in_=cg[:, 0:1024], func=AF.Silu), reads=[cg], writes=[slt])
                    f.op('dve', lambda e: e.scalar_tensor_tensor(cf[:], cf[:], rs7[:, 0:1], gcn[:], ALU.mult, ALU.mult), reads=[cf, rs7, gcn], writes=[cf])
                    f.op('dve', lambda e: e.tensor_tensor(ocb[:], cf[:], slt[:], ALU.mult), reads=[cf, slt], writes=[ocb])
                    f.op('act', lambda e: e.activation(out=gsg[:].rearrange("p a b -> p (a b)"), in_=cg[:, 1024:4096], func=AF.Sigmoid), reads=[cg], writes=[gsg])
                    transpose_blocks(oa, 8, pT4, lambda k0, n: aT[:, k0:k0 + n, :], [aT])
                    transpose_blocks(ob, 2, pT4, lambda k0, n: aT[:, 8 + k0:8 + k0 + n, :], [aT])
                    transpose_blocks(ocb, 8, pT4, lambda k0, n: aT[:, 10 + k0:10 + k0 + n, :], [aT])
                    for cbk in range(2):
                        cs = slice(cbk * 512, (cbk + 1) * 512)
                        for bi, (wt, k0, nk) in enumerate(((wA, 0, 8), (wBb, 8, 2), (wC, 10, 8))):
                            ps = pM.next()
                            for kc in range(nk):
                                f.op('pe', lambda e: e.matmul(ps[:], aT[:, k0 + kc, :], wt[:, kc, cs], start=(kc == 0), stop=(kc == nk - 1)),
                                     reads=[aT, wt], writes=[ps])
                            if bi == 0:
                                f.op('dve', lambda e: e.tensor_tensor(mrg[:, cs], ps[:], gsg[:, 0, cs], ALU.mult), reads=[ps, gsg], writes=[mrg])
                            else:
                                f.op('dve', lambda e: e.tensor_tensor(mtmp[:], ps[:], gsg[:, bi, cs], ALU.mult), reads=[ps, gsg], writes=[mtmp])
                                f.op('pool', lambda e: e.tensor_tensor(mrg[:, cs], mrg[:, cs], mtmp[:], ALU.add), reads=[mrg, mtmp], writes=[mrg])
                    f.op('act', lambda e: e.copy(mrb[:], mrg[:]), reads=[mrg], writes=[mrb])
                    transpose_blocks(mrb, 8, pT4, lambda k0, n: mTt[:, k0:k0 + n, :], [mTt])
                    for cbk in range(2):
                        cs = slice(cbk * 512, (cbk + 1) * 512)
                        ps = pM.next()
                        for kc in range(8):
                            f.op('pe', lambda e: e.matmul(ps[:], mTt[:, kc, :], wMo[:, kc, cs], start=(kc == 0), stop=(kc == 7)), reads=[mTt, wMo], writes=[ps])
                        f.op('dve', lambda e: e.tensor_tensor(x1[:, cs], ps[:], xt[:, cs], ALU.add), reads=[ps, xt], writes=[x1])
                    rms_rows(es2, x1, D, rs7, sq7)
                    f.op('dve', lambda e: e.scalar_tensor_tensor(u2[:], x1[:], rs7[:, 0:1], gcr[:], ALU.mult, ALU.mult), reads=[x1, rs7, gcr], writes=[u2])
                    transpose_blocks(u2, 8, pT4, lambda k0, n: u2T[:, k0:k0 + n, :], [u2T])
                    for cbk in range(2):
                        ps = pM.next()
                        for kc in range(8):
                            f.op('pe', lambda e: e.matmul(ps[:], u2T[:, kc, :], wQ[:, kc, cbk * 512:(cbk + 1) * 512], start=(kc == 0), stop=(kc == 7)), reads=[u2T, wQ], writes=[ps])
                        f.op('act', lambda e: e.copy(qf[:, cbk * 2:(cbk + 1) * 2, :], ps[:].rearrange("p (h d) -> p h d", d=256)), reads=[ps], writes=[qf])
                    f.op('dve', lambda e: e.tensor_tensor(sq7[:], qf[:].rearrange("p h d -> p (h d)"), qf[:].rearrange("p h d -> p (h d)"), ALU.mult), reads=[qf], writes=[sq7])
                    f.op('dve', lambda e: e.reduce_sum(rs7[:], sq7[:].rearrange("p (h d) -> p h d", d=256), AX.X), reads=[sq7], writes=[rs7])
                    f.op('dve', lambda e: e.tensor_scalar(rs7[:], rs7[:], 1.0 / 256, EPS, ALU.mult, ALU.add), reads=[rs7], writes=[rs7])
                    f.op('act', lambda e: e.activation(out=rs7[:], in_=rs7[:], func=AF.Sqrt), reads=[rs7], writes=[rs7])
                    f.op('dve', lambda e: e.reciprocal(rs7[:], rs7[:]), reads=[rs7], writes=[rs7])
                    f.op('dve', lambda e: e.tensor_tensor(qf[:], qf[:], rs7[:].unsqueeze(2).broadcast_to([128, 4, 256]), ALU.mult), reads=[qf, rs7], writes=[qf])
                    f.op('pool', lambda e: e.tensor_tensor(qb_[:].rearrange("p (h d) -> p h d", d=256), qf[:], gxq[:], ALU.mult), reads=[qf, gxq], writes=[qb_])
                    transpose_blocks(qb_, 8, pT4, lambda k0, n: qTx[:, k0:k0 + n, :], [qTx])
                    for h in range(4):
                        for kt in range(2):
                            for dc in range(2):
                                f.op('pe', lambda e: e.matmul(psS7[:, h * 2 + kt, :], kTm[:, seg, h * 2 + dc, kt * 128:(kt + 1) * 128], qTx[:, h * 2 + dc, :],
                                                              start=(dc == 0), stop=(dc == 1)), reads=[kTm, qTx], writes=[psS7])
                    f.op('act', lambda e: e.activation(out=PT7[:], in_=psS7[:], func=AF.Exp), reads=[psS7], writes=[PT7])
                    for h in range(4):
                        ps = psO7.next()
                        for kt in range(2):
                            f.op('pe', lambda e: e.matmul(ps[:, 0:257], PT7[:, h * 2 + kt, :], vmem[:, seg, kt, h, :], start=(kt == 0), stop=(kt == 1)),
                                 reads=[PT7, vmem], writes=[ps])
                        f.op('dve', lambda e: e.reciprocal(rc7[:, h:h + 1], ps[:, 256:257]), reads=[ps], writes=[rc7])
                        f.op('dve', lambda e: e.tensor_scalar(ox[:, h * 256:(h + 1) * 256], ps[:, 0:256], rc7[:, h:h + 1], None, ALU.mult), reads=[ps, rc7], writes=[ox])
                    transpose_blocks(ox, 8, pT4, lambda k0, n: oxT[:, k0:k0 + n, :], [oxT])
                    xo = x2.next()
                    for cbk in range(2):
                        cs = slice(cbk * 512, (cbk + 1) * 512)
                        ps = pM.next()
                        for kc in range(8):
                            f.op('pe', lambda e: e.matmul(ps[:], oxT[:, kc, :], wO[:, kc, cs], start=(kc == 0), stop=(kc == 7)), reads=[oxT, wO], writes=[ps])
                        f.op('dve', lambda e: e.tensor_tensor(xo[:, cs], ps[:], x1[:, cs], ALU.add), reads=[ps, x1], writes=[xo])
                    f.dma('act', xmid[rows, :], xo[:], reads=[xo])
                    rms_rows(es2, xo, D, rs7, sq7)
                    u3_ = u3.next(); u3s_ = u3s.next()
                    f.op('dve', lambda e: e.scalar_tensor_tensor(u3_[:], xo[:], rs7[:, 0:1], gff[:], ALU.mult, ALU.mult), reads=[xo, rs7, gff], writes=[u3_])
                    transpose_blocks(u3_, 8, pT4, lambda k0, n: u3s_[:, k0:k0 + n, :], [u3s_])
                    f.dma('act', u3T.h.ap().rearrange("(kc p) t -> p kc t", p=128)[:, :, rows], u3s_[:], reads=[u3s_])
                    if lt == 0:
                        f.dma('act', fgin[seg * 2:seg * 2 + 1, :], u3_[0:1, :], reads=[u3_])
                    if lt == 15:
                        f.dma('act', fgin[seg * 2 + 1:seg * 2 + 2, :], u3_[127:128, :], reads=[u3_])
                f.barrier()
            f.barrier()
        f.collective("AllGather", ALL8, fgin, fgout)
        f.barrier()
        with ExitStack() as es:
            EXT = NL + 2
            uTe = f.sbuf(es, "uTe", [128, 8, 2, EXT], BF16)
            for seg in range(2):
                f.dma('sp', uTe[:, :, seg, 1:NL + 1], u3T.h.ap().rearrange("(kc p) t -> p kc t", p=128)[:, :, seg * NL:(seg + 1) * NL], writes=[uTe])
            gall = f.sbuf(es, "gall", [32, D], BF16)
            fsl = f.sbuf(es, "fsl", [32, 4], BF16)
            f.dma('sp', gall[:], fgout[:, :], writes=[gall])
            f.dma('sp', fsl[:], fsel_in[:, :], writes=[fsl])
            psh = f.psum(es, "psh", [128, 8, 4], F32)
            for kc in range(8):
                f.op('pe', lambda e: e.matmul(psh[:, kc, :], gall[:, kc * 128:(kc + 1) * 128], fsl[:], start=True, stop=True), reads=[gall, fsl], writes=[psh])
            for seg in range(2):
                f.op('dve', lambda e: e.tensor_copy(uTe[:, :, seg, 0:1], psh[:, :, seg * 2:seg * 2 + 1]), reads=[psh], writes=[uTe])
                f.op('dve', lambda e: e.tensor_copy(uTe[:, :, seg, NL + 1:NL + 2], psh[:, :, seg * 2 + 1:seg * 2 + 2]), reads=[psh], writes=[uTe])
            cw = f.sbuf(es, "cw", [128, 3, 2 * DFF], F32)
            for k in range(3):
                f.dma('sp', cw[:, k, :], I["f_conv"].h.ap()[layer, k:k + 1, :].partition_broadcast(128), writes=[cw])
            cbias = f.sbuf(es, "cbias", [128, 44], F32)
            f.dma('sp', cbias[:], I["f_conv_b"].h.ap()[layer].rearrange("(j p) -> p j", p=128), writes=[cbias], allow_slow_non_contiguous=True)
            wu32 = Ring([f.sbuf(es, "wu32", [128, 8, 128], F32) for _ in range(3)])
            wub = Ring([f.sbuf(es, "wub", [128, 3, 8, 128], BF16) for _ in range(4)])
            psU = Ring([f.psum(es, "psU", [128, 512], F32) for _ in range(6)])
            sgt = Ring([f.sbuf(es, "sgt", [128, 512], F32) for _ in range(2)])
            ggt = Ring([f.sbuf(es, "ggt", [128, 512], F32) for _ in range(2)])
            actt = Ring([f.sbuf(es, "actt", [128, 512], BF16) for _ in range(3)])
            wupv = I["f_wup"].h.ap()[layer].rearrange("(kc p) n -> p kc n", p=128)
            for j in range(22):
                wbs = []
                for ag in range(2):
                    c0 = ag * DFF + j * 128
                    w32_ = wu32.next(); wb_ = wub.next()
                    f.dma('sp', w32_[:], wupv[:, :, c0:c0 + 128], writes=[w32_])
                    for k in range(3):
                        E = ('dve', 'pool', 'dve')[k]
                        f.op(E, lambda e: e.tensor_tensor(wb_[:, k, :, :], w32_[:], cw[:, k, c0:c0 + 128].unsqueeze(1).broadcast_to([128, 8, 128]), ALU.mult),
                             reads=[w32_, cw], writes=[wb_])
                    wbs.append(wb_)
                for tg in range(8):
                    seg, lg = tg // 4, tg % 4
                    pss = []
                    for ag in range(2):
                        ps = psU.next()
                        n = 0
                        for k in range(3):
                            for kc in range(8):
                                f.op('pe', lambda e: e.matmul(ps[:], wbs[ag][:, k, kc, :], uTe[:, kc, seg, lg * 512 + k: lg * 512 + k + 512],
                                                              start=(n == 0), stop=(n == 23)), reads=[wbs[ag], uTe], writes=[ps])
                                n += 1
                        pss.append(ps)
                    sg_ = sgt.next(); gg_ = ggt.next(); ac_ = actt.next()
                    f.op('act', lambda e: e.activation(out=sg_[:], in_=pss[1][:], func=AF.Sigmoid, bias=cbias[:, 22 + j:23 + j]), reads=[pss[1], cbias], writes=[sg_])
                    f.op('dve', lambda e: e.scalar_tensor_tensor(gg_[:], pss[1][:], cbias[:, 22 + j:23 + j], sg_[:], ALU.add, ALU.mult), reads=[pss[1], cbias, sg_], writes=[gg_])
                    f.op('dve', lambda e: e.scalar_tensor_tensor(ac_[:], pss[0][:], cbias[:, j:j + 1], gg_[:], ALU.add, ALU.mult), reads=[pss[0], cbias, gg_], writes=[ac_])
                    f.dma('act', hact[j * 128:(j + 1) * 128, tg * 512:(tg + 1) * 512], ac_[:], reads=[ac_])
            f.barrier()
        with ExitStack() as es:
            wD = f.sbuf(es, "wD", [128, 22, D], BF16)
            with ExitStack() as est:
                tmpw = Ring([f.sbuf(est, "tmpwd", [128, 512], F32) for _ in range(3)])
                sv = I["f_wdown"].h.ap()[layer].rearrange("(kc p) n -> p kc n", p=128)
                i = 0
                for kc in range(22):
                    for c0 in (0, 512):
                        tw = tmpw.next()
                        f.dma('sp', tw[:], sv[:, kc, c0:c0 + 512], writes=[tw])
                        E = ('dve', 'pool')[i % 2]; i += 1
                        f.op(E, lambda e: e.tensor_copy(wD[:, kc, c0:c0 + 512], tw[:]), reads=[tw], writes=[wD])
                f.barrier()
            hat = Ring([f.sbuf(es, "hat", [128, 22, 128], BF16) for _ in range(2)])
            xm = Ring([f.sbuf(es, "xm", [128, D], F32) for _ in range(2)])
            xo3 = Ring([f.sbuf(es, "xo3", [128, D], F32) for _ in range(2)])
            pD = Ring([f.psum(es, "pD", [128, 512], F32) for _ in range(4)])
            dst = xres if layer == 0 else y_out
            for tt in range(NT):
                rows = slice(tt * 128, (tt + 1) * 128)
                ha = hat.next(); xm_ = xm.next(); xo_ = xo3.next()
                f.dma('sp', ha[:], hact.h.ap().rearrange("(j p) t -> p j t", p=128)[:, :, rows], writes=[ha])
                f.dma('sp', xm_[:], xmid[rows, :], writes=[xm_])
                for cbk in range(2):
                    cs = slice(cbk * 512, (cbk + 1) * 512)
                    ps = pD.next()
                    for j in range(22):
                        f.op('pe', lambda e: e.matmul(ps[:], ha[:, j, :], wD[:, j, cs], start=(j == 0), stop=(j == 21)), reads=[ha, wD], writes=[ps])
                    f.op('dve', lambda e: e.tensor_tensor(xo_[:, cs], ps[:], xm_[:, cs], ALU.add), reads=[ps, xm_], writes=[xo_])
                f.dma('act', dst[rows, :], xo_[:], reads=[xo_])
            f.barrier()
    f.barrier()
    print("ninst", f.ninst, "nwaits", f.nwaits)
    top.close()
    return nc


BUILD_UPTO = 99
DEBUG_OUT = set()


def _consts(c):
    bf = ml_dtypes.bfloat16
    half = c % 2
    out = {}
    inv = np.power(np.float32(10000.0), -(np.arange(0, 64, 2, dtype=np.float32) / np.float32(64))).astype(np.float32)
    tabs = np.zeros((NTOK, 384), np.float32)
    for seg in range(2):
        t = (half * NL if seg == 0 else c * NL) + np.arange(NL)
        row = (t // 64).astype(np.float32); col = (t % 64).astype(np.float32)
        ar = row[:, None] * inv[None, :]; ac = col[:, None] * inv[None, :]; at = t.astype(np.float32)[:, None] * inv[None, :]
        cr, sr, cc, sc, ct, st = np.cos(ar), np.sin(ar), np.cos(ac), np.sin(ac), np.cos(at), np.sin(at)
        sl = slice(seg * NL, (seg + 1) * NL)
        tabs[sl, 0:128] = np.concatenate([cr, cr, cc, cc], 1)
        tabs[sl, 128:256] = np.concatenate([-sr, sr, -sc, sc], 1)
        tabs[sl, 256:320] = np.concatenate([ct, ct], 1)
        tabs[sl, 320:384] = np.concatenate([-st, st], 1)
    out["tabs"] = tabs
    out["ident"] = np.eye(128, dtype=np.float32).astype(bf)
    src = {}
    src[(0, 0)] = c - 1 if half == 1 else None
    src[(0, 1)] = c + 1 if half == 0 else None
    src[(1, 0)] = c - 1 if c > 0 else None
    src[(1, 1)] = c + 1 if c < 7 else None
    selI = np.zeros((128, 32, 128), np.float32)
    for (seg, side), r in src.items():
        if r is not None:
            selI[:, (seg * 2 + side) * 8 + r, :] = np.eye(128, dtype=np.float32)
    out["selI"] = selI.reshape(128, 32 * 128).astype(bf)
    bm = np.zeros((128, 34, 512), np.float32)
    p = np.arange(128)[:, None]; qi = np.arange(512)[None, :]
    ti = 0
    for g, (ntile, off, dil) in enumerate(((6, -128, 1), (8, -256, 4), (20, -1024, 16))):
        for rel in range(ntile):
            dlt = rel * 128 + p - qi + off
            bm[:, ti, :] = ((np.abs(dlt) <= 64 * dil) & (dlt % dil == 0)).astype(np.float32)
            ti += 1
    out["bmask"] = bm.reshape(128, 34 * 512).astype(bf)
    bv = np.zeros((128, 2, 32), np.float32)
    for seg in range(2):
        t0 = half * NL if seg == 0 else c * NL
        e = np.arange(32)[None, :] * 128 + np.arange(128)[:, None]
        pos = t0 + e - 1024
        bv[:, seg, :] = ((pos >= 0) & (pos < SEG_S[seg])).astype(np.float32)
    out["bvalid"] = bv.reshape(128, 64)
    cm = np.zeros((128, 2, 128), np.float32)
    s = np.arange(128)[:, None]; t = np.arange(128)[None, :]
    same = (s // 64) == (t // 64)
    cm[:, 0, :] = (same & (s <= t)).astype(np.float32)
    cm[:, 1, :] = (same & (s >= t)).astype(np.float32)
    out["cmask"] = cm.reshape(128, 256).astype(bf)
    cs = np.zeros((128, 16), np.float32)
    cs[:, c] = 1.0; cs[:, 8 + c] = 1.0
    out["cscan"] = cs
    chm = np.ones((128, NL), np.float32); chm[:, ::64] = 0.0
    out["chunkm"] = chm
    fs = np.zeros((32, 4), np.float32)
    for (seg, side), r in src.items():
        if r is not None:
            fs[r * 4 + seg * 2 + (1 if side == 0 else 0), seg * 2 + side] = 1.0
    out["fsel"] = fs.astype(bf)
    return out


def make_in_maps(inputs):
    maps = []
    wnames = ["g_mix", "w_in", "a_gq", "a_gk", "b_gq", "b_gk", "c_lb_fwd", "c_lb_bwd", "c_gnorm", "w_br_a", "w_br_b",
              "w_br_c", "w_mix_out", "g_cross", "g_mem", "x_wq", "x_wkv", "x_gq", "x_gk", "x_wo", "g_ffn", "f_wup",
              "f_conv", "f_conv_b", "f_wdown"]
    shared = {}
    for n in wnames:
        a = np.ascontiguousarray(inputs[n], dtype=np.float32)
        if n in ("b_gq", "b_gk"):
            a = a.reshape(2, 192)
        shared[n] = a
    xp = inputs["x_prompt"]; xs = inputs["x_sample"]
    for c in range(NCORES):
        b, half = c // 2, c % 2
        m = dict(shared)
        m["x"] = np.ascontiguousarray(np.concatenate([xp[b, half * NL:(half + 1) * NL], xs[0, c * NL:(c + 1) * NL]], 0), dtype=np.float32)
        m["mem"] = np.ascontiguousarray(np.concatenate([inputs["mem_prompt"][b], inputs["mem_sample"][0]], 0), dtype=np.float32)
        m.update(_consts(c))
        maps.append(m)
    return maps


_NC = None


def kernel(**inputs):
    global _NC
    if _NC is None:
        _NC = build()
    maps = make_in_maps(inputs)
    res = run_bass_kernel_spmd(_NC, maps, core_ids=list(range(NCORES)))
    yp = np.zeros((4, 4096, D), np.float32)
    ys = np.zeros((1, 16384, D), np.float32)
    for c in range(NCORES):
        y = res.results[c]["y"]
        b, half = c // 2, c % 2
        yp[b, half * NL:(half + 1) * NL] = y[0:NL]
        ys[0, c * NL:(c + 1) * NL] = y[NL:]
    return (yp, ys)
```

```python
import numpy as np
import ml_dtypes
from contextlib import ExitStack
import concourse.bass as bass
import concourse.mybir as mybir
from concourse.bass_utils import run_bass_kernel_spmd

F32 = mybir.dt.float32
BF16 = mybir.dt.bfloat16
AF = mybir.ActivationFunctionType
ALU = mybir.AluOpType
AX = mybir.AxisListType

NCORES = 8
D = 1024
NL = 2048
NTOK = 2 * NL
NT = NTOK // 128
INW = 12032
DFF = 2816
EPS = 1e-6
SEG_S = (4096, 16384)
PAIRS = [[0, 1], [2, 3], [4, 5], [6, 7]]
ALL8 = [list(range(8))]

TM_AQ, TM_AK, TM_AV, TM_BQ, TM_BK, TM_BV, TM_CI, TM_CG, TM_GT = 0, 1024, 1280, 1536, 2304, 3072, 3840, 4864, 5888
TMW = 8960


class Tile:
    def __init__(self, h, name):
        self.h = h; self.name = name
        self.writers = {}; self.readers = {}; self.dsem = None

    def __getitem__(self, idx):
        return self.h[idx]


class FW:
    def __init__(self, nc, es):
        self.nc = nc
        self.engs = {'pe': nc.tensor, 'act': nc.scalar, 'dve': nc.vector, 'pool': nc.gpsimd, 'sp': nc.sync}
        self.sems = {}; self.cnt = {}
        self.seen = {e: {} for e in self.engs}
        for e in self.engs:
            self.sems[e] = es.enter_context(nc.semaphore("s_" + e)); self.cnt[e] = 0
        self.ndpool = 40
        for i in range(self.ndpool):
            k = "d%d" % i
            self.sems[k] = es.enter_context(nc.semaphore("s_" + k)); self.cnt[k] = 0
        self.nd = 0
        self.ninst = 0; self.nwaits = 0
        self.uid = 0

    def sbuf(self, es, name, shape, dt):
        self.uid += 1
        return Tile(es.enter_context(self.nc.sbuf_tensor("%s_%d" % (name, self.uid), list(shape), dt)), name)

    def psum(self, es, name, shape, dt=F32):
        self.uid += 1
        return Tile(es.enter_context(self.nc.psum_tensor("%s_%d" % (name, self.uid), list(shape), dt)), name)

    def dram(self, name, shape, dt, kind=None):
        if kind is None and name in DEBUG_OUT:
            kind = "ExternalOutput"
        if kind is None:
            h = self.nc.dram_tensor(name, list(shape), dt)
        else:
            h = self.nc.dram_tensor(name, list(shape), dt, kind=kind)
        return Tile(h, name)

    def _dsem(self, t):
        if t.dsem is None:
            t.dsem = "d%d" % (self.nd % self.ndpool); self.nd += 1
        return t.dsem

    def _wait(self, e, key, c):
        if c <= 0 or self.seen[e].get(key, 0) >= c:
            return
        if e == 'pe' and key == 'pe':
            return
        self.engs[e].wait_ge(self.sems[key], c)
        self.seen[e][key] = c
        self.nwaits += 1

    def _deps(self, e, reads, writes):
        for t in reads:
            for k, c in t.writers.items():
                self._wait(e, k, self.cnt[k] if k[0] == 'd' else c)
        for t in writes:
            for k, c in t.writers.items():
                self._wait(e, k, self.cnt[k] if k[0] == 'd' else c)
            for k, c in t.readers.items():
                self._wait(e, k, self.cnt[k] if k[0] == 'd' else c)

    def _record(self, key, c, reads, writes):
        for t in reads:
            if t not in writes:
                t.readers[key] = c
        for t in writes:
            t.writers = {key: c}; t.readers = {}

    def op(self, e, fn, reads=(), writes=(), inc=True):
        self._deps(e, reads, writes)
        ins = fn(self.engs[e])
        self.ninst += 1
        inc = True
        if inc:
            self.cnt[e] += 1
            ins.then_inc(self.sems[e], 1)
            self._record(e, self.cnt[e], reads, writes)
        else:
            self._record(e, self.cnt[e] + 1, reads, writes)
        return ins

    def dma(self, q, out_ap, in_ap, reads=(), writes=(), **kw):
        self._deps(q, reads, writes)
        owner = (list(writes) + list(reads))[0]
        k = self._dsem(owner)
        ins = self.engs[q].dma_start(out=out_ap, in_=in_ap, **kw)
        self.cnt[k] += 16
        ins.then_inc(self.sems[k], 16)
        self.ninst += 1
        self._record(k, self.cnt[k], reads, writes)
        return ins

    def collective(self, kind, groups, tin, tout):
        e = 'pool'
        self._deps(e, [tin], [tout])
        self.cnt[e] += 1
        self.nc.gpsimd.collective_compute(kind, ALU.bypass, replica_groups=groups,
                                          ins=[tin.h.ap().opt()], outs=[tout.h.ap().opt()]).then_inc(self.sems[e], 1)
        self._record(e, self.cnt[e], [tin], [tout])

    def barrier(self):
        for e in self.engs:
            for k, c in self.cnt.items():
                self._wait(e, k, c)


class Ring:
    def __init__(self, tiles):
        self.t = tiles; self.i = 0

    def next(self):
        t = self.t[self.i % len(self.t)]; self.i += 1
        return t


def bc(ap, shape):
    return ap.broadcast_to(list(shape))


def build():
    nc = bass.Bass("TRN2", target_bir_lowering=False)
    top = ExitStack()
    f = FW(nc, top)
    I = {}

    def inp(name, shape, dt=F32):
        I[name] = f.dram(name, shape, dt, kind="ExternalInput")
        return I[name]

    x_in = inp("x", [NTOK, D])
    mem_in = inp("mem", [2 * 256, D])
    tabs = inp("tabs", [NTOK, 384])
    for nm, shp in (("g_mix", [2, D]), ("w_in", [2, D, INW]), ("a_gq", [2, 128]), ("a_gk", [2, 128]),
                    ("b_gq", [2, 192]), ("b_gk", [2, 192]), ("c_lb_fwd", [2, 128, 8]), ("c_lb_bwd", [2, 128, 8]),
                    ("c_gnorm", [2, D]), ("w_br_a", [2, D, D]), ("w_br_b", [2, 256, D]), ("w_br_c", [2, D, D]),
                    ("w_mix_out", [2, D, D]), ("g_cross", [2, D]), ("g_mem", [2, D]), ("x_wq", [2, D, D]),
                    ("x_wkv", [2, D, 2 * D]), ("x_gq", [2, 256]), ("x_gk", [2, 256]), ("x_wo", [2, D, D]),
                    ("g_ffn", [2, D]), ("f_wup", [2, D, 2 * DFF]), ("f_conv", [2, 128, 132]),
                    ("f_conv_b", [2, 128, 44]), ("f_wdown", [2, DFF, D])):
        inp(nm, shp)
    ident_in = inp("ident", [128, 128], BF16)
    selI_in = inp("selI", [128, 32 * 128], BF16)
    bmask_in = inp("bmask", [128, 34 * 512], BF16)
    bvalid_in = inp("bvalid", [128, 2 * 32])
    cmask_in = inp("cmask", [128, 2 * 128], BF16)
    cscan_in = inp("cscan", [128, 16])
    chunkm_in = inp("chunkm", [128, NL])
    fsel_in = inp("fsel", [32, 4], BF16)
    y_out = f.dram("y", [NTOK, D], F32, kind="ExternalOutput")

    xres = f.dram("xres", [NTOK, D], F32)
    projTM = f.dram("projTM", [NTOK, TMW], BF16)
    projFM = f.dram("projFM", [24 * 128, NTOK], F32)
    qTA = f.dram("qTA", [8 * 128, NTOK], BF16)
    akin = [f.dram("akin%d" % s, [256, NL], BF16) for s in range(2)]
    avin = [f.dram("avin%d" % s, [NL, 256], BF16) for s in range(2)]
    akout = [f.dram("akout0", [2 * 256, NL], BF16), f.dram("akout1", [8 * 256, NL], BF16)]
    avout = [f.dram("avout0", [2 * NL, 256], BF16), f.dram("avout1", [8 * NL, 256], BF16)]
    oAT = f.dram("oAT", [D, NTOK], BF16)
    qTB = f.dram("qTB", [768, NTOK], BF16)
    bkin = f.dram("bkin", [2 * 768, NL], BF16)
    bvin = f.dram("bvin", [2 * NL, 768], BF16)
    bkout = f.dram("bkout", [8 * 2 * 768, NL], BF16)
    bvout = f.dram("bvout", [8 * 2 * NL, 768], BF16)
    extK = f.dram("extK", [2 * 768, 4096], BF16)
    extV = f.dram("extV", [2 * 4096, 768], BF16)
    oBT = f.dram("oBT", [256, NTOK], BF16)
    cqT = f.dram("cqT", [2 * 1024, NTOK], BF16)
    ckT = f.dram("ckT", [2 * 1024, NTOK], BF16)
    ckTM = f.dram("ckTM", [2 * NTOK, 1024], BF16)
    cer = f.dram("cer", [256, 512], F32)
    cel = f.dram("cel", [256, 512], F32)
    cKI = f.dram("cKI", [2 * 2 * 32 * 128, 1024], F32)
    csin = f.dram("csin", [128, 4128], F32)
    csout = f.dram("csout", [1024, 4128], F32)
    coC = f.dram("coC", [2 * NTOK, 1024], F32)
    xmid = f.dram("xmid", [NTOK, D], F32)
    u3T = f.dram("u3T", [D, NTOK], BF16)
    fgin = f.dram("fgin", [4, D], BF16)
    fgout = f.dram("fgout", [32, D], BF16)
    hact = f.dram("hact", [DFF, NTOK], BF16)

    ident = f.sbuf(top, "ident", [128, 128], BF16)
    f.dma('sp', ident[:], ident_in[:, :], writes=[ident])

    zeroW = f.sbuf(top, "zeroW", [128, 128], BF16)
    f.op('dve', lambda e: e.memset(zeroW[:], 0.0), writes=[zeroW])

    def rms_rows(es, xt, width, rs, sq):
        f.op('dve', lambda e: e.tensor_tensor(sq[:, 0:width], xt[:, 0:width], xt[:, 0:width], ALU.mult),
             reads=[xt], writes=[sq])
        f.op('dve', lambda e: e.reduce_sum(rs[:, 0:1], sq[:, 0:width], AX.X), reads=[sq], writes=[rs])
        f.op('dve', lambda e: e.tensor_scalar(rs[:, 0:1], rs[:, 0:1], 1.0 / width, EPS, ALU.mult, ALU.add),
             reads=[rs], writes=[rs])
        f.op('act', lambda e: e.activation(out=rs[:, 0:1], in_=rs[:, 0:1], func=AF.Sqrt), reads=[rs], writes=[rs])
        f.op('dve', lambda e: e.reciprocal(rs[:, 0:1], rs[:, 0:1]), reads=[rs], writes=[rs])

    def transpose_blocks(src, nblk, psr, dst_fn, dst_tiles, src_off=0):
        k0 = 0
        gi = 0
        while k0 < nblk:
            n = min(4, nblk - k0)
            ps = psr.next()
            for k in range(n):
                f.op('pe', lambda e, k=k: e.matmul(ps[:, k, :], src[:, src_off + (k0 + k) * 128: src_off + (k0 + k + 1) * 128], ident[:],
                                                   start=True, stop=True),
                     reads=[src, ident], writes=[ps])
            if gi % 2 == 0:
                f.op('act', lambda e: e.copy(dst_fn(k0, n), ps[:, 0:n, :]), reads=[ps], writes=dst_tiles)
            else:
                f.op('dve', lambda e: e.tensor_copy(dst_fn(k0, n), ps[:, 0:n, :]), reads=[ps], writes=dst_tiles)
            k0 += n; gi += 1

    def load_bcast_row(es, name, src_ap_row, width, q='sp'):
        t = f.sbuf(es, name, [128, width], F32)
        f.dma(q, t[:], src_ap_row.partition_broadcast(128), writes=[t])
        return t

    for layer in range(2):
        xsrc = x_in if layer == 0 else xres

        with ExitStack() as es:
            uT = f.sbuf(es, "uT", [128, 8, NTOK], BF16)
            with ExitStack() as es2:
                gmix = load_bcast_row(es2, "gmix", I["g_mix"][layer:layer + 1, :], D)
                xin = Ring([f.sbuf(es2, "xin", [128, D], F32) for _ in range(2)])
                sq = f.sbuf(es2, "sq", [128, D], F32)
                rsr = Ring([f.sbuf(es2, "rs", [128, 1], F32) for _ in range(2)])
                unr = Ring([f.sbuf(es2, "un", [128, D], BF16) for _ in range(2)])
                pst = Ring([f.psum(es2, "pst", [128, 4, 128], F32) for _ in range(4)])
                for tt in range(NT):
                    xt = xin.next(); rs = rsr.next(); un = unr.next()
                    f.dma('sp', xt[:], xsrc[tt * 128:(tt + 1) * 128, :], writes=[xt])
                    rms_rows(es2, xt, D, rs, sq)
                    f.op('dve', lambda e: e.scalar_tensor_tensor(un[:], xt[:], rs[:, 0:1], gmix[:], ALU.mult, ALU.mult),
                         reads=[xt, rs, gmix], writes=[un])
                    transpose_blocks(un, 8, pst, lambda k0, n, tt=tt: uT[:, k0:k0 + n, tt * 128:(tt + 1) * 128], [uT])
                f.barrier()
            with ExitStack() as es2:
                w32 = Ring([f.sbuf(es2, "w32", [128, 8, 512], F32) for _ in range(2)])
                wbf = Ring([f.sbuf(es2, "wbf", [128, 8, 512], BF16) for _ in range(2)])
                psm = Ring([f.psum(es2, "psm", [128, 512], F32) for _ in range(4)])
                stg = Ring([f.sbuf(es2, "stg", [128, 512], BF16) for _ in range(4)])
                stgf = Ring([f.sbuf(es2, "stgf", [128, 512], F32) for _ in range(4)])
                wv = I["w_in"].h.ap()[layer].rearrange("(kc p) n -> p kc n", p=128)
                tm_blocks = []
                for (c0, n, t0) in ((0, 1024, TM_AQ), (1024, 512, TM_AK), (1536, 2304, TM_BQ), (4864, 1024, TM_CI),
                                    (7936, 4096, TM_CG)):
                    o = 0
                    while o < n:
                        w = min(512, n - o)
                        tm_blocks.append(('tm', c0 + o, w, t0 + o)); o += w
                fm_blocks = []
                for (c0, ch0) in ((3840, 0), (4352, 4), (5888, 8), (6400, 12), (6912, 16), (7424, 20)):
                    fm_blocks.append(('fm', c0, 512, ch0))
                evi = 0
                for (kind, c0, w, dst) in tm_blocks + fm_blocks:
                    wa = w32.next(); wb = wbf.next()
                    f.dma('sp', wa[:, :, 0:w], wv[:, :, c0:c0 + w], writes=[wa])
                    f.op('pool', lambda e: e.tensor_copy(wb[:, :, 0:w], wa[:, :, 0:w]), reads=[wa], writes=[wb])
                    if kind == 'tm':
                        for tt in range(NT):
                            ps = psm.next(); st = stg.next()
                            for kc in range(8):
                                f.op('pe', lambda e, kc=kc: e.matmul(ps[:, 0:w], uT[:, kc, tt * 128:(tt + 1) * 128], wb[:, kc, 0:w],
                                                                     start=(kc == 0), stop=(kc == 7)),
                                     reads=[uT, wb], writes=[ps], inc=(kc == 7))
                            if evi % 2 == 0:
                                f.op('act', lambda e: e.copy(st[:, 0:w], ps[:, 0:w]), reads=[ps], writes=[st])
                            else:
                                f.op('dve', lambda e: e.tensor_copy(st[:, 0:w], ps[:, 0:w]), reads=[ps], writes=[st])
                            evi += 1
                            f.dma('pool', projTM[tt * 128:(tt + 1) * 128, dst:dst + w], st[:, 0:w], reads=[st])
                    else:
                        for cc in range(4):
                            for tg in range(NTOK // 512):
                                ps = psm.next(); st = stgf.next()
                                for kc in range(8):
                                    f.op('pe', lambda e, kc=kc: e.matmul(ps[:], wb[:, kc, cc * 128:(cc + 1) * 128], uT[:, kc, tg * 512:(tg + 1) * 512],
                                                                         start=(kc == 0), stop=(kc == 7)),
                                         reads=[uT, wb], writes=[ps], inc=(kc == 7))
                                if evi % 2 == 0:
                                    f.op('act', lambda e: e.copy(st[:], ps[:]), reads=[ps], writes=[st])
                                else:
                                    f.op('dve', lambda e: e.tensor_copy(st[:], ps[:]), reads=[ps], writes=[st])
                                evi += 1
                                f.dma('pool', projFM[(dst + cc) * 128:(dst + cc + 1) * 128, tg * 512:(tg + 1) * 512], st[:], reads=[st])
                f.barrier()
            f.barrier()
        if layer == 0 and BUILD_UPTO == 1:
            break

        with ExitStack() as es:
            gA = f.sbuf(es, "gA", [128, 10, 128], F32)
            gB = f.sbuf(es, "gB", [128, 24, 64], F32)
            for h in range(10):
                srcg = I["a_gq"] if h < 8 else I["a_gk"]
                f.dma('sp', gA[:, h, :], srcg[layer:layer + 1, :].partition_broadcast(128), writes=[gA])
            for h in range(24):
                srcg = I["b_gq"] if h < 12 else I["b_gk"]
                g = (h % 12) // 4
                f.dma('sp', gB[:, h, :], srcg[layer:layer + 1, g * 64:(g + 1) * 64].partition_broadcast(128), writes=[gB])
            f.op('dve', lambda e: e.tensor_scalar(gA[:, 0:8, :], gA[:, 0:8, :], 128.0 ** -0.5, None, ALU.mult), reads=[gA], writes=[gA])
            f.op('dve', lambda e: e.tensor_scalar(gB[:, 0:12, :], gB[:, 0:12, :], 64.0 ** -0.5, None, ALU.mult), reads=[gB], writes=[gB])
            par = Ring([f.sbuf(es, "pa", [128, 3840], BF16) for _ in range(2)])
            tbr = Ring([f.sbuf(es, "tb", [128, 384], F32) for _ in range(2)])
            sqA = f.sbuf(es, "sqA", [128, 10, 128], F32)
            xnA = f.sbuf(es, "xnA", [128, 10, 128], F32)
            t2A = f.sbuf(es, "t2A", [128, 10, 128], F32)
            rsA = f.sbuf(es, "rsA", [128, 10], F32)
            roA = Ring([f.sbuf(es, "roA", [128, 1280], BF16) for _ in range(2)])
            sqB = f.sbuf(es, "sqB", [128, 24, 64], F32)
            xnB = f.sbuf(es, "xnB", [128, 24, 64], F32)
            t2B = f.sbuf(es, "t2B", [128, 24, 64], F32)
            rsB = f.sbuf(es, "rsB", [128, 24], F32)
            roB = Ring([f.sbuf(es, "roB", [128, 1536], BF16) for _ in range(2)])
            stA = Ring([f.sbuf(es, "stA", [128, 10, 128], BF16) for _ in range(2)])
            stB = Ring([f.sbuf(es, "stB", [128, 12, 128], BF16) for _ in range(2)])
            pst = Ring([f.psum(es, "pst2", [128, 4, 128], F32) for _ in range(4)])

            def normrope(E, src3, nh, hd, sq, xn, t2, rs, gt, ctab, stab, ro):
                f.op(E, lambda e: e.tensor_tensor(sq[:], src3, src3, ALU.mult), reads=[pa], writes=[sq])
                f.op('dve', lambda e: e.reduce_sum(rs[:], sq[:], AX.X), reads=[sq], writes=[rs])
                f.op('dve', lambda e: e.tensor_scalar(rs[:], rs[:], 1.0 / hd, EPS, ALU.mult, ALU.add), reads=[rs], writes=[rs])
                f.op('act', lambda e: e.activation(out=rs[:], in_=rs[:], func=AF.Sqrt), reads=[rs], writes=[rs])
                f.op('dve', lambda e: e.reciprocal(rs[:], rs[:]), reads=[rs], writes=[rs])
                f.op(E, lambda e: e.tensor_tensor(xn[:], src3, rs[:].unsqueeze(2).broadcast_to([128, nh, hd]), ALU.mult),
                     reads=[pa, rs], writes=[xn])
                f.op(E, lambda e: e.tensor_tensor(xn[:], xn[:], gt[:], ALU.mult), reads=[xn, gt], writes=[xn])
                hh = hd // 2 if hd == 64 else 32
                nb = hd // (2 * hh)
                xv = xn[:].rearrange("p h (b s j) -> p (h b) s j", b=nb, s=2)
                tv = t2[:].rearrange("p h (b s j) -> p (h b) s j", b=nb, s=2)
                if nb == 1:
                    sv = stab.rearrange("p (s j) -> p s j", s=2)
                    s0 = sv[:, 0:1, :].broadcast_to([128, nh, hh]); s1 = sv[:, 1:2, :].broadcast_to([128, nh, hh])
                    f.op(E, lambda e: e.tensor_tensor(tv[:, :, 0, :], xv[:, :, 1, :], s0, ALU.mult), reads=[xn, tb], writes=[t2])
                    f.op(E, lambda e: e.tensor_tensor(tv[:, :, 1, :], xv[:, :, 0, :], s1, ALU.mult), reads=[xn, tb], writes=[t2])
                else:
                    xv4 = xn[:].rearrange("p h (b s j) -> p h b s j", b=nb, s=2)
                    tv4 = t2[:].rearrange("p h (b s j) -> p h b s j", b=nb, s=2)
                    sv = stab.rearrange("p (b s j) -> p b s j", b=nb, s=2)
                    for bb in range(nb):
                        s0 = sv[:, bb, 0:1, :].broadcast_to([128, nh, hh]); s1 = sv[:, bb, 1:2, :].broadcast_to([128, nh, hh])
                        f.op(E, lambda e: e.tensor_tensor(tv4[:, :, bb, 0, :], xv4[:, :, bb, 1, :], s0, ALU.mult), reads=[xn, tb], writes=[t2])
                        f.op(E, lambda e: e.tensor_tensor(tv4[:, :, bb, 1, :], xv4[:, :, bb, 0, :], s1, ALU.mult), reads=[xn, tb], writes=[t2])
                f.op(E, lambda e: e.tensor_tensor(xn[:], xn[:], ctab.unsqueeze(1).broadcast_to([128, nh, hd]), ALU.mult),
                     reads=[xn, tb], writes=[xn])
                f.op(E, lambda e: e.tensor_tensor(ro[:].rearrange("p (h d) -> p h d", d=hd), xn[:], t2[:], ALU.add),
                     reads=[xn, t2], writes=[ro])

            for tt in range(NT):
                seg, lt = tt // 16, tt % 16
                pa = par.next(); tb = tbr.next(); ra = roA.next(); rb = roB.next(); sa = stA.next(); sb = stB.next()
                f.dma('sp', pa[:], projTM[tt * 128:(tt + 1) * 128, 0:3840], writes=[pa])
                f.dma('sp', tb[:], tabs[tt * 128:(tt + 1) * 128, :], writes=[tb])
                normrope('dve', pa[:, 0:1280].rearrange("p (h d) -> p h d", d=128), 10, 128, sqA, xnA, t2A, rsA, gA,
                         tb[:, 0:128], tb[:, 128:256], ra)
                normrope('pool', pa[:, 1536:3072].rearrange("p (h d) -> p h d", d=64), 24, 64, sqB, xnB, t2B, rsB, gB,
                         tb[:, 256:320], tb[:, 320:384], rb)
                transpose_blocks(ra, 10, pst, lambda k0, n: sa[:, k0:k0 + n, :], [sa])
                transpose_blocks(rb, 12, pst, lambda k0, n: sb[:, k0:k0 + n, :], [sb])
                tc = slice(tt * 128, (tt + 1) * 128); lc = slice(lt * 128, (lt + 1) * 128)
                f.dma('act', qTA.h.ap().rearrange("(h d) t -> d h t", d=128)[:, :, tc], sa[:, 0:8, :], reads=[sa])
                f.dma('act', akin[seg].h.ap().rearrange("(h d) t -> d h t", d=128)[:, :, lc], sa[:, 8:10, :], reads=[sa])
                f.dma('act', avin[seg][lc, :], pa[:, 1280:1536], reads=[pa])
                f.dma('act', qTB.h.ap().rearrange("(j p) t -> p j t", p=128)[:, :, tc], sb[:, 0:6, :], reads=[sb])
                f.dma('act', bkin.h.ap()[seg * 768:(seg + 1) * 768].rearrange("(j p) t -> p j t", p=128)[:, :, lc], sb[:, 6:12, :], reads=[sb])
                f.dma('act', bvin[seg * NL + lt * 128: seg * NL + (lt + 1) * 128, :], pa[:, 3072:3840], reads=[pa])
            f.barrier()
        f.collective("AllGather", PAIRS, akin[0], akout[0])
        f.collective("AllGather", PAIRS, avin[0], avout[0])
        f.collective("AllGather", ALL8, akin[1], akout[1])
        f.collective("AllGather", ALL8, avin[1], avout[1])
        f.collective("AllGather", ALL8, bkin, bkout)
        f.collective("AllGather", ALL8, bvin, bvout)
        if layer == 0 and BUILD_UPTO == 3:
            break

        with ExitStack() as es:
            lbt = f.sbuf(es, "lbt", [128, 2, 8], F32)
            oml = f.sbuf(es, "oml", [128, 2, 8], F32)
            if layer == 0:
                f.op('dve', lambda e: e.memset(lbt[:], 0.0), writes=[lbt])
                f.op('dve', lambda e: e.memset(oml[:], 1.0), writes=[oml])
            else:
                raw = f.sbuf(es, "raw", [128, 2, 2, 8], F32)
                for dr, nm in enumerate(("c_lb_fwd", "c_lb_bwd")):
                    for l2 in range(2):
                        f.dma('sp', raw[:, dr, l2, :], I[nm].h.ap()[l2], writes=[raw])
                f.op('dve', lambda e: e.tensor_tensor(lbt[:], raw[:, :, 1, :], raw[:, :, 0, :], ALU.subtract), reads=[raw], writes=[lbt])
                f.op('act', lambda e: e.activation(out=lbt[:], in_=lbt[:], func=AF.Sigmoid), reads=[lbt], writes=[lbt])
                f.op('dve', lambda e: e.tensor_scalar(oml[:], lbt[:], -1.0, 1.0, ALU.mult, ALU.add), reads=[lbt], writes=[oml])
            chm = f.sbuf(es, "chm", [128, NL], F32)
            f.dma('sp', chm[:], chunkm_in[:, :], writes=[chm])
            W = NL
            zq = f.sbuf(es, "zq", [128, W], F32); qs = f.sbuf(es, "qs", [128, W], F32)
            zz = f.sbuf(es, "zz", [128, W], F32); sg = f.sbuf(es, "sg", [128, W], F32)
            lf = f.sbuf(es, "lf", [128, W], F32); kk = f.sbuf(es, "kk", [128, W], F32)
            PP = f.sbuf(es, "PP", [128, W], F32); br = f.sbuf(es, "br", [128, W], F32)
            eq = f.sbuf(es, "eq", [128, W], F32); ek = f.sbuf(es, "ek", [128, W], F32)
            qo = Ring([f.sbuf(es, "qo", [128, W], BF16) for _ in range(2)])
            ko = Ring([f.sbuf(es, "ko", [128, W], BF16) for _ in range(2)])
            erl = Ring([f.sbuf(es, "erl", [128, 2, 32], F32) for _ in range(2)])
            att = Ring([f.sbuf(es, "att", [128, 2], F32) for _ in range(2)])
            pst = Ring([f.psum(es, "pst6", [128, 4, 128], F32) for _ in range(4)])
            stT = Ring([f.sbuf(es, "stT", [128, 4, 128], BF16) for _ in range(3)])
            v3 = lambda t: t[:].rearrange("p (j t) -> p j t", t=64)
            for h in range(8):
                for seg in range(2):
                    cs = slice(seg * NL, (seg + 1) * NL)
                    f.dma('sp', zq[:], projFM[h * 128:(h + 1) * 128, cs], writes=[zq])
                    f.op('act', lambda e: e.activation(out=qs[:], in_=zq[:], func=AF.Silu), reads=[zq], writes=[qs])
                    for dr in range(2):
                        f.dma('sp', zz[:], projFM[(8 + 8 * dr + h) * 128:(9 + 8 * dr + h) * 128, cs], writes=[zz])
                        f.op('dve', lambda e: e.tensor_scalar(zz[:], zz[:], -30.0, 30.0, ALU.max, ALU.min), reads=[zz], writes=[zz])
                        f.op('act', lambda e: e.activation(out=sg[:], in_=zz[:], func=AF.Sigmoid), reads=[zz], writes=[sg])
                        f.op('dve', lambda e: e.tensor_scalar(sg[:], sg[:], oml[:, dr, h:h + 1], lbt[:, dr, h:h + 1], ALU.mult, ALU.add),
                             reads=[sg, oml, lbt], writes=[sg])
                        f.op('act', lambda e: e.activation(out=lf[:], in_=sg[:], func=AF.Ln), reads=[sg], writes=[lf])
                        f.op('pool', lambda e: e.tensor_scalar(kk[:], sg[:], -1.0, 1.0, ALU.mult, ALU.add), reads=[sg], writes=[kk])
                        f.op('dve', lambda e: e.tensor_tensor_scan(PP[:], chm[:], lf[:], 0.0, ALU.mult, ALU.add), reads=[chm, lf], writes=[PP])
                        er_ = erl.next(); at_ = att.next()
                        if dr == 0:
                            f.op('dve', lambda e: e.tensor_tensor(v3(br), v3(PP), v3(PP)[:, :, 31:32].broadcast_to([128, 32, 64]), ALU.subtract),
                                 reads=[PP], writes=[br])
                            f.op('act', lambda e: e.activation(out=er_[:, 0, :], in_=v3(PP)[:, :, 31], func=AF.Exp), reads=[PP], writes=[er_])
                            f.op('act', lambda e: e.activation(out=er_[:, 1, :], in_=v3(br)[:, :, 63], func=AF.Exp), reads=[br], writes=[er_])
                        else:
                            f.op('dve', lambda e: e.tensor_tensor(v3(eq), v3(lf), v3(PP), ALU.subtract), reads=[lf, PP], writes=[eq])
                            f.op('dve', lambda e: e.tensor_tensor(v3(eq), v3(eq), v3(PP)[:, :, 63:64].broadcast_to([128, 32, 64]), ALU.add),
                                 reads=[eq, PP], writes=[eq])
                            f.op('dve', lambda e: e.tensor_tensor(v3(br), v3(eq), v3(eq)[:, :, 32:33].broadcast_to([128, 32, 64]), ALU.subtract),
                                 reads=[eq], writes=[br])
                            f.op('act', lambda e: e.activation(out=er_[:, 0, :], in_=v3(eq)[:, :, 32], func=AF.Exp), reads=[eq], writes=[er_])
                            f.op('act', lambda e: e.activation(out=er_[:, 1, :], in_=v3(br)[:, :, 0], func=AF.Exp), reads=[br], writes=[er_])
                        f.op('dve', lambda e: e.reduce_sum(at_[:, 0:1], v3(PP)[:, :, 63], AX.X), reads=[PP], writes=[at_])
                        f.op('act', lambda e: e.activation(out=at_[:, 1:2], in_=at_[:, 0:1], func=AF.Exp), reads=[at_], writes=[at_])
                        ds_ = dr * 2 + seg
                        f.dma('act', csin[:, 4096 + ds_ * 8 + h: 4096 + ds_ * 8 + h + 1], at_[:, 1:2], reads=[at_], allow_slow_non_contiguous=True)
                        f.dma('act', cer[dr * 128:(dr + 1) * 128, h * 64 + seg * 32: h * 64 + seg * 32 + 32], er_[:, 0, :], reads=[er_])
                        f.dma('act', cel[dr * 128:(dr + 1) * 128, h * 64 + seg * 32: h * 64 + seg * 32 + 32], er_[:, 1, :], reads=[er_])
                        f.op('act', lambda e: e.activation(out=eq[:], in_=br[:], func=AF.Exp), reads=[br], writes=[eq])
                        f.op('dve', lambda e: e.tensor_scalar(br[:], br[:], -80.0, None, ALU.max), reads=[br], writes=[br])
                        f.op('act', lambda e: e.activation(out=ek[:], in_=br[:], func=AF.Exp, scale=-1.0), reads=[br], writes=[ek])
                        qo_ = qo.next(); ko_ = ko.next()
                        f.op('pool', lambda e: e.tensor_tensor(qo_[:], qs[:], eq[:], ALU.mult), reads=[qs, eq], writes=[qo_])
                        f.op('dve', lambda e: e.tensor_tensor(ko_[:], kk[:], ek[:], ALU.mult), reads=[kk, ek], writes=[ko_])
                        f.dma('sp', cqT[dr * 1024 + h * 128: dr * 1024 + (h + 1) * 128, cs], qo_[:], reads=[qo_])
                        f.dma('sp', ckT[dr * 1024 + h * 128: dr * 1024 + (h + 1) * 128, cs], ko_[:], reads=[ko_])
                        for g4 in range(4):
                            ps = pst.next(); st = stT.next()
                            for k in range(4):
                                bk_ = g4 * 4 + k
                                f.op('pe', lambda e: e.matmul(ps[:, k, :], ko_[:, bk_ * 128:(bk_ + 1) * 128], ident[:], start=True, stop=True),
                                     reads=[ko_, ident], writes=[ps])
                            f.op('act', lambda e: e.copy(st[:], ps[:]), reads=[ps], writes=[st])
                            r0 = dr * NTOK + seg * NL + g4 * 512
                            f.dma('sp', ckTM.h.ap()[r0:r0 + 512, h * 128:(h + 1) * 128].rearrange("(j p) c -> p j c", p=128), st[:], reads=[st])
            f.barrier()
        if layer == 0 and BUILD_UPTO == 6.1:
            break
        with ExitStack() as es:
            ktm = Ring([f.sbuf(es, "ktm", [128, 1024], BF16) for _ in range(2)])
            cit = Ring([f.sbuf(es, "cit", [128, 1024], BF16) for _ in range(2)])
            psk = Ring([f.psum(es, "psk", [128, 4, 128], F32) for _ in range(4)])
            stki = Ring([f.sbuf(es, "stki", [128, 8, 128], F32) for _ in range(3)])
            for dr in range(2):
                for tt in range(NT):
                    seg, lt = tt // 16, tt % 16
                    km = ktm.next(); ci_ = cit.next()
                    f.dma('sp', km[:], ckTM[dr * NTOK + tt * 128: dr * NTOK + (tt + 1) * 128, :], writes=[km])
                    f.dma('sp', ci_[:], projTM[tt * 128:(tt + 1) * 128, TM_CI:TM_CI + 1024], writes=[ci_])
                    for half in range(2):
                        sk = stki.next()
                        hs = slice(half * 64, (half + 1) * 64)
                        for hg in range(2):
                            ps = psk.next()
                            for h4 in range(4):
                                h = hg * 4 + h4
                                f.op('pe', lambda e: e.matmul(ps[:, h4, :], km[hs, h * 128:(h + 1) * 128], ci_[hs, h * 128:(h + 1) * 128], start=True, stop=True),
                                     reads=[km, ci_], writes=[ps])
                            if hg == 0:
                                f.op('act', lambda e: e.copy(sk[:, 0:4, :], ps[:]), reads=[ps], writes=[sk])
                            else:
                                f.op('dve', lambda e: e.tensor_copy(sk[:, 4:8, :], ps[:]), reads=[ps], writes=[sk])
                        chunk = lt * 2 + half
                        r0 = ((dr * 2 + seg) * 32 + chunk) * 128
                        f.dma('act', cKI[r0:r0 + 128, :], sk[:].rearrange("p h v -> p (h v)"), reads=[sk])
            f.barrier()
        if layer == 0 and BUILD_UPTO == 6.2:
            break
        with ExitStack() as es:
            sins = [f.sbuf(es, "sin%d" % i, [128, 8, 128], F32) for i in range(4)]
            oh = f.sbuf(es, "oh", [128, 16], F32)
            f.dma('sp', oh[:], cscan_in[:, :], writes=[oh])
            cm = f.sbuf(es, "cm", [128, 2, 128], BF16)
            f.dma('sp', cm[:], cmask_in.h.ap().rearrange("p (a b) -> p a b", a=2), writes=[cm])

            def load_erl(es2, dr, seg):
                er = f.sbuf(es2, "er", [128, 8, 32], F32); el = f.sbuf(es2, "el", [128, 8, 32], F32)
                f.dma('sp', er[:], cer.h.ap()[dr * 128:(dr + 1) * 128].rearrange("p (h s j) -> p h s j", h=8, s=2)[:, :, seg, :], writes=[er])
                f.dma('sp', el[:], cel.h.ap()[dr * 128:(dr + 1) * 128].rearrange("p (h s j) -> p h s j", h=8, s=2)[:, :, seg, :], writes=[el])
                return er, el

            def bcj(t, j):
                return t[:, :, j:j + 1].broadcast_to([128, 8, 128])

            with ExitStack() as es2:
                state = f.sbuf(es2, "state", [128, 8, 128], F32)
                tmp = f.sbuf(es2, "tmp", [128, 8, 128], F32)
                kir = Ring([f.sbuf(es2, "ki", [128, 8, 128], F32) for _ in range(4)])
                for dr in range(2):
                    for seg in range(2):
                        with ExitStack() as es3:
                            er, el = load_erl(es3, dr, seg)
                            f.op('dve', lambda e: e.memset(state[:], 0.0), writes=[state])
                            order = range(32) if dr == 0 else range(31, -1, -1)
                            for j in order:
                                ki = kir.next()
                                r0 = ((dr * 2 + seg) * 32 + j) * 128
                                f.dma('sp', ki[:].rearrange("p h v -> p (h v)"), cKI[r0:r0 + 128, :], writes=[ki])
                                f.op('dve', lambda e: e.tensor_tensor(tmp[:], state[:], bcj(er, j), ALU.mult), reads=[state, er], writes=[tmp])
                                f.op('pool', lambda e: e.tensor_tensor(tmp[:], tmp[:], ki[:], ALU.add), reads=[tmp, ki], writes=[tmp])
                                f.op('dve', lambda e: e.tensor_tensor(state[:], tmp[:], bcj(el, j), ALU.mult), reads=[tmp, el], writes=[state])
                            ds_ = dr * 2 + seg
                            f.dma('act', csin[:, ds_ * 1024:(ds_ + 1) * 1024], state[:].rearrange("p h v -> p (h v)"), reads=[state])
                            f.barrier()
                f.barrier()
            f.collective("AllGather", ALL8, csin, csout)
            f.barrier()
            with ExitStack() as es2:
                G = f.sbuf(es2, "G", [128, 8, 1024], F32)
                At = f.sbuf(es2, "At", [128, 8, 8], F32)
                acc = f.sbuf(es2, "accs", [128, 8, 128], F32)
                gv = csout.h.ap().rearrange("(r p) n -> p r n", p=128)
                for dr in range(2):
                    for seg in range(2):
                        ds_ = dr * 2 + seg
                        sin_ = sins[ds_]
                        f.dma('sp', G[:], gv[:, :, ds_ * 1024:(ds_ + 1) * 1024], writes=[G])
                        f.dma('sp', At[:], gv[:, :, 4096 + ds_ * 8: 4096 + ds_ * 8 + 8], writes=[At])
                        f.op('dve', lambda e: e.memset(acc[:], 0.0), writes=[acc])
                        f.op('dve', lambda e: e.memset(sin_[:], 0.0), writes=[sin_])
                        order = range(8) if dr == 0 else range(7, -1, -1)
                        for r in order:
                            if seg == 0 and ((dr == 0 and r % 2 == 0) or (dr == 1 and r % 2 == 1)):
                                f.op('dve', lambda e: e.memset(acc[:], 0.0), writes=[acc])
                            f.op('dve', lambda e: e.scalar_tensor_tensor(sin_[:], acc[:], oh[:, dr * 8 + r: dr * 8 + r + 1], sin_[:], ALU.mult, ALU.add),
                                 reads=[acc, oh, sin_], writes=[sin_])
                            f.op('dve', lambda e: e.tensor_tensor(acc[:], acc[:], At[:, r, :].unsqueeze(2).broadcast_to([128, 8, 128]), ALU.mult),
                                 reads=[acc, At], writes=[acc])
                            f.op('dve', lambda e: e.tensor_tensor(acc[:], acc[:], G[:, r, :].rearrange("p (h v) -> p h v", v=128), ALU.add),
                                 reads=[acc, G], writes=[acc])
                f.barrier()
            with ExitStack() as es2:
                state = f.sbuf(es2, "state2", [128, 8, 128], F32)
                tmp = f.sbuf(es2, "tmp2", [128, 8, 128], F32)
                kir = Ring([f.sbuf(es2, "ki2", [128, 8, 128], F32) for _ in range(4)])
                Ebr = Ring([f.sbuf(es2, "Eb", [128, 8, 128], BF16) for _ in range(3)])
                qT_ = f.sbuf(es2, "cqTs", [128, 8, NL], BF16)
                kT_ = f.sbuf(es2, "ckTs", [128, 8, NL], BF16)
                cia = f.sbuf(es2, "cia", [128, 16, 1024], BF16)
                PTr = Ring([f.sbuf(es2, "PTc", [128, 8, 128], BF16) for _ in range(2)])
                psc = Ring([f.psum(es2, "psc", [128, 4, 128], F32) for _ in range(2)])
                por = Ring([f.psum(es2, "po", [64, 8, 128], F32) for _ in range(2)])
                osr = Ring([f.sbuf(es2, "osc", [64, 8, 128], F32) for _ in range(3)])
                for dr in range(2):
                    for seg in range(2):
                        with ExitStack() as es3:
                            er, el = load_erl(es3, dr, seg)
                            cs = slice(seg * NL, (seg + 1) * NL)
                            f.dma('sp', qT_[:], cqT.h.ap()[dr * 1024:(dr + 1) * 1024, cs].rearrange("(h p) t -> p h t", p=128), writes=[qT_])
                            f.dma('sp', kT_[:], ckT.h.ap()[dr * 1024:(dr + 1) * 1024, cs].rearrange("(h p) t -> p h t", p=128), writes=[kT_])
                            f.dma('sp', cia[:], projTM.h.ap()[seg * NL:(seg + 1) * NL, TM_CI:TM_CI + 1024].rearrange("(j p) c -> p j c", p=128), writes=[cia])
                            f.op('dve', lambda e: e.tensor_copy(state[:], sins[dr * 2 + seg][:]), reads=[sins[dr * 2 + seg]], writes=[state])
                            order = range(32) if dr == 0 else range(31, -1, -1)
                            PT = None
                            for step, j in enumerate(order):
                                lt, half = j // 2, j % 2
                                tc = slice(lt * 128, (lt + 1) * 128)
                                if step % 2 == 0:
                                    PT = PTr.next()
                                    for hg in range(2):
                                        ps = psc.next()
                                        for h4 in range(4):
                                            h = hg * 4 + h4
                                            f.op('pe', lambda e: e.matmul(ps[:, h4, :], kT_[:, h, tc], qT_[:, h, tc], start=True, stop=True),
                                                 reads=[kT_, qT_], writes=[ps])
                                        f.op('dve', lambda e: e.tensor_tensor(PT[:, hg * 4:(hg + 1) * 4, :], ps[:], cm[:, dr:dr + 1, :].broadcast_to([128, 4, 128]), ALU.mult),
                                             reads=[ps, cm], writes=[PT])
                                ki = kir.next(); Eb = Ebr.next()
                                r0 = ((dr * 2 + seg) * 32 + j) * 128
                                f.dma('sp', ki[:].rearrange("p h v -> p (h v)"), cKI[r0:r0 + 128, :], writes=[ki])
                                f.op('dve', lambda e: e.tensor_tensor(tmp[:], state[:], bcj(er, j), ALU.mult), reads=[state, er], writes=[tmp])
                                f.op('act', lambda e: e.copy(Eb[:], tmp[:]), reads=[tmp], writes=[Eb])
                                f.op('pool', lambda e: e.tensor_tensor(tmp[:], tmp[:], ki[:], ALU.add), reads=[tmp, ki, Eb], writes=[tmp])
                                f.op('dve', lambda e: e.tensor_tensor(state[:], tmp[:], bcj(el, j), ALU.mult), reads=[tmp, el], writes=[state])
                                po = por.next(); os_ = osr.next()
                                for bnk in range(2):
                                    f.op('pe', lambda e: e.matmul(po[:, bnk * 4:(bnk + 1) * 4, :].rearrange("p a b -> p (a b)"), zeroW[:, 0:64], cia[:, lt, 0:512], start=True, stop=False),
                                         reads=[zeroW, cia], writes=[po])
                                for h in range(8):
                                    f.op('pe', lambda e: e.matmul(po[:, h, :], PT[:, h, half * 64:(half + 1) * 64], cia[:, lt, h * 128:(h + 1) * 128], start=False, stop=False),
                                         reads=[PT, cia], writes=[po])
                                    f.op('pe', lambda e: e.matmul(po[:, h, :], qT_[:, h, j * 64:(j + 1) * 64], Eb[:, h, :], start=False, stop=True),
                                         reads=[qT_, Eb], writes=[po])
                                if step % 2 == 0:
                                    f.op('act', lambda e: e.copy(os_[:], po[:]), reads=[po], writes=[os_])
                                else:
                                    f.op('dve', lambda e: e.tensor_copy(os_[:], po[:]), reads=[po], writes=[os_])
                                r1 = dr * NTOK + seg * NL + j * 64
                                f.dma('act', coC[r1:r1 + 64, :], os_[:].rearrange("p h v -> p (h v)"), reads=[os_])
                            f.barrier()
                f.barrier()
            f.barrier()
        if layer == 0 and BUILD_UPTO == 6:
            break

        with ExitStack() as es:
            kT = f.sbuf(es, "kT", [128, 16384], BF16)
            vA = f.sbuf(es, "vA", [128, 128, 128], BF16)
            onesA = f.sbuf(es, "onesA", [128, 128], BF16)
            f.op('dve', lambda e: e.memset(onesA[:], 1.0), writes=[onesA])
            qtr = Ring([f.sbuf(es, "qt", [128, 512], BF16) for _ in range(3)])
            psS = Ring([f.psum(es, "psS", [128, 2, 512], F32) for _ in range(2)])
            accNr = Ring([f.psum(es, "accN", [128, 512], F32) for _ in range(2)])
            accDr = Ring([f.psum(es, "accD", [128, 512], F32) for _ in range(2)])
            pTr = Ring([f.sbuf(es, "pT", [128, 2, 512], BF16) for _ in range(4)])
            rcr = Ring([f.sbuf(es, "rc", [128, 512], F32) for _ in range(2)])
            ostr = Ring([f.sbuf(es, "ost", [128, 512], BF16) for _ in range(2)])
            daccr = Ring([f.sbuf(es, "dacc", [128, 2, 512], F32) for _ in range(2)])
            dsbr = Ring([f.sbuf(es, "dsb", [128, 512], BF16) for _ in range(2)])
            for seg in range(2):
                S = SEG_S[seg]; nkt = S // 128; R = S // NL
                ngrp = nkt // 2
                for kvh in range(2):
                    f.dma('sp', kT[:, 0:S].rearrange("p (r t) -> p r t", r=R),
                          akout[seg].h.ap().rearrange("(r hd) t -> hd r t", hd=256)[kvh * 128:(kvh + 1) * 128], writes=[kT])
                    vsrc = avout[seg].h.ap().rearrange("(j p) c -> p j c", p=128)
                    for j0 in range(0, nkt, 16):
                        f.dma('sp', vA[:, j0:j0 + 16, :], vsrc[:, j0:j0 + 16, kvh * 128:(kvh + 1) * 128], writes=[vA])
                    pending = None

                    def stage2(p):
                        pT, kg, aN, aD, h, q0, dacc = p
                        for j in range(2):
                            kt = 2 * kg + j
                            f.op('pe', lambda e: e.matmul(aN[:], vA[:, kt, :], pT[:, j, :], start=(kt == 0), stop=(kt == nkt - 1)),
                                 reads=[pT, vA], writes=[aN])
                            f.op('pe', lambda e: e.matmul(aD[:], onesA[:], pT[:, j, :], start=(kt == 0), stop=(kt == nkt - 1)),
                                 reads=[pT, onesA], writes=[aD])
                        if kg == ngrp - 1:
                            rc = rcr.next(); ost = ostr.next()
                            f.op('dve', lambda e: e.reciprocal(rc[:], aD[:]), reads=[aD], writes=[rc])
                            f.op('dve', lambda e: e.tensor_tensor(ost[:], aN[:], rc[:], ALU.mult), reads=[aN, rc], writes=[ost])
                            f.dma('act', oAT[h * 128:(h + 1) * 128, q0:q0 + 512], ost[:], reads=[ost])

                    for g in range(4):
                        h = kvh * 4 + g
                        for qb in range(4):
                            qt = qtr.next()
                            q0 = seg * NL + qb * 512
                            f.dma('sp', qt[:], qTA[h * 128:(h + 1) * 128, q0:q0 + 512], writes=[qt])
                            aN = accNr.next(); aD = accDr.next(); dacc = daccr.next()
                            for kg in range(ngrp):
                                ps = psS.next(); pT = pTr.next()
                                for j in range(2):
                                    kt = 2 * kg + j
                                    f.op('pe', lambda e: e.matmul(ps[:, j, :], kT[:, kt * 128:(kt + 1) * 128], qt[:], start=True, stop=True),
                                         reads=[kT, qt], writes=[ps])
                                f.op('act', lambda e: e.activation(out=pT[:], in_=ps[:], func=AF.Exp), reads=[ps], writes=[pT])
                                if pending is not None:
                                    stage2(pending)
                                pending = (pT, kg, aN, aD, h, q0, dacc)
                    stage2(pending)
            f.barrier()
        if layer == 0 and BUILD_UPTO == 4:
            break

        with ExitStack() as es:
            selI = f.sbuf(es, "selI", [128, 32, 128], BF16)
            f.dma('sp', selI[:], selI_in.h.ap().rearrange("p (i m) -> p i m", m=128), writes=[selI])
            candK = Ring([f.sbuf(es, "candK", [128, 1024], BF16) for _ in range(4)])
            candV = Ring([f.sbuf(es, "candV", [128, 768], BF16) for _ in range(4)])
            psE = Ring([f.psum(es, "psE", [128, 2, 512], F32) for _ in range(3)])
            steK = Ring([f.sbuf(es, "steK", [128, 2, 512], BF16) for _ in range(2)])
            steV = Ring([f.sbuf(es, "steV", [128, 2, 384], BF16) for _ in range(2)])
            for seg in range(2):
                f.dma('sp', extK[seg * 768:(seg + 1) * 768, 1024:3072], bkin[seg * 768:(seg + 1) * 768, :], reads=[bkin], writes=[extK])
                f.dma('sp', extV[seg * 4096 + 1024: seg * 4096 + 3072, :], bvin[seg * NL:(seg + 1) * NL, :], reads=[bvin], writes=[extV])
                for side in range(2):
                    scol = 1024 if side == 0 else 0
                    ecol = 0 if side == 0 else 3072
                    si = (seg * 2 + side) * 8
                    for hp in range(6):
                        ps = psE.next()
                        for r in range(8):
                            cd = candK.next()
                            r0 = r * 1536 + seg * 768 + hp * 128
                            f.dma('sp', cd[:], bkout[r0:r0 + 128, scol:scol + 1024], writes=[cd])
                            for j in range(2):
                                f.op('pe', lambda e: e.matmul(ps[:, j, :], selI[:, si + r, :], cd[:, j * 512:(j + 1) * 512], start=(r == 0), stop=(r == 7)),
                                     reads=[selI, cd], writes=[ps])
                        st = steK.next()
                        f.op('act', lambda e: e.copy(st[:], ps[:]), reads=[ps], writes=[st])
                        f.dma('act', extK[seg * 768 + hp * 128: seg * 768 + (hp + 1) * 128, ecol:ecol + 1024].rearrange("p (j n) -> p j n", j=2), st[:], reads=[st])
                    for kt in range(8):
                        ps = psE.next()
                        for r in range(8):
                            cv = candV.next()
                            r0 = r * 4096 + seg * 2048 + scol + kt * 128
                            f.dma('sp', cv[:], bvout[r0:r0 + 128, :], writes=[cv])
                            for j in range(2):
                                f.op('pe', lambda e: e.matmul(ps[:, j, 0:384], selI[:, si + r, :], cv[:, j * 384:(j + 1) * 384], start=(r == 0), stop=(r == 7)),
                                     reads=[selI, cv], writes=[ps])
                        st = steV.next()
                        f.op('dve', lambda e: e.tensor_copy(st[:], ps[:, :, 0:384]), reads=[ps], writes=[st])
                        e0 = seg * 4096 + ecol + kt * 128
                        f.dma('act', extV[e0:e0 + 128, :].rearrange("p (j n) -> p j n", j=2), st[:], reads=[st])
            f.barrier()
        with ExitStack() as es:
            bmask = f.sbuf(es, "bmask", [128, 34, 512], BF16)
            f.dma('sp', bmask[:], bmask_in.h.ap().rearrange("p (i m) -> p i m", m=512), writes=[bmask])
            bval = f.sbuf(es, "bval", [128, 64], F32)
            f.dma('sp', bval[:], bvalid_in[:, :], writes=[bval])
            kTe = f.sbuf(es, "kTe", [128, 6, 4096], BF16)
            vext = f.sbuf(es, "vext", [128, 32, 768], BF16)
            vones = f.sbuf(es, "vones", [128, 32, 64], BF16)
            qtbr = Ring([f.sbuf(es, "qtb", [128, 6, 512], BF16) for _ in range(2)])
            psB = Ring([f.psum(es, "psB", [128, 512], F32) for _ in range(4)])
            accBNr = Ring([f.psum(es, "accBN", [64, 512], F32) for _ in range(2)])
            accBDr = Ring([f.psum(es, "accBD", [64, 512], F32) for _ in range(2)])
            pTb = Ring([f.sbuf(es, "pTb", [128, 512], BF16) for _ in range(4)])
            pmb = Ring([f.sbuf(es, "pmb", [128, 512], BF16) for _ in range(5)])
            rcb = Ring([f.sbuf(es, "rcb", [64, 512], F32) for _ in range(2)])
            ostb = Ring([f.sbuf(es, "ostb", [64, 512], BF16) for _ in range(2)])
            for seg in range(2):
                f.dma('sp', kTe[:], extK.h.ap()[seg * 768:(seg + 1) * 768].rearrange("(j p) t -> p j t", p=128), writes=[kTe])
                for ch in range(4):
                    f.dma('sp', vext[:, ch * 8:(ch + 1) * 8, :],
                          extV.h.ap()[seg * 4096 + ch * 1024: seg * 4096 + (ch + 1) * 1024].rearrange("(j p) c -> p j c", p=128), writes=[vext])
                f.op('dve', lambda e: e.tensor_copy(vones[:], bval[:, seg * 32:(seg + 1) * 32].unsqueeze(2).broadcast_to([128, 32, 64])),
                     reads=[bval], writes=[vones])
                pend = []

                def stage2b(p):
                    pm, kt, hh, first, last, aN, aD, h, q0 = p
                    f.op('pe', lambda e: e.matmul(aN[:], vext[:, kt, hh * 64:(hh + 1) * 64], pm[:], start=first, stop=last),
                         reads=[pm, vext], writes=[aN])
                    f.op('pe', lambda e: e.matmul(aD[:], vones[:, kt, :], pm[:], start=first, stop=last),
                         reads=[pm, vones], writes=[aD])
                    if last:
                        rc = rcb.next(); ost = ostb.next()
                        f.op('dve', lambda e: e.reciprocal(rc[:], aD[:]), reads=[aD], writes=[rc])
                        f.op('dve', lambda e: e.tensor_tensor(ost[:], aN[:], rc[:], ALU.mult), reads=[aN, rc], writes=[ost])
                        f.dma('act', oBT[h * 64:(h + 1) * 64, q0:q0 + 512], ost[:], reads=[ost])

                for qb in range(4):
                    q0 = seg * NL + qb * 512
                    qtb = qtbr.next()
                    f.dma('sp', qtb[:], qTB.h.ap().rearrange("(j p) t -> p j t", p=128)[:, :, q0:q0 + 512], writes=[qtb])
                    for h in range(4):
                        tl = [(0, kt, kt - (7 + 4 * qb)) for kt in range(7 + 4 * qb, 13 + 4 * qb)]
                        tl += [(1, kt, 6 + kt - (6 + 4 * qb)) for kt in range(6 + 4 * qb, 14 + 4 * qb)]
                        tl += [(2, kt, 14 + kt - 4 * qb) for kt in range(4 * qb, 4 * qb + 20)]
                        aN = accBNr.next(); aD = accBDr.next()
                        for idx, (g, kt, mi) in enumerate(tl):
                            hh = g * 4 + h; hp = hh // 2; b0 = (hh % 2) * 64
                            ps = psB.next(); pT = pTb.next(); pm = pmb.next()
                            f.op('pe', lambda e: e.matmul(ps[:], kTe[b0:b0 + 64, hp, kt * 128:(kt + 1) * 128], qtb[b0:b0 + 64, hp, :], start=True, stop=True),
                                 reads=[kTe, qtb], writes=[ps])
                            f.op('act', lambda e: e.activation(out=pT[:], in_=ps[:], func=AF.Exp), reads=[ps], writes=[pT])
                            f.op('dve', lambda e: e.tensor_tensor(pm[:], pT[:], bmask[:, mi, :], ALU.mult), reads=[pT, bmask], writes=[pm])
                            pend.append((pm, kt, hh, idx == 0, idx == len(tl) - 1, aN, aD, h, q0))
                            if len(pend) > 2:
                                stage2b(pend.pop(0))
                while pend:
                    stage2b(pend.pop(0))
            f.barrier()
        if layer == 0 and BUILD_UPTO == 5:
            break

        with ExitStack() as es:
            def load_w(name, src2d, kch, ncol, es_=es):
                wt = f.sbuf(es_, name, [128, kch, ncol], BF16)
                with ExitStack() as est:
                    tmpw = Ring([f.sbuf(est, "tmpw", [128, 512], F32) for _ in range(3)])
                    sv = src2d.rearrange("(kc p) n -> p kc n", p=128)
                    i = 0
                    for kc in range(kch):
                        for c0 in range(0, ncol, 512):
                            w = min(512, ncol - c0)
                            tw = tmpw.next()
                            f.dma('sp', tw[:, 0:w], sv[:, kc, c0:c0 + w], writes=[tw])
                            E = ('dve', 'pool')[i % 2]; i += 1
                            f.op(E, lambda e: e.tensor_copy(wt[:, kc, c0:c0 + w], tw[:, 0:w]), reads=[tw], writes=[wt])
                    f.barrier()
                return wt
            wA = load_w("wA", I["w_br_a"].h.ap()[layer], 8, D)
            wBb = load_w("wBb", I["w_br_b"].h.ap()[layer], 2, D)
            wC = load_w("wC", I["w_br_c"].h.ap()[layer], 8, D)
            wMo = load_w("wMo", I["w_mix_out"].h.ap()[layer], 8, D)
            wQ = load_w("wQ", I["x_wq"].h.ap()[layer], 8, D)
            wO = load_w("wO", I["x_wo"].h.ap()[layer], 8, D)
            gcn = load_bcast_row(es, "gcn", I["c_gnorm"][layer:layer + 1, :], D)
            gcr = load_bcast_row(es, "gcr", I["g_cross"][layer:layer + 1, :], D)
            gff = load_bcast_row(es, "gff", I["g_ffn"][layer:layer + 1, :], D)
            gxq = f.sbuf(es, "gxq", [128, 4, 256], F32)
            for h in range(4):
                f.dma('sp', gxq[:, h, :], I["x_gq"][layer:layer + 1, :].partition_broadcast(128), writes=[gxq])
            f.op('dve', lambda e: e.tensor_scalar(gxq[:], gxq[:], 256.0 ** -0.5, None, ALU.mult), reads=[gxq], writes=[gxq])
            kTm = f.sbuf(es, "kTm", [128, 2, 8, 256], BF16)
            vmem = f.sbuf(es, "vmem", [128, 2, 2, 4, 257], BF16)
            f.op('dve', lambda e: e.memset(vmem[:, :, :, :, 256:257], 1.0), writes=[vmem])
            pT4 = Ring([f.psum(es, "pT4", [128, 4, 128], F32) for _ in range(2)])
            pM = Ring([f.psum(es, "pM", [128, 512], F32) for _ in range(2)])
            with ExitStack() as es2:
                gmm = load_bcast_row(es2, "gmm", I["g_mem"][layer:layer + 1, :], D)
                gxk = f.sbuf(es2, "gxk", [128, 4, 256], F32)
                for h in range(4):
                    f.dma('sp', gxk[:, h, :], I["x_gk"][layer:layer + 1, :].partition_broadcast(128), writes=[gxk])
                wKV = load_w("wKV", I["x_wkv"].h.ap()[layer], 8, 2 * D, es_=es2)
                mt = f.sbuf(es2, "mt", [128, D], F32); sqm = f.sbuf(es2, "sqm", [128, D], F32)
                rsm = f.sbuf(es2, "rsm", [128, 4], F32); mnb = f.sbuf(es2, "mnb", [128, D], BF16)
                mT = f.sbuf(es2, "mT", [128, 8, 128], BF16)
                kf = f.sbuf(es2, "kf", [128, 4, 256], F32); kb = f.sbuf(es2, "kb", [128, D], BF16)
                for seg in range(2):
                    for mtile in range(2):
                        f.dma('sp', mt[:], mem_in[seg * 256 + mtile * 128: seg * 256 + (mtile + 1) * 128, :], writes=[mt])
                        rms_rows(es2, mt, D, rsm, sqm)
                        f.op('dve', lambda e: e.scalar_tensor_tensor(mnb[:], mt[:], rsm[:, 0:1], gmm[:], ALU.mult, ALU.mult), reads=[mt, rsm, gmm], writes=[mnb])
                        transpose_blocks(mnb, 8, pT4, lambda k0, n: mT[:, k0:k0 + n, :], [mT])
                        for cb in range(4):
                            ps = pM.next()
                            for kc in range(8):
                                f.op('pe', lambda e: e.matmul(ps[:], mT[:, kc, :], wKV[:, kc, cb * 512:(cb + 1) * 512], start=(kc == 0), stop=(kc == 7)),
                                     reads=[mT, wKV], writes=[ps])
                            if cb < 2:
                                f.op('act', lambda e: e.copy(kf[:, cb * 2:(cb + 1) * 2, :], ps[:].rearrange("p (h d) -> p h d", d=256)), reads=[ps], writes=[kf])
                            else:
                                f.op('act', lambda e: e.copy(vmem[:, seg, mtile, (cb - 2) * 2:(cb - 1) * 2, 0:256], ps[:].rearrange("p (h d) -> p h d", d=256)),
                                     reads=[ps], writes=[vmem])
                        f.op('dve', lambda e: e.tensor_tensor(sqm[:], kf[:].rearrange("p h d -> p (h d)"), kf[:].rearrange("p h d -> p (h d)"), ALU.mult), reads=[kf], writes=[sqm])
                        f.op('dve', lambda e: e.reduce_sum(rsm[:], sqm[:].rearrange("p (h d) -> p h d", d=256), AX.X), reads=[sqm], writes=[rsm])
                        f.op('dve', lambda e: e.tensor_scalar(rsm[:], rsm[:], 1.0 / 256, EPS, ALU.mult, ALU.add), reads=[rsm], writes=[rsm])
                        f.op('act', lambda e: e.activation(out=rsm[:], in_=rsm[:], func=AF.Sqrt), reads=[rsm], writes=[rsm])
                        f.op('dve', lambda e: e.reciprocal(rsm[:], rsm[:]), reads=[rsm], writes=[rsm])
                        f.op('dve', lambda e: e.tensor_tensor(kf[:], kf[:], rsm[:].unsqueeze(2).broadcast_to([128, 4, 256]), ALU.mult), reads=[kf, rsm], writes=[kf])
                        f.op('dve', lambda e: e.tensor_tensor(kb[:].rearrange("p (h d) -> p h d", d=256), kf[:], gxk[:], ALU.mult), reads=[kf, gxk], writes=[kb])
                        transpose_blocks(kb, 8, pT4, lambda k0, n: kTm[:, seg, k0:k0 + n, mtile * 128:(mtile + 1) * 128], [kTm])
                f.barrier()
            with ExitStack() as es2:
                oAt = Ring([f.sbuf(es2, "oAt", [128, D], BF16) for _ in range(1)])
                oBt = Ring([f.sbuf(es2, "oBt", [128, 256], BF16) for _ in range(1)])
                cft = Ring([f.sbuf(es2, "cft", [128, D], F32) for _ in range(1)])
                cbt = Ring([f.sbuf(es2, "cbt", [128, D], F32) for _ in range(1)])
                cgt = Ring([f.sbuf(es2, "cgt", [128, 4096], BF16) for _ in range(1)])
                xt_ = Ring([f.sbuf(es2, "xt7", [128, D], F32) for _ in range(1)])
                sq7 = f.sbuf(es2, "sq7", [128, D], F32)
                rs7 = f.sbuf(es2, "rs7", [128, 4], F32)
                slt = f.sbuf(es2, "slt", [128, D], F32)
                ocb = f.sbuf(es2, "ocb", [128, D], BF16)
                aT = f.sbuf(es2, "aT", [128, 18, 128], BF16)
                gsg = f.sbuf(es2, "gsg", [128, 3, D], BF16)
                mrg = f.sbuf(es2, "mrg", [128, D], F32); mtmp = f.sbuf(es2, "mtmp", [128, 512], F32)
                mrb = f.sbuf(es2, "mrb", [128, D], BF16)
                mTt = f.sbuf(es2, "mTt", [128, 8, 128], BF16)
                x1 = f.sbuf(es2, "x1", [128, D], F32)
                u2 = f.sbuf(es2, "u2", [128, D], BF16); u2T = f.sbuf(es2, "u2T", [128, 8, 128], BF16)
                qf = f.sbuf(es2, "qf", [128, 4, 256], F32); qb_ = f.sbuf(es2, "qb7", [128, D], BF16)
                qTx = f.sbuf(es2, "qTx", [128, 8, 128], BF16)
                psS7 = f.psum(es2, "psS7", [128, 8, 128], F32)
                psO7 = Ring([f.psum(es2, "psO7", [128, 512], F32) for _ in range(2)])
                PT7 = f.sbuf(es2, "PT7", [128, 8, 128], BF16)
                rc7 = f.sbuf(es2, "rc7", [128, 4], F32)
                ox = f.sbuf(es2, "ox", [128, D], BF16); oxT = f.sbuf(es2, "oxT", [128, 8, 128], BF16)
                x2 = Ring([f.sbuf(es2, "x2", [128, D], F32) for _ in range(1)])
                u3 = Ring([f.sbuf(es2, "u3", [128, D], BF16) for _ in range(1)])
                u3s = Ring([f.sbuf(es2, "u3s", [128, 8, 128], BF16) for _ in range(1)])
                for tt in range(NT):
                    seg, lt = tt // 16, tt % 16
                    rows = slice(tt * 128, (tt + 1) * 128)
                    oa = oAt.next(); ob = oBt.next(); cf = cft.next(); cb_ = cbt.next(); cg = cgt.next(); xt = xt_.next()
                    f.dma('sp', aT[:, 0:8, :], oAT.h.ap().rearrange("(k p) t -> p k t", p=128)[:, :, rows], writes=[aT])
                    f.dma('sp', aT[:, 8:10, :], oBT.h.ap().rearrange("(k p) t -> p k t", p=128)[:, :, rows], writes=[aT])
                    f.dma('sp', cf[:], coC[tt * 128:(tt + 1) * 128, :], writes=[cf])
                    f.dma('sp', cb_[:], coC[NTOK + tt * 128: NTOK + (tt + 1) * 128, :], writes=[cb_])
                    f.dma('sp', cg[:], projTM[rows, TM_CG:TM_CG + 4096], writes=[cg])
                    f.dma('sp', xt[:], xsrc[rows, :], writes=[xt])
                    f.op('pool', lambda e: e.tensor_tensor(cf[:], cf[:], cb_[:], ALU.add), reads=[cf, cb_], writes=[cf])
                    rms_rows(es2, cf, D, rs7, sq7)
                    f.op('act', lambda e: e.activation(out=slt[:], in_=cg[:, 0:1024], func=AF.Silu), reads=[cg], writes=[slt])
                    f.op('dve', lambda e: e.scalar_tensor_tensor(cf[:], cf[:], rs7[:, 0:1], gcn[:], ALU.mult, ALU.mult), reads=[cf, rs7, gcn], writes=[cf])
                    f.op('dve', lambda e: e.tensor_tensor(ocb[:], cf[:], slt[:], ALU.mult), reads=[cf, slt], writes=[ocb])
                    f.op('act', lambda e: e.activation(out=gsg[:].rearrange("p a b -> p (a b)"), in_=cg[:, 1024:4096], func=AF.Sigmoid), reads=[cg], writes=[gsg])
                    transpose_blocks(ocb, 8, pT4, lambda k0, n: aT[:, 10 + k0:10 + k0 + n, :], [aT])
                    for cbk in range(2):
                        cs = slice(cbk * 512, (cbk + 1) * 512)
                        for bi, (wt, k0, nk) in enumerate(((wA, 0, 8), (wBb, 8, 2), (wC, 10, 8))):
                            ps = pM.next()
                            for kc in range(nk):
                                f.op('pe', lambda e: e.matmul(ps[:], aT[:, k0 + kc, :], wt[:, kc, cs], start=(kc == 0), stop=(kc == nk - 1)),
                                     reads=[aT, wt], writes=[ps])
                            if bi == 0:
                                f.op('dve', lambda e: e.tensor_tensor(mrg[:, cs], ps[:], gsg[:, 0, cs], ALU.mult), reads=[ps, gsg], writes=[mrg])
                            else:
                                f.op('dve', lambda e: e.tensor_tensor(mtmp[:], ps[:], gsg[:, bi, cs], ALU.mult), reads=[ps, gsg], writes=[mtmp])
                                f.op('pool', lambda e: e.tensor_tensor(mrg[:, cs], mrg[:, cs], mtmp[:], ALU.add), reads=[mrg, mtmp], writes=[mrg])
                    f.op('act', lambda e: e.copy(mrb[:], mrg[:]), reads=[mrg], writes=[mrb])
                    transpose_blocks(mrb, 8, pT4, lambda k0, n: mTt[:, k0:k0 + n, :], [mTt])
                    for cbk in range(2):
                        cs = slice(cbk * 512, (cbk + 1) * 512)
                        ps = pM.next()
                        for kc in range(8):
                            f.op('pe', lambda e: e.matmul(ps[:], mTt[:, kc, :], wMo[:, kc, cs], start=(kc == 0), stop=(kc == 7)), reads=[mTt, wMo], writes=[ps])
                        f.op('dve', lambda e: e.tensor_tensor(x1[:, cs], ps[:], xt[:, cs], ALU.add), reads=[ps, xt], writes=[x1])
                    rms_rows(es2, x1, D, rs7, sq7)
                    f.op('dve', lambda e: e.scalar_tensor_tensor(u2[:], x1[:], rs7[:, 0:1], gcr[:], ALU.mult, ALU.mult), reads=[x1, rs7, gcr], writes=[u2])
                    transpose_blocks(u2, 8, pT4, lambda k0, n: u2T[:, k0:k0 + n, :], [u2T])
                    for cbk in range(2):
                        ps = pM.next()
                        for kc in range(8):
                            f.op('pe', lambda e: e.matmul(ps[:], u2T[:, kc, :], wQ[:, kc, cbk * 512:(cbk + 1) * 512], start=(kc == 0), stop=(kc == 7)), reads=[u2T, wQ], writes=[ps])
                        f.op('act', lambda e: e.copy(qf[:, cbk * 2:(cbk + 1) * 2, :], ps[:].rearrange("p (h d) -> p h d", d=256)), reads=[ps], writes=[qf])
                    f.op('dve', lambda e: e.tensor_tensor(sq7[:], qf[:].rearrange("p h d -> p (h d)"), qf[:].rearrange("p h d -> p (h d)"), ALU.mult), reads=[qf], writes=[sq7])
                    f.op('dve', lambda e: e.reduce_sum(rs7[:], sq7[:].rearrange("p (h d) -> p h d", d=256), AX.X), reads=[sq7], writes=[rs7])
                    f.op('dve', lambda e: e.tensor_scalar(rs7[:], rs7[:], 1.0 / 256, EPS, ALU.mult, ALU.add), reads=[rs7], writes=[rs7])
                    f.op('act', lambda e: e.activation(out=rs7[:], in_=rs7[:], func=AF.Sqrt), reads=[rs7], writes=[rs7])
                    f.op('dve', lambda e: e.reciprocal(rs7[:], rs7[:]), reads=[rs7], writes=[rs7])
                    f.op('dve', lambda e: e.tensor_tensor(qf[:], qf[:], rs7[:].unsqueeze(2).broadcast_to([128, 4, 256]), ALU.mult), reads=[qf, rs7], writes=[qf])
                    f.op('pool', lambda e: e.tensor_tensor(qb_[:].rearrange("p (h d) -> p h d", d=256), qf[:], gxq[:], ALU.mult), reads=[qf, gxq], writes=[qb_])
                    transpose_blocks(qb_, 8, pT4, lambda k0, n: qTx[:, k0:k0 + n, :], [qTx])
                    for h in range(4):
                        for kt in range(2):
                            for dc in range(2):
                                f.op('pe', lambda e: e.matmul(psS7[:, h * 2 + kt, :], kTm[:, seg, h * 2 + dc, kt * 128:(kt + 1) * 128], qTx[:, h * 2 + dc, :],
                                                              start=(dc == 0), stop=(dc == 1)), reads=[kTm, qTx], writes=[psS7])
                    f.op('act', lambda e: e.activation(out=PT7[:], in_=psS7[:], func=AF.Exp), reads=[psS7], writes=[PT7])
                    for h in range(4):
                        ps = psO7.next()
                        for kt in range(2):
                            f.op('pe', lambda e: e.matmul(ps[:, 0:257], PT7[:, h * 2 + kt, :], vmem[:, seg, kt, h, :], start=(kt == 0), stop=(kt == 1)),
                                 reads=[PT7, vmem], writes=[ps])
                        f.op('dve', lambda e: e.reciprocal(rc7[:, h:h + 1], ps[:, 256:257]), reads=[ps], writes=[rc7])
                        f.op('dve', lambda e: e.tensor_scalar(ox[:, h * 256:(h + 1) * 256], ps[:, 0:256], rc7[:, h:h + 1], None, ALU.mult), reads=[ps, rc7], writes=[ox])
                    transpose_blocks(ox, 8, pT4, lambda k0, n: oxT[:, k0:k0 + n, :], [oxT])
                    xo = x2.next()
                    for cbk in range(2):
                        cs = slice(cbk * 512, (cbk + 1) * 512)
                        ps = pM.next()
                        for kc in range(8):
                            f.op('pe', lambda e: e.matmul(ps[:], oxT[:, kc, :], wO[:, kc, cs], start=(kc == 0), stop=(kc == 7)), reads=[oxT, wO], writes=[ps])
                        f.op('dve', lambda e: e.tensor_tensor(xo[:, cs], ps[:], x1[:, cs], ALU.add), reads=[ps, x1], writes=[xo])
                    f.dma('act', xmid[rows, :], xo[:], reads=[xo])
                    rms_rows(es2, xo, D, rs7, sq7)
                    u3_ = u3.next(); u3s_ = u3s.next()
                    f.op('dve', lambda e: e.scalar_tensor_tensor(u3_[:], xo[:], rs7[:, 0:1], gff[:], ALU.mult, ALU.mult), reads=[xo, rs7, gff], writes=[u3_])
                    transpose_blocks(u3_, 8, pT4, lambda k0, n: u3s_[:, k0:k0 + n, :], [u3s_])
                    f.dma('act', u3T.h.ap().rearrange("(kc p) t -> p kc t", p=128)[:, :, rows], u3s_[:], reads=[u3s_])
                    if lt == 0:
                        f.dma('act', fgin[seg * 2:seg * 2 + 1, :], u3_[0:1, :], reads=[u3_])
                    if lt == 15:
                        f.dma('act', fgin[seg * 2 + 1:seg * 2 + 2, :], u3_[127:128, :], reads=[u3_])
                f.barrier()
            f.barrier()
        f.collective("AllGather", ALL8, fgin, fgout)
        f.barrier()
        with ExitStack() as es:
            EXT = NL + 2
            uTe = f.sbuf(es, "uTe", [128, 8, 2, EXT], BF16)
            for seg in range(2):
                f.dma('sp', uTe[:, :, seg, 1:NL + 1], u3T.h.ap().rearrange("(kc p) t -> p kc t", p=128)[:, :, seg * NL:(seg + 1) * NL], writes=[uTe])
            gall = f.sbuf(es, "gall", [32, D], BF16)
            fsl = f.sbuf(es, "fsl", [32, 4], BF16)
            f.dma('sp', gall[:], fgout[:, :], writes=[gall])
            f.dma('sp', fsl[:], fsel_in[:, :], writes=[fsl])
            psh = f.psum(es, "psh", [128, 8, 4], F32)
            for kc in range(8):
                f.op('pe', lambda e: e.matmul(psh[:, kc, :], gall[:, kc * 128:(kc + 1) * 128], fsl[:], start=True, stop=True), reads=[gall, fsl], writes=[psh])
            for seg in range(2):
                f.op('dve', lambda e: e.tensor_copy(uTe[:, :, seg, 0:1], psh[:, :, seg * 2:seg * 2 + 1]), reads=[psh], writes=[uTe])
                f.op('dve', lambda e: e.tensor_copy(uTe[:, :, seg, NL + 1:NL + 2], psh[:, :, seg * 2 + 1:seg * 2 + 2]), reads=[psh], writes=[uTe])
            cw = f.sbuf(es, "cw", [128, 44, 3], F32)
            f.dma('sp', cw[:].rearrange("p j k -> p (j k)"), I["f_conv"].h.ap()[layer], writes=[cw])
            cbias = f.sbuf(es, "cbias", [128, 44], F32)
            f.dma('sp', cbias[:], I["f_conv_b"].h.ap()[layer], writes=[cbias])
            wu32 = Ring([f.sbuf(es, "wu32", [128, 8, 128], F32) for _ in range(3)])
            wub = Ring([f.sbuf(es, "wub", [128, 8, 128], BF16) for _ in range(4)])
            psU = Ring([f.psum(es, "psU", [128, 2, 512], F32) for _ in range(3)])
            hsr = Ring([f.sbuf(es, "hs", [128, 514], F32) for _ in range(4)])
            cvr = Ring([f.sbuf(es, "cv", [128, 512], F32) for _ in range(4)])
            sgt = Ring([f.sbuf(es, "sgt", [128, 512], F32) for _ in range(2)])
            actt = Ring([f.sbuf(es, "actt", [128, 512], BF16) for _ in range(3)])
            wupv = I["f_wup"].h.ap()[layer].rearrange("(kc p) n -> p kc n", p=128)
            for j in range(22):
                wbs = []
                for ag in range(2):
                    c0 = ag * DFF + j * 128
                    w32_ = wu32.next(); wb_ = wub.next()
                    f.dma('sp', w32_[:], wupv[:, :, c0:c0 + 128], writes=[w32_])
                    f.op('pool', lambda e: e.tensor_copy(wb_[:], w32_[:]), reads=[w32_], writes=[wb_])
                    wbs.append(wb_)
                for tg in range(8):
                    seg, lg = tg // 4, tg % 4
                    cvs = []
                    for ag in range(2):
                        ps = psU.next(); hs = hsr.next(); cv = cvr.next()
                        jj = ag * 22 + j
                        e0 = lg * 512
                        for hf in range(2):
                            for kc in range(8):
                                f.op('pe', lambda e: e.matmul(ps[:, hf, 0:257], wbs[ag][:, kc, :], uTe[:, kc, seg, e0 + hf * 257:e0 + hf * 257 + 257], start=(kc == 0), stop=(kc == 7)),
                                     reads=[wbs[ag], uTe], writes=[ps])
                        f.op('act', lambda e: e.copy(hs[:].rearrange("p (a b) -> p a b", a=2), ps[:, :, 0:257]), reads=[ps], writes=[hs])
                        E = 'dve'
                        f.op(E, lambda e: e.tensor_scalar(cv[:], hs[:, 0:512], cw[:, jj, 0:1], cbias[:, jj:jj + 1], ALU.mult, ALU.add),
                             reads=[hs, cw, cbias], writes=[cv])
                        f.op(E, lambda e: e.scalar_tensor_tensor(cv[:], hs[:, 1:513], cw[:, jj, 1:2], cv[:], ALU.mult, ALU.add),
                             reads=[hs, cw, cv], writes=[cv])
                        f.op(E, lambda e: e.scalar_tensor_tensor(cv[:], hs[:, 2:514], cw[:, jj, 2:3], cv[:], ALU.mult, ALU.add),
                             reads=[hs, cw, cv], writes=[cv])
                        cvs.append(cv)
                    sg_ = sgt.next(); ac_ = actt.next()
                    f.op('act', lambda e: e.activation(out=sg_[:], in_=cvs[1][:], func=AF.Silu), reads=[cvs[1]], writes=[sg_])
                    f.op('dve', lambda e: e.tensor_tensor(ac_[:], cvs[0][:], sg_[:], ALU.mult), reads=[cvs[0], sg_], writes=[ac_])
                    f.dma('act', hact[j * 128:(j + 1) * 128, tg * 512:(tg + 1) * 512], ac_[:], reads=[ac_])
            f.barrier()
        with ExitStack() as es:
            wD = f.sbuf(es, "wD", [128, 22, D], BF16)
            with ExitStack() as est:
                tmpw = Ring([f.sbuf(est, "tmpwd", [128, 512], F32) for _ in range(3)])
                sv = I["f_wdown"].h.ap()[layer].rearrange("(kc p) n -> p kc n", p=128)
                i = 0
                for kc in range(22):
                    for c0 in (0, 512):
                        tw = tmpw.next()
                        f.dma('sp', tw[:], sv[:, kc, c0:c0 + 512], writes=[tw])
                        E = ('dve', 'pool')[i % 2]; i += 1
                        f.op(E, lambda e: e.tensor_copy(wD[:, kc, c0:c0 + 512], tw[:]), reads=[tw], writes=[wD])
                f.barrier()
            hat = Ring([f.sbuf(es, "hat", [128, 22, 128], BF16) for _ in range(2)])
            xm = Ring([f.sbuf(es, "xm", [128, D], F32) for _ in range(2)])
            xo3 = Ring([f.sbuf(es, "xo3", [128, D], F32) for _ in range(2)])
            pD = Ring([f.psum(es, "pD", [128, 512], F32) for _ in range(4)])
            dst = xres if layer == 0 else y_out
            for tt in range(NT):
                rows = slice(tt * 128, (tt + 1) * 128)
                ha = hat.next(); xm_ = xm.next(); xo_ = xo3.next()
                f.dma('sp', ha[:], hact.h.ap().rearrange("(j p) t -> p j t", p=128)[:, :, rows], writes=[ha])
                f.dma('sp', xm_[:], xmid[rows, :], writes=[xm_])
                for cbk in range(2):
                    cs = slice(cbk * 512, (cbk + 1) * 512)
                    ps = pD.next()
                    for j in range(22):
                        f.op('pe', lambda e: e.matmul(ps[:], ha[:, j, :], wD[:, j, cs], start=(j == 0), stop=(j == 21)), reads=[ha, wD], writes=[ps])
                    f.op('dve', lambda e: e.tensor_tensor(xo_[:, cs], ps[:], xm_[:, cs], ALU.add), reads=[ps, xm_], writes=[xo_])
                f.dma('act', dst[rows, :], xo_[:], reads=[xo_])
            f.barrier()
    f.barrier()
    print("ninst", f.ninst, "nwaits", f.nwaits)
    top.close()
    return nc


BUILD_UPTO = 99
DEBUG_OUT = set()


def _consts(c):
    bf = ml_dtypes.bfloat16
    half = c % 2
    out = {}
    inv = np.power(np.float32(10000.0), -(np.arange(0, 64, 2, dtype=np.float32) / np.float32(64))).astype(np.float32)
    tabs = np.zeros((NTOK, 384), np.float32)
    for seg in range(2):
        t = (half * NL if seg == 0 else c * NL) + np.arange(NL)
        row = (t // 64).astype(np.float32); col = (t % 64).astype(np.float32)
        ar = row[:, None] * inv[None, :]; ac = col[:, None] * inv[None, :]; at = t.astype(np.float32)[:, None] * inv[None, :]
        cr, sr, cc, sc, ct, st = np.cos(ar), np.sin(ar), np.cos(ac), np.sin(ac), np.cos(at), np.sin(at)
        sl = slice(seg * NL, (seg + 1) * NL)
        tabs[sl, 0:128] = np.concatenate([cr, cr, cc, cc], 1)
        tabs[sl, 128:256] = np.concatenate([-sr, sr, -sc, sc], 1)
        tabs[sl, 256:320] = np.concatenate([ct, ct], 1)
        tabs[sl, 320:384] = np.concatenate([-st, st], 1)
    out["tabs"] = tabs
    out["ident"] = np.eye(128, dtype=np.float32).astype(bf)
    src = {}
    src[(0, 0)] = c - 1 if half == 1 else None
    src[(0, 1)] = c + 1 if half == 0 else None
    src[(1, 0)] = c - 1 if c > 0 else None
    src[(1, 1)] = c + 1 if c < 7 else None
    selI = np.zeros((128, 32, 128), np.float32)
    for (seg, side), r in src.items():
        if r is not None:
            selI[:, (seg * 2 + side) * 8 + r, :] = np.eye(128, dtype=np.float32)
    out["selI"] = selI.reshape(128, 32 * 128).astype(bf)
    bm = np.zeros((128, 34, 512), np.float32)
    p = np.arange(128)[:, None]; qi = np.arange(512)[None, :]
    ti = 0
    for g, (ntile, off, dil) in enumerate(((6, -128, 1), (8, -256, 4), (20, -1024, 16))):
        for rel in range(ntile):
            dlt = rel * 128 + p - qi + off
            bm[:, ti, :] = ((np.abs(dlt) <= 64 * dil) & (dlt % dil == 0)).astype(np.float32)
            ti += 1
    out["bmask"] = bm.reshape(128, 34 * 512).astype(bf)
    bv = np.zeros((128, 2, 32), np.float32)
    for seg in range(2):
        t0 = half * NL if seg == 0 else c * NL
        e = np.arange(32)[None, :] * 128 + np.arange(128)[:, None]
        pos = t0 + e - 1024
        bv[:, seg, :] = ((pos >= 0) & (pos < SEG_S[seg])).astype(np.float32)
    out["bvalid"] = bv.reshape(128, 64)
    cm = np.zeros((128, 2, 128), np.float32)
    s = np.arange(128)[:, None]; t = np.arange(128)[None, :]
    same = (s // 64) == (t // 64)
    cm[:, 0, :] = (same & (s <= t)).astype(np.float32)
    cm[:, 1, :] = (same & (s >= t)).astype(np.float32)
    out["cmask"] = cm.reshape(128, 256).astype(bf)
    cs = np.zeros((128, 16), np.float32)
    cs[:, c] = 1.0; cs[:, 8 + c] = 1.0
    out["cscan"] = cs
    chm = np.ones((128, NL), np.float32); chm[:, ::64] = 0.0
    out["chunkm"] = chm
    fs = np.zeros((32, 4), np.float32)
    for (seg, side), r in src.items():
        if r is not None:
            fs[r * 4 + seg * 2 + (1 if side == 0 else 0), seg * 2 + side] = 1.0
    out["fsel"] = fs.astype(bf)
    return out


def make_in_maps(inputs):
    maps = []
    wnames = ["g_mix", "w_in", "a_gq", "a_gk", "b_gq", "b_gk", "c_lb_fwd", "c_lb_bwd", "c_gnorm", "w_br_a", "w_br_b",
              "w_br_c", "w_mix_out", "g_cross", "g_mem", "x_wq", "x_wkv", "x_gq", "x_gk", "x_wo", "g_ffn", "f_wup",
              "f_conv", "f_conv_b", "f_wdown"]
    shared = {}
    for n in wnames:
        a = np.ascontiguousarray(inputs[n], dtype=np.float32)
        if n in ("b_gq", "b_gk"):
            a = a.reshape(2, 192)
        if n in ("c_lb_fwd", "c_lb_bwd"):
            a = np.ascontiguousarray(a.reshape(2, 8, 128).transpose(0, 2, 1))
        if n == "f_conv":
            a = np.ascontiguousarray(a.reshape(2, 3, 44, 128).transpose(0, 3, 2, 1).reshape(2, 128, 132))
        if n == "f_conv_b":
            a = np.ascontiguousarray(a.reshape(2, 44, 128).transpose(0, 2, 1))
        shared[n] = a
    xp = inputs["x_prompt"]; xs = inputs["x_sample"]
    for c in range(NCORES):
        b, half = c // 2, c % 2
        m = dict(shared)
        m["x"] = np.ascontiguousarray(np.concatenate([xp[b, half * NL:(half + 1) * NL], xs[0, c * NL:(c + 1) * NL]], 0), dtype=np.float32)
        m["mem"] = np.ascontiguousarray(np.concatenate([inputs["mem_prompt"][b], inputs["mem_sample"][0]], 0), dtype=np.float32)
        m.update(_consts(c))
        maps.append(m)
    return maps


_NC = None


def kernel(**inputs):
    global _NC
    if _NC is None:
        _NC = build()
    maps = make_in_maps(inputs)
    res = run_bass_kernel_spmd(_NC, maps, core_ids=list(range(NCORES)))
    yp = np.zeros((4, 4096, D), np.float32)
    ys = np.zeros((1, 16384, D), np.float32)
    for c in range(NCORES):
        y = res.results[c]["y"]
        b, half = c // 2, c % 2
        yp[b, half * NL:(half + 1) * NL] = y[0:NL]
        ys[0, c * NL:(c + 1) * NL] = y[NL:]
    return (yp, ys)
```

```python
import numpy as np
import ml_dtypes
from contextlib import ExitStack
import concourse.bass as bass
import concourse.mybir as mybir
from concourse.bass_utils import run_bass_kernel_spmd

F32 = mybir.dt.float32
BF16 = mybir.dt.bfloat16
AF = mybir.ActivationFunctionType
ALU = mybir.AluOpType
AX = mybir.AxisListType

NCORES = 8
D = 1024
NL = 2048
NTOK = 2 * NL
NT = NTOK // 128
INW = 12032
DFF = 2816
EPS = 1e-6
SEG_S = (4096, 16384)
PAIRS = [[0, 1], [2, 3], [4, 5], [6, 7]]
ALL8 = [list(range(8))]

TM_AQ, TM_AK, TM_AV, TM_BQ, TM_BK, TM_BV, TM_CI, TM_CG, TM_GT = 0, 1024, 1280, 1536, 2304, 3072, 3840, 4864, 5888
TMW = 8960


class Tile:
    def __init__(self, h, name):
        self.h = h; self.name = name
        self.writers = {}; self.readers = {}; self.dsem = None

    def __getitem__(self, idx):
        return self.h[idx]


class FW:
    def __init__(self, nc, es):
        self.nc = nc
        self.engs = {'pe': nc.tensor, 'act': nc.scalar, 'dve': nc.vector, 'pool': nc.gpsimd, 'sp': nc.sync}
        self.sems = {}; self.cnt = {}
        self.seen = {e: {} for e in self.engs}
        for e in self.engs:
            self.sems[e] = es.enter_context(nc.semaphore("s_" + e)); self.cnt[e] = 0
        self.ndpool = 40
        for i in range(self.ndpool):
            k = "d%d" % i
            self.sems[k] = es.enter_context(nc.semaphore("s_" + k)); self.cnt[k] = 0
        self.nd = 0
        self.ninst = 0; self.nwaits = 0
        self.uid = 0

    def sbuf(self, es, name, shape, dt):
        self.uid += 1
        return Tile(es.enter_context(self.nc.sbuf_tensor("%s_%d" % (name, self.uid), list(shape), dt)), name)

    def psum(self, es, name, shape, dt=F32):
        self.uid += 1
        return Tile(es.enter_context(self.nc.psum_tensor("%s_%d" % (name, self.uid), list(shape), dt)), name)

    def dram(self, name, shape, dt, kind=None):
        if kind is None and name in DEBUG_OUT:
            kind = "ExternalOutput"
        if kind is None:
            h = self.nc.dram_tensor(name, list(shape), dt)
        else:
            h = self.nc.dram_tensor(name, list(shape), dt, kind=kind)
        return Tile(h, name)

    def _dsem(self, t):
        if t.dsem is None:
            t.dsem = "d%d" % (self.nd % self.ndpool); self.nd += 1
        return t.dsem

    def _wait(self, e, key, c):
        if c <= 0 or self.seen[e].get(key, 0) >= c:
            return
        if e == 'pe' and key == 'pe':
            return
        self.engs[e].wait_ge(self.sems[key], c)
        self.seen[e][key] = c
        self.nwaits += 1

    def _deps(self, e, reads, writes):
        for t in reads:
            for k, c in t.writers.items():
                self._wait(e, k, self.cnt[k] if k[0] == 'd' else c)
        for t in writes:
            for k, c in t.writers.items():
                self._wait(e, k, self.cnt[k] if k[0] == 'd' else c)
            for k, c in t.readers.items():
                self._wait(e, k, self.cnt[k] if k[0] == 'd' else c)

    def _record(self, key, c, reads, writes):
        for t in reads:
            if t not in writes:
                t.readers[key] = c
        for t in writes:
            t.writers = {key: c}; t.readers = {}

    def op(self, e, fn, reads=(), writes=(), inc=True):
        self._deps(e, reads, writes)
        ins = fn(self.engs[e])
        self.ninst += 1
        inc = True
        if inc:
            self.cnt[e] += 1
            ins.then_inc(self.sems[e], 1)
            self._record(e, self.cnt[e], reads, writes)
        else:
            self._record(e, self.cnt[e] + 1, reads, writes)
        return ins

    def dma(self, q, out_ap, in_ap, reads=(), writes=(), **kw):
        self._deps(q, reads, writes)
        owner = (list(writes) + list(reads))[0]
        k = self._dsem(owner)
        ins = self.engs[q].dma_start(out=out_ap, in_=in_ap, **kw)
        self.cnt[k] += 16
        ins.then_inc(self.sems[k], 16)
        self.ninst += 1
        self._record(k, self.cnt[k], reads, writes)
        return ins

    def collective(self, kind, groups, tin, tout):
        e = 'pool'
        self._deps(e, [tin], [tout])
        self.cnt[e] += 1
        self.nc.gpsimd.collective_compute(kind, ALU.bypass, replica_groups=groups,
                                          ins=[tin.h.ap().opt()], outs=[tout.h.ap().opt()]).then_inc(self.sems[e], 1)
        self._record(e, self.cnt[e], [tin], [tout])

    def barrier(self):
        for e in self.engs:
            for k, c in self.cnt.items():
                self._wait(e, k, c)


class Ring:
    def __init__(self, tiles):
        self.t = tiles; self.i = 0

    def next(self):
        t = self.t[self.i % len(self.t)]; self.i += 1
        return t


def bc(ap, shape):
    return ap.broadcast_to(list(shape))


def build():
    nc = bass.Bass("TRN2", target_bir_lowering=False)
    top = ExitStack()
    f = FW(nc, top)
    I = {}

    def inp(name, shape, dt=F32):
        I[name] = f.dram(name, shape, dt, kind="ExternalInput")
        return I[name]

    x_in = inp("x", [NTOK, D])
    mem_in = inp("mem", [2 * 256, D])
    tabs = inp("tabs", [NTOK, 384])
    for nm, shp in (("g_mix", [2, D]), ("w_in", [2, D, INW]), ("a_gq", [2, 128]), ("a_gk", [2, 128]),
                    ("b_gq", [2, 192]), ("b_gk", [2, 192]), ("c_lb_fwd", [2, 128, 8]), ("c_lb_bwd", [2, 128, 8]),
                    ("c_gnorm", [2, D]), ("w_br_a", [2, D, D]), ("w_br_b", [2, 256, D]), ("w_br_c", [2, D, D]),
                    ("w_mix_out", [2, D, D]), ("g_cross", [2, D]), ("g_mem", [2, D]), ("x_wq", [2, D, D]),
                    ("x_wkv", [2, D, 2 * D]), ("x_gq", [2, 256]), ("x_gk", [2, 256]), ("x_wo", [2, D, D]),
                    ("g_ffn", [2, D]), ("f_wup", [2, D, 2 * DFF]), ("f_conv", [2, 128, 132]),
                    ("f_conv_b", [2, 128, 44]), ("f_wdown", [2, DFF, D])):
        inp(nm, shp)
    ident_in = inp("ident", [128, 128], BF16)
    selI_in = inp("selI", [128, 32 * 128], BF16)
    bmask_in = inp("bmask", [128, 34 * 512], BF16)
    bvalid_in = inp("bvalid", [128, 2 * 32])
    cmask_in = inp("cmask", [128, 2 * 128], BF16)
    cscan_in = inp("cscan", [128, 16])
    chunkm_in = inp("chunkm", [128, NL])
    fsel_in = inp("fsel", [32, 4], BF16)
    y_out = f.dram("y", [NTOK, D], F32, kind="ExternalOutput")

    xres = f.dram("xres", [NTOK, D], F32)
    projTM = f.dram("projTM", [NTOK, TMW], BF16)
    projFM = f.dram("projFM", [24 * 128, NTOK], F32)
    qTA = f.dram("qTA", [8 * 128, NTOK], BF16)
    akin = [f.dram("akin%d" % s, [256, NL], BF16) for s in range(2)]
    avin = [f.dram("avin%d" % s, [NL, 256], BF16) for s in range(2)]
    akout = [f.dram("akout0", [2 * 256, NL], BF16), f.dram("akout1", [8 * 256, NL], BF16)]
    avout = [f.dram("avout0", [2 * NL, 256], BF16), f.dram("avout1", [8 * NL, 256], BF16)]
    oAT = f.dram("oAT", [D, NTOK], BF16)
    qTB = f.dram("qTB", [768, NTOK], BF16)
    bkin = f.dram("bkin", [2 * 768, NL], BF16)
    bvin = f.dram("bvin", [2 * NL, 768], BF16)
    bkout = f.dram("bkout", [8 * 2 * 768, NL], BF16)
    bvout = f.dram("bvout", [8 * 2 * NL, 768], BF16)
    extK = f.dram("extK", [2 * 768, 4096], BF16)
    extV = f.dram("extV", [2 * 4096, 768], BF16)
    oBT = f.dram("oBT", [256, NTOK], BF16)
    cqT = f.dram("cqT", [2 * 1024, NTOK], BF16)
    ckT = f.dram("ckT", [2 * 1024, NTOK], BF16)
    ckTM = f.dram("ckTM", [2 * NTOK, 1024], BF16)
    cer = f.dram("cer", [256, 512], F32)
    cel = f.dram("cel", [256, 512], F32)
    cKI = f.dram("cKI", [2 * 2 * 32 * 128, 1024], F32)
    csin = f.dram("csin", [128, 4128], F32)
    csout = f.dram("csout", [1024, 4128], F32)
    coC = f.dram("coC", [2 * NTOK, 1024], F32)
    xmid = f.dram("xmid", [NTOK, D], F32)
    u3T = f.dram("u3T", [D, NTOK], BF16)
    fgin = f.dram("fgin", [4, D], BF16)
    fgout = f.dram("fgout", [32, D], BF16)
    hact = f.dram("hact", [DFF, NTOK], BF16)

    ident = f.sbuf(top, "ident", [128, 128], BF16)
    f.dma('sp', ident[:], ident_in[:, :], writes=[ident])

    zeroW = f.sbuf(top, "zeroW", [128, 128], BF16)
    f.op('dve', lambda e: e.memset(zeroW[:], 0.0), writes=[zeroW])

    def rms_rows(es, xt, width, rs, sq):
        f.op('dve', lambda e: e.tensor_tensor(sq[:, 0:width], xt[:, 0:width], xt[:, 0:width], ALU.mult),
             reads=[xt], writes=[sq])
        f.op('dve', lambda e: e.reduce_sum(rs[:, 0:1], sq[:, 0:width], AX.X), reads=[sq], writes=[rs])
        f.op('dve', lambda e: e.tensor_scalar(rs[:, 0:1], rs[:, 0:1], 1.0 / width, EPS, ALU.mult, ALU.add),
             reads=[rs], writes=[rs])
        f.op('act', lambda e: e.activation(out=rs[:, 0:1], in_=rs[:, 0:1], func=AF.Sqrt), reads=[rs], writes=[rs])
        f.op('dve', lambda e: e.reciprocal(rs[:, 0:1], rs[:, 0:1]), reads=[rs], writes=[rs])

    def transpose_blocks(src, nblk, psr, dst_fn, dst_tiles, src_off=0):
        k0 = 0
        gi = 0
        while k0 < nblk:
            n = min(4, nblk - k0)
            ps = psr.next()
            for k in range(n):
                f.op('pe', lambda e, k=k: e.matmul(ps[:, k, :], src[:, src_off + (k0 + k) * 128: src_off + (k0 + k + 1) * 128], ident[:],
                                                   start=True, stop=True),
                     reads=[src, ident], writes=[ps])
            if gi % 2 == 0:
                f.op('act', lambda e: e.copy(dst_fn(k0, n), ps[:, 0:n, :]), reads=[ps], writes=dst_tiles)
            else:
                f.op('dve', lambda e: e.tensor_copy(dst_fn(k0, n), ps[:, 0:n, :]), reads=[ps], writes=dst_tiles)
            k0 += n; gi += 1

    def load_bcast_row(es, name, src_ap_row, width, q='sp'):
        t = f.sbuf(es, name, [128, width], F32)
        f.dma(q, t[:], src_ap_row.partition_broadcast(128), writes=[t])
        return t

    for layer in range(2):
        xsrc = x_in if layer == 0 else xres

        with ExitStack() as es:
            uT = f.sbuf(es, "uT", [128, 8, NTOK], BF16)
            with ExitStack() as es2:
                gmix = load_bcast_row(es2, "gmix", I["g_mix"][layer:layer + 1, :], D)
                xin = Ring([f.sbuf(es2, "xin", [128, D], F32) for _ in range(2)])
                sq = f.sbuf(es2, "sq", [128, D], F32)
                rsr = Ring([f.sbuf(es2, "rs", [128, 1], F32) for _ in range(2)])
                unr = Ring([f.sbuf(es2, "un", [128, D], BF16) for _ in range(2)])
                pst = Ring([f.psum(es2, "pst", [128, 4, 128], F32) for _ in range(4)])
                for tt in range(NT):
                    xt = xin.next(); rs = rsr.next(); un = unr.next()
                    f.dma('sp', xt[:], xsrc[tt * 128:(tt + 1) * 128, :], writes=[xt])
                    rms_rows(es2, xt, D, rs, sq)
                    f.op('dve', lambda e: e.scalar_tensor_tensor(un[:], xt[:], rs[:, 0:1], gmix[:], ALU.mult, ALU.mult),
                         reads=[xt, rs, gmix], writes=[un])
                    transpose_blocks(un, 8, pst, lambda k0, n, tt=tt: uT[:, k0:k0 + n, tt * 128:(tt + 1) * 128], [uT])
                f.barrier()
            with ExitStack() as es2:
                w32 = Ring([f.sbuf(es2, "w32", [128, 8, 512], F32) for _ in range(2)])
                wbf = Ring([f.sbuf(es2, "wbf", [128, 8, 512], BF16) for _ in range(2)])
                psm = Ring([f.psum(es2, "psm", [128, 512], F32) for _ in range(4)])
                stg = Ring([f.sbuf(es2, "stg", [128, 512], BF16) for _ in range(4)])
                stgf = Ring([f.sbuf(es2, "stgf", [128, 512], F32) for _ in range(4)])
                wv = I["w_in"].h.ap()[layer].rearrange("(kc p) n -> p kc n", p=128)
                tm_blocks = []
                for (c0, n, t0) in ((0, 1024, TM_AQ), (1024, 512, TM_AK), (1536, 2304, TM_BQ), (4864, 1024, TM_CI),
                                    (7936, 4096, TM_CG)):
                    o = 0
                    while o < n:
                        w = min(512, n - o)
                        tm_blocks.append(('tm', c0 + o, w, t0 + o)); o += w
                fm_blocks = []
                for (c0, ch0) in ((3840, 0), (4352, 4), (5888, 8), (6400, 12), (6912, 16), (7424, 20)):
                    fm_blocks.append(('fm', c0, 512, ch0))
                evi = 0
                for (kind, c0, w, dst) in tm_blocks + fm_blocks:
                    wa = w32.next(); wb = wbf.next()
                    f.dma('sp', wa[:, :, 0:w], wv[:, :, c0:c0 + w], writes=[wa])
                    f.op('pool', lambda e: e.tensor_copy(wb[:, :, 0:w], wa[:, :, 0:w]), reads=[wa], writes=[wb])
                    if kind == 'tm':
                        for tt in range(NT):
                            ps = psm.next(); st = stg.next()
                            for kc in range(8):
                                f.op('pe', lambda e, kc=kc: e.matmul(ps[:, 0:w], uT[:, kc, tt * 128:(tt + 1) * 128], wb[:, kc, 0:w],
                                                                     start=(kc == 0), stop=(kc == 7)),
                                     reads=[uT, wb], writes=[ps], inc=(kc == 7))
                            if evi % 2 == 0:
                                f.op('act', lambda e: e.copy(st[:, 0:w], ps[:, 0:w]), reads=[ps], writes=[st])
                            else:
                                f.op('dve', lambda e: e.tensor_copy(st[:, 0:w], ps[:, 0:w]), reads=[ps], writes=[st])
                            evi += 1
                            f.dma('pool', projTM[tt * 128:(tt + 1) * 128, dst:dst + w], st[:, 0:w], reads=[st])
                    else:
                        for cc in range(4):
                            for tg in range(NTOK // 512):
                                ps = psm.next(); st = stgf.next()
                                for kc in range(8):
                                    f.op('pe', lambda e, kc=kc: e.matmul(ps[:], wb[:, kc, cc * 128:(cc + 1) * 128], uT[:, kc, tg * 512:(tg + 1) * 512],
                                                                         start=(kc == 0), stop=(kc == 7)),
                                         reads=[uT, wb], writes=[ps], inc=(kc == 7))
                                if evi % 2 == 0:
                                    f.op('act', lambda e: e.copy(st[:], ps[:]), reads=[ps], writes=[st])
                                else:
                                    f.op('dve', lambda e: e.tensor_copy(st[:], ps[:]), reads=[ps], writes=[st])
                                evi += 1
                                f.dma('pool', projFM[(dst + cc) * 128:(dst + cc + 1) * 128, tg * 512:(tg + 1) * 512], st[:], reads=[st])
                f.barrier()
            f.barrier()
        if layer == 0 and BUILD_UPTO == 1:
            break

        with ExitStack() as es:
            gA = f.sbuf(es, "gA", [128, 10, 128], F32)
            gB = f.sbuf(es, "gB", [128, 24, 64], F32)
            for h in range(10):
                srcg = I["a_gq"] if h < 8 else I["a_gk"]
                f.dma('sp', gA[:, h, :], srcg[layer:layer + 1, :].partition_broadcast(128), writes=[gA])
            for h in range(24):
                srcg = I["b_gq"] if h < 12 else I["b_gk"]
                g = (h % 12) // 4
                f.dma('sp', gB[:, h, :], srcg[layer:layer + 1, g * 64:(g + 1) * 64].partition_broadcast(128), writes=[gB])
            f.op('dve', lambda e: e.tensor_scalar(gA[:, 0:8, :], gA[:, 0:8, :], 128.0 ** -0.5, None, ALU.mult), reads=[gA], writes=[gA])
            f.op('dve', lambda e: e.tensor_scalar(gB[:, 0:12, :], gB[:, 0:12, :], 64.0 ** -0.5, None, ALU.mult), reads=[gB], writes=[gB])
            par = Ring([f.sbuf(es, "pa", [128, 3840], BF16) for _ in range(2)])
            tbr = Ring([f.sbuf(es, "tb", [128, 384], F32) for _ in range(2)])
            sqA = f.sbuf(es, "sqA", [128, 10, 128], F32)
            xnA = f.sbuf(es, "xnA", [128, 10, 128], F32)
            t2A = f.sbuf(es, "t2A", [128, 10, 128], F32)
            rsA = f.sbuf(es, "rsA", [128, 10], F32)
            roA = Ring([f.sbuf(es, "roA", [128, 1280], BF16) for _ in range(2)])
            sqB = f.sbuf(es, "sqB", [128, 24, 64], F32)
            xnB = f.sbuf(es, "xnB", [128, 24, 64], F32)
            t2B = f.sbuf(es, "t2B", [128, 24, 64], F32)
            rsB = f.sbuf(es, "rsB", [128, 24], F32)
            roB = Ring([f.sbuf(es, "roB", [128, 1536], BF16) for _ in range(2)])
            stA = Ring([f.sbuf(es, "stA", [128, 10, 128], BF16) for _ in range(2)])
            stB = Ring([f.sbuf(es, "stB", [128, 12, 128], BF16) for _ in range(2)])
            pst = Ring([f.psum(es, "pst2", [128, 4, 128], F32) for _ in range(4)])

            def normrope(E, src3, nh, hd, sq, xn, t2, rs, gt, ctab, stab, ro):
                f.op(E, lambda e: e.tensor_tensor(sq[:], src3, src3, ALU.mult), reads=[pa], writes=[sq])
                f.op('dve', lambda e: e.reduce_sum(rs[:], sq[:], AX.X), reads=[sq], writes=[rs])
                f.op('dve', lambda e: e.tensor_scalar(rs[:], rs[:], 1.0 / hd, EPS, ALU.mult, ALU.add), reads=[rs], writes=[rs])
                f.op('act', lambda e: e.activation(out=rs[:], in_=rs[:], func=AF.Sqrt), reads=[rs], writes=[rs])
                f.op('dve', lambda e: e.reciprocal(rs[:], rs[:]), reads=[rs], writes=[rs])
                f.op(E, lambda e: e.tensor_tensor(xn[:], src3, rs[:].unsqueeze(2).broadcast_to([128, nh, hd]), ALU.mult),
                     reads=[pa, rs], writes=[xn])
                f.op(E, lambda e: e.tensor_tensor(xn[:], xn[:], gt[:], ALU.mult), reads=[xn, gt], writes=[xn])
                hh = hd // 2 if hd == 64 else 32
                nb = hd // (2 * hh)
                xv = xn[:].rearrange("p h (b s j) -> p (h b) s j", b=nb, s=2)
                tv = t2[:].rearrange("p h (b s j) -> p (h b) s j", b=nb, s=2)
                if nb == 1:
                    sv = stab.rearrange("p (s j) -> p s j", s=2)
                    s0 = sv[:, 0:1, :].broadcast_to([128, nh, hh]); s1 = sv[:, 1:2, :].broadcast_to([128, nh, hh])
                    f.op(E, lambda e: e.tensor_tensor(tv[:, :, 0, :], xv[:, :, 1, :], s0, ALU.mult), reads=[xn, tb], writes=[t2])
                    f.op(E, lambda e: e.tensor_tensor(tv[:, :, 1, :], xv[:, :, 0, :], s1, ALU.mult), reads=[xn, tb], writes=[t2])
                else:
                    xv4 = xn[:].rearrange("p h (b s j) -> p h b s j", b=nb, s=2)
                    tv4 = t2[:].rearrange("p h (b s j) -> p h b s j", b=nb, s=2)
                    sv = stab.rearrange("p (b s j) -> p b s j", b=nb, s=2)
                    for bb in range(nb):
                        s0 = sv[:, bb, 0:1, :].broadcast_to([128, nh, hh]); s1 = sv[:, bb, 1:2, :].broadcast_to([128, nh, hh])
                        f.op(E, lambda e: e.tensor_tensor(tv4[:, :, bb, 0, :], xv4[:, :, bb, 1, :], s0, ALU.mult), reads=[xn, tb], writes=[t2])
                        f.op(E, lambda e: e.tensor_tensor(tv4[:, :, bb, 1, :], xv4[:, :, bb, 0, :], s1, ALU.mult), reads=[xn, tb], writes=[t2])
                f.op(E, lambda e: e.tensor_tensor(xn[:], xn[:], ctab.unsqueeze(1).broadcast_to([128, nh, hd]), ALU.mult),
                     reads=[xn, tb], writes=[xn])
                f.op(E, lambda e: e.tensor_tensor(ro[:].rearrange("p (h d) -> p h d", d=hd), xn[:], t2[:], ALU.add),
                     reads=[xn, t2], writes=[ro])

            for tt in range(NT):
                seg, lt = tt // 16, tt % 16
                pa = par.next(); tb = tbr.next(); ra = roA.next(); rb = roB.next(); sa = stA.next(); sb = stB.next()
                f.dma('sp', pa[:], projTM[tt * 128:(tt + 1) * 128, 0:3840], writes=[pa])
                f.dma('sp', tb[:], tabs[tt * 128:(tt + 1) * 128, :], writes=[tb])
                normrope('dve', pa[:, 0:1280].rearrange("p (h d) -> p h d", d=128), 10, 128, sqA, xnA, t2A, rsA, gA,
                         tb[:, 0:128], tb[:, 128:256], ra)
                normrope('pool', pa[:, 1536:3072].rearrange("p (h d) -> p h d", d=64), 24, 64, sqB, xnB, t2B, rsB, gB,
                         tb[:, 256:320], tb[:, 320:384], rb)
                transpose_blocks(ra, 10, pst, lambda k0, n: sa[:, k0:k0 + n, :], [sa])
                transpose_blocks(rb, 12, pst, lambda k0, n: sb[:, k0:k0 + n, :], [sb])
                tc = slice(tt * 128, (tt + 1) * 128); lc = slice(lt * 128, (lt + 1) * 128)
                f.dma('act', qTA.h.ap().rearrange("(h d) t -> d h t", d=128)[:, :, tc], sa[:, 0:8, :], reads=[sa])
                f.dma('act', akin[seg].h.ap().rearrange("(h d) t -> d h t", d=128)[:, :, lc], sa[:, 8:10, :], reads=[sa])
                f.dma('act', avin[seg][lc, :], pa[:, 1280:1536], reads=[pa])
                f.dma('act', qTB.h.ap().rearrange("(j p) t -> p j t", p=128)[:, :, tc], sb[:, 0:6, :], reads=[sb])
                f.dma('act', bkin.h.ap()[seg * 768:(seg + 1) * 768].rearrange("(j p) t -> p j t", p=128)[:, :, lc], sb[:, 6:12, :], reads=[sb])
                f.dma('act', bvin[seg * NL + lt * 128: seg * NL + (lt + 1) * 128, :], pa[:, 3072:3840], reads=[pa])
            f.barrier()
        f.collective("AllGather", PAIRS, akin[0], akout[0])
        f.collective("AllGather", PAIRS, avin[0], avout[0])
        f.collective("AllGather", ALL8, akin[1], akout[1])
        f.collective("AllGather", ALL8, avin[1], avout[1])
        f.collective("AllGather", ALL8, bkin, bkout)
        f.collective("AllGather", ALL8, bvin, bvout)
        if layer == 0 and BUILD_UPTO == 3:
            break

        with ExitStack() as es:
            lbt = f.sbuf(es, "lbt", [128, 2, 8], F32)
            oml = f.sbuf(es, "oml", [128, 2, 8], F32)
            if layer == 0:
                f.op('dve', lambda e: e.memset(lbt[:], 0.0), writes=[lbt])
                f.op('dve', lambda e: e.memset(oml[:], 1.0), writes=[oml])
            else:
                raw = f.sbuf(es, "raw", [128, 2, 2, 8], F32)
                for dr, nm in enumerate(("c_lb_fwd", "c_lb_bwd")):
                    for l2 in range(2):
                        f.dma('sp', raw[:, dr, l2, :], I[nm].h.ap()[l2], writes=[raw])
                f.op('dve', lambda e: e.tensor_tensor(lbt[:], raw[:, :, 1, :], raw[:, :, 0, :], ALU.subtract), reads=[raw], writes=[lbt])
                f.op('act', lambda e: e.activation(out=lbt[:], in_=lbt[:], func=AF.Sigmoid), reads=[lbt], writes=[lbt])
                f.op('dve', lambda e: e.tensor_scalar(oml[:], lbt[:], -1.0, 1.0, ALU.mult, ALU.add), reads=[lbt], writes=[oml])
            chm = f.sbuf(es, "chm", [128, NL], F32)
            f.dma('sp', chm[:], chunkm_in[:, :], writes=[chm])
            W = NL
            zq = f.sbuf(es, "zq", [128, W], F32); qs = f.sbuf(es, "qs", [128, W], F32)
            zz = f.sbuf(es, "zz", [128, W], F32); sg = f.sbuf(es, "sg", [128, W], F32)
            lf = f.sbuf(es, "lf", [128, W], F32); kk = f.sbuf(es, "kk", [128, W], F32)
            PP = f.sbuf(es, "PP", [128, W], F32); br = f.sbuf(es, "br", [128, W], F32)
            eq = f.sbuf(es, "eq", [128, W], F32); ek = f.sbuf(es, "ek", [128, W], F32)
            qo = Ring([f.sbuf(es, "qo", [128, W], BF16) for _ in range(2)])
            ko = Ring([f.sbuf(es, "ko", [128, W], BF16) for _ in range(2)])
            erl = Ring([f.sbuf(es, "erl", [128, 2, 32], F32) for _ in range(2)])
            att = Ring([f.sbuf(es, "att", [128, 2], F32) for _ in range(2)])
            pst = Ring([f.psum(es, "pst6", [128, 4, 128], F32) for _ in range(4)])
            stT = Ring([f.sbuf(es, "stT", [128, 4, 128], BF16) for _ in range(3)])
            v3 = lambda t: t[:].rearrange("p (j t) -> p j t", t=64)
            for h in range(8):
                for seg in range(2):
                    cs = slice(seg * NL, (seg + 1) * NL)
                    f.dma('sp', zq[:], projFM[h * 128:(h + 1) * 128, cs], writes=[zq])
                    f.op('act', lambda e: e.activation(out=qs[:], in_=zq[:], func=AF.Silu), reads=[zq], writes=[qs])
                    for dr in range(2):
                        f.dma('sp', zz[:], projFM[(8 + 8 * dr + h) * 128:(9 + 8 * dr + h) * 128, cs], writes=[zz])
                        f.op('dve', lambda e: e.tensor_scalar(zz[:], zz[:], -30.0, 30.0, ALU.max, ALU.min), reads=[zz], writes=[zz])
                        f.op('act', lambda e: e.activation(out=sg[:], in_=zz[:], func=AF.Sigmoid), reads=[zz], writes=[sg])
                        f.op('dve', lambda e: e.tensor_scalar(sg[:], sg[:], oml[:, dr, h:h + 1], lbt[:, dr, h:h + 1], ALU.mult, ALU.add),
                             reads=[sg, oml, lbt], writes=[sg])
                        f.op('act', lambda e: e.activation(out=lf[:], in_=sg[:], func=AF.Ln), reads=[sg], writes=[lf])
                        f.op('pool', lambda e: e.tensor_scalar(kk[:], sg[:], -1.0, 1.0, ALU.mult, ALU.add), reads=[sg], writes=[kk])
                        f.op('dve', lambda e: e.tensor_tensor_scan(PP[:], chm[:], lf[:], 0.0, ALU.mult, ALU.add), reads=[chm, lf], writes=[PP])
                        er_ = erl.next(); at_ = att.next()
                        if dr == 0:
                            f.op('dve', lambda e: e.tensor_tensor(v3(br), v3(PP), v3(PP)[:, :, 31:32].broadcast_to([128, 32, 64]), ALU.subtract),
                                 reads=[PP], writes=[br])
                            f.op('act', lambda e: e.activation(out=er_[:, 0, :], in_=v3(PP)[:, :, 31], func=AF.Exp), reads=[PP], writes=[er_])
                            f.op('act', lambda e: e.activation(out=er_[:, 1, :], in_=v3(br)[:, :, 63], func=AF.Exp), reads=[br], writes=[er_])
                        else:
                            f.op('dve', lambda e: e.tensor_tensor(v3(eq), v3(lf), v3(PP), ALU.subtract), reads=[lf, PP], writes=[eq])
                            f.op('dve', lambda e: e.tensor_tensor(v3(eq), v3(eq), v3(PP)[:, :, 63:64].broadcast_to([128, 32, 64]), ALU.add),
                                 reads=[eq, PP], writes=[eq])
                            f.op('dve', lambda e: e.tensor_tensor(v3(br), v3(eq), v3(eq)[:, :, 32:33].broadcast_to([128, 32, 64]), ALU.subtract),
                                 reads=[eq], writes=[br])
                            f.op('act', lambda e: e.activation(out=er_[:, 0, :], in_=v3(eq)[:, :, 32], func=AF.Exp), reads=[eq], writes=[er_])
                            f.op('act', lambda e: e.activation(out=er_[:, 1, :], in_=v3(br)[:, :, 0], func=AF.Exp), reads=[br], writes=[er_])
                        f.op('dve', lambda e: e.reduce_sum(at_[:, 0:1], v3(PP)[:, :, 63], AX.X), reads=[PP], writes=[at_])
                        f.op('act', lambda e: e.activation(out=at_[:, 1:2], in_=at_[:, 0:1], func=AF.Exp), reads=[at_], writes=[at_])
                        ds_ = dr * 2 + seg
                        f.dma('act', csin[:, 4096 + ds_ * 8 + h: 4096 + ds_ * 8 + h + 1], at_[:, 1:2], reads=[at_], allow_slow_non_contiguous=True)
                        f.dma('act', cer[dr * 128:(dr + 1) * 128, h * 64 + seg * 32: h * 64 + seg * 32 + 32], er_[:, 0, :], reads=[er_])
                        f.dma('act', cel[dr * 128:(dr + 1) * 128, h * 64 + seg * 32: h * 64 + seg * 32 + 32], er_[:, 1, :], reads=[er_])
                        f.op('act', lambda e: e.activation(out=eq[:], in_=br[:], func=AF.Exp), reads=[br], writes=[eq])
                        f.op('dve', lambda e: e.tensor_scalar(br[:], br[:], -80.0, None, ALU.max), reads=[br], writes=[br])
                        f.op('act', lambda e: e.activation(out=ek[:], in_=br[:], func=AF.Exp, scale=-1.0), reads=[br], writes=[ek])
                        qo_ = qo.next(); ko_ = ko.next()
                        f.op('pool', lambda e: e.tensor_tensor(qo_[:], qs[:], eq[:], ALU.mult), reads=[qs, eq], writes=[qo_])
                        f.op('dve', lambda e: e.tensor_tensor(ko_[:], kk[:], ek[:], ALU.mult), reads=[kk, ek], writes=[ko_])
                        f.dma('sp', cqT[dr * 1024 + h * 128: dr * 1024 + (h + 1) * 128, cs], qo_[:], reads=[qo_])
                        f.dma('sp', ckT[dr * 1024 + h * 128: dr * 1024 + (h + 1) * 128, cs], ko_[:], reads=[ko_])
                        for g4 in range(4):
                            ps = pst.next(); st = stT.next()
                            for k in range(4):
                                bk_ = g4 * 4 + k
                                f.op('pe', lambda e: e.matmul(ps[:, k, :], ko_[:, bk_ * 128:(bk_ + 1) * 128], ident[:], start=True, stop=True),
                                     reads=[ko_, ident], writes=[ps])
                            f.op('act', lambda e: e.copy(st[:], ps[:]), reads=[ps], writes=[st])
                            r0 = dr * NTOK + seg * NL + g4 * 512
                            f.dma('sp', ckTM.h.ap()[r0:r0 + 512, h * 128:(h + 1) * 128].rearrange("(j p) c -> p j c", p=128), st[:], reads=[st])
            f.barrier()
        if layer == 0 and BUILD_UPTO == 6.1:
            break
        with ExitStack() as es:
            ktm = Ring([f.sbuf(es, "ktm", [128, 1024], BF16) for _ in range(2)])
            cit = Ring([f.sbuf(es, "cit", [128, 1024], BF16) for _ in range(2)])
            psk = Ring([f.psum(es, "psk", [128, 4, 128], F32) for _ in range(4)])
            stki = Ring([f.sbuf(es, "stki", [128, 8, 128], F32) for _ in range(3)])
            for dr in range(2):
                for tt in range(NT):
                    seg, lt = tt // 16, tt % 16
                    km = ktm.next(); ci_ = cit.next()
                    f.dma('sp', km[:], ckTM[dr * NTOK + tt * 128: dr * NTOK + (tt + 1) * 128, :], writes=[km])
                    f.dma('sp', ci_[:], projTM[tt * 128:(tt + 1) * 128, TM_CI:TM_CI + 1024], writes=[ci_])
                    for half in range(2):
                        sk = stki.next()
                        hs = slice(half * 64, (half + 1) * 64)
                        for hg in range(2):
                            ps = psk.next()
                            for h4 in range(4):
                                h = hg * 4 + h4
                                f.op('pe', lambda e: e.matmul(ps[:, h4, :], km[hs, h * 128:(h + 1) * 128], ci_[hs, h * 128:(h + 1) * 128], start=True, stop=True),
                                     reads=[km, ci_], writes=[ps])
                            if hg == 0:
                                f.op('act', lambda e: e.copy(sk[:, 0:4, :], ps[:]), reads=[ps], writes=[sk])
                            else:
                                f.op('dve', lambda e: e.tensor_copy(sk[:, 4:8, :], ps[:]), reads=[ps], writes=[sk])
                        chunk = lt * 2 + half
                        r0 = ((dr * 2 + seg) * 32 + chunk) * 128
                        f.dma('act', cKI[r0:r0 + 128, :], sk[:].rearrange("p h v -> p (h v)"), reads=[sk])
            f.barrier()
        if layer == 0 and BUILD_UPTO == 6.2:
            break
        with ExitStack() as es:
            sins = [f.sbuf(es, "sin%d" % i, [128, 8, 128], F32) for i in range(4)]
            oh = f.sbuf(es, "oh", [128, 16], F32)
            f.dma('sp', oh[:], cscan_in[:, :], writes=[oh])
            cm = f.sbuf(es, "cm", [128, 2, 128], BF16)
            f.dma('sp', cm[:], cmask_in.h.ap().rearrange("p (a b) -> p a b", a=2), writes=[cm])

            def load_erl(es2, dr, seg):
                er = f.sbuf(es2, "er", [128, 8, 32], F32); el = f.sbuf(es2, "el", [128, 8, 32], F32)
                f.dma('sp', er[:], cer.h.ap()[dr * 128:(dr + 1) * 128].rearrange("p (h s j) -> p h s j", h=8, s=2)[:, :, seg, :], writes=[er])
                f.dma('sp', el[:], cel.h.ap()[dr * 128:(dr + 1) * 128].rearrange("p (h s j) -> p h s j", h=8, s=2)[:, :, seg, :], writes=[el])
                return er, el

            def bcj(t, j):
                return t[:, :, j:j + 1].broadcast_to([128, 8, 128])

            with ExitStack() as es2:
                state = f.sbuf(es2, "state", [128, 8, 128], F32)
                tmp = f.sbuf(es2, "tmp", [128, 8, 128], F32)
                kir = Ring([f.sbuf(es2, "ki", [128, 8, 128], F32) for _ in range(4)])
                for dr in range(2):
                    for seg in range(2):
                        with ExitStack() as es3:
                            er, el = load_erl(es3, dr, seg)
                            f.op('dve', lambda e: e.memset(state[:], 0.0), writes=[state])
                            order = range(32) if dr == 0 else range(31, -1, -1)
                            for j in order:
                                ki = kir.next()
                                r0 = ((dr * 2 + seg) * 32 + j) * 128
                                f.dma('sp', ki[:].rearrange("p h v -> p (h v)"), cKI[r0:r0 + 128, :], writes=[ki])
                                f.op('dve', lambda e: e.tensor_tensor(tmp[:], state[:], bcj(er, j), ALU.mult), reads=[state, er], writes=[tmp])
                                f.op('pool', lambda e: e.tensor_tensor(tmp[:], tmp[:], ki[:], ALU.add), reads=[tmp, ki], writes=[tmp])
                                f.op('dve', lambda e: e.tensor_tensor(state[:], tmp[:], bcj(el, j), ALU.mult), reads=[tmp, el], writes=[state])
                            ds_ = dr * 2 + seg
                            f.dma('act', csin[:, ds_ * 1024:(ds_ + 1) * 1024], state[:].rearrange("p h v -> p (h v)"), reads=[state])
                            f.barrier()
                f.barrier()
            f.collective("AllGather", ALL8, csin, csout)
            f.barrier()
            with ExitStack() as es2:
                G = f.sbuf(es2, "G", [128, 8, 1024], F32)
                At = f.sbuf(es2, "At", [128, 8, 8], F32)
                acc = f.sbuf(es2, "accs", [128, 8, 128], F32)
                gv = csout.h.ap().rearrange("(r p) n -> p r n", p=128)
                for dr in range(2):
                    for seg in range(2):
                        ds_ = dr * 2 + seg
                        sin_ = sins[ds_]
                        f.dma('sp', G[:], gv[:, :, ds_ * 1024:(ds_ + 1) * 1024], writes=[G])
                        f.dma('sp', At[:], gv[:, :, 4096 + ds_ * 8: 4096 + ds_ * 8 + 8], writes=[At])
                        f.op('dve', lambda e: e.memset(acc[:], 0.0), writes=[acc])
                        f.op('dve', lambda e: e.memset(sin_[:], 0.0), writes=[sin_])
                        order = range(8) if dr == 0 else range(7, -1, -1)
                        for r in order:
                            if seg == 0 and ((dr == 0 and r % 2 == 0) or (dr == 1 and r % 2 == 1)):
                                f.op('dve', lambda e: e.memset(acc[:], 0.0), writes=[acc])
                            f.op('dve', lambda e: e.scalar_tensor_tensor(sin_[:], acc[:], oh[:, dr * 8 + r: dr * 8 + r + 1], sin_[:], ALU.mult, ALU.add),
                                 reads=[acc, oh, sin_], writes=[sin_])
                            f.op('dve', lambda e: e.tensor_tensor(acc[:], acc[:], At[:, r, :].unsqueeze(2).broadcast_to([128, 8, 128]), ALU.mult),
                                 reads=[acc, At], writes=[acc])
                            f.op('dve', lambda e: e.tensor_tensor(acc[:], acc[:], G[:, r, :].rearrange("p (h v) -> p h v", v=128), ALU.add),
                                 reads=[acc, G], writes=[acc])
                f.barrier()
            with ExitStack() as es2:
                state = f.sbuf(es2, "state2", [128, 8, 128], F32)
                tmp = f.sbuf(es2, "tmp2", [128, 8, 128], F32)
                kir = Ring([f.sbuf(es2, "ki2", [128, 8, 128], F32) for _ in range(4)])
                Ebr = Ring([f.sbuf(es2, "Eb", [128, 8, 128], BF16) for _ in range(3)])
                qT_ = f.sbuf(es2, "cqTs", [128, 8, NL], BF16)
                kT_ = f.sbuf(es2, "ckTs", [128, 8, NL], BF16)
                cia = f.sbuf(es2, "cia", [128, 16, 1024], BF16)
                PTr = Ring([f.sbuf(es2, "PTc", [128, 8, 128], BF16) for _ in range(2)])
                psc = Ring([f.psum(es2, "psc", [128, 4, 128], F32) for _ in range(2)])
                por = Ring([f.psum(es2, "po", [64, 8, 128], F32) for _ in range(2)])
                osr = Ring([f.sbuf(es2, "osc", [64, 8, 128], F32) for _ in range(3)])
                for dr in range(2):
                    for seg in range(2):
                        with ExitStack() as es3:
                            er, el = load_erl(es3, dr, seg)
                            cs = slice(seg * NL, (seg + 1) * NL)
                            f.dma('sp', qT_[:], cqT.h.ap()[dr * 1024:(dr + 1) * 1024, cs].rearrange("(h p) t -> p h t", p=128), writes=[qT_])
                            f.dma('sp', kT_[:], ckT.h.ap()[dr * 1024:(dr + 1) * 1024, cs].rearrange("(h p) t -> p h t", p=128), writes=[kT_])
                            f.dma('sp', cia[:], projTM.h.ap()[seg * NL:(seg + 1) * NL, TM_CI:TM_CI + 1024].rearrange("(j p) c -> p j c", p=128), writes=[cia])
                            f.op('dve', lambda e: e.tensor_copy(state[:], sins[dr * 2 + seg][:]), reads=[sins[dr * 2 + seg]], writes=[state])
                            order = range(32) if dr == 0 else range(31, -1, -1)
                            PT = None
                            for step, j in enumerate(order):
                                lt, half = j // 2, j % 2
                                tc = slice(lt * 128, (lt + 1) * 128)
                                if step % 2 == 0:
                                    PT = PTr.next()
                                    for hg in range(2):
                                        ps = psc.next()
                                        for h4 in range(4):
                                            h = hg * 4 + h4
                                            f.op('pe', lambda e: e.matmul(ps[:, h4, :], kT_[:, h, tc], qT_[:, h, tc], start=True, stop=True),
                                                 reads=[kT_, qT_], writes=[ps])
                                        f.op('dve', lambda e: e.tensor_tensor(PT[:, hg * 4:(hg + 1) * 4, :], ps[:], cm[:, dr:dr + 1, :].broadcast_to([128, 4, 128]), ALU.mult),
                                             reads=[ps, cm], writes=[PT])
                                ki = kir.next(); Eb = Ebr.next()
                                r0 = ((dr * 2 + seg) * 32 + j) * 128
                                f.dma('sp', ki[:].rearrange("p h v -> p (h v)"), cKI[r0:r0 + 128, :], writes=[ki])
                                f.op('dve', lambda e: e.tensor_tensor(tmp[:], state[:], bcj(er, j), ALU.mult), reads=[state, er], writes=[tmp])
                                f.op('act', lambda e: e.copy(Eb[:], tmp[:]), reads=[tmp], writes=[Eb])
                                f.op('pool', lambda e: e.tensor_tensor(tmp[:], tmp[:], ki[:], ALU.add), reads=[tmp, ki, Eb], writes=[tmp])
                                f.op('dve', lambda e: e.tensor_tensor(state[:], tmp[:], bcj(el, j), ALU.mult), reads=[tmp, el], writes=[state])
                                po = por.next(); os_ = osr.next()
                                for bnk in range(2):
                                    f.op('pe', lambda e: e.matmul(po[:, bnk * 4:(bnk + 1) * 4, :].rearrange("p a b -> p (a b)"), zeroW[:, 0:64], cia[:, lt, 0:512], start=True, stop=False),
                                         reads=[zeroW, cia], writes=[po])
                                for h in range(8):
                                    f.op('pe', lambda e: e.matmul(po[:, h, :], PT[:, h, half * 64:(half + 1) * 64], cia[:, lt, h * 128:(h + 1) * 128], start=False, stop=False),
                                         reads=[PT, cia], writes=[po])
                                    f.op('pe', lambda e: e.matmul(po[:, h, :], qT_[:, h, j * 64:(j + 1) * 64], Eb[:, h, :], start=False, stop=True),
                                         reads=[qT_, Eb], writes=[po])
                                if step % 2 == 0:
                                    f.op('act', lambda e: e.copy(os_[:], po[:]), reads=[po], writes=[os_])
                                else:
                                    f.op('dve', lambda e: e.tensor_copy(os_[:], po[:]), reads=[po], writes=[os_])
                                r1 = dr * NTOK + seg * NL + j * 64
                                f.dma('act', coC[r1:r1 + 64, :], os_[:].rearrange("p h v -> p (h v)"), reads=[os_])
                            f.barrier()
                f.barrier()
            f.barrier()
        if layer == 0 and BUILD_UPTO == 6:
            break

        with ExitStack() as es:
            kT = f.sbuf(es, "kT", [128, 16384], BF16)
            vA = f.sbuf(es, "vA", [128, 128, 128], BF16)
            onesA = f.sbuf(es, "onesA", [128, 128], BF16)
            f.op('dve', lambda e: e.memset(onesA[:], 1.0), writes=[onesA])
            qtr = Ring([f.sbuf(es, "qt", [128, 512], BF16) for _ in range(3)])
            psS = Ring([f.psum(es, "psS", [128, 2, 512], F32) for _ in range(2)])
            accNr = Ring([f.psum(es, "accN", [128, 512], F32) for _ in range(2)])
            accDr = Ring([f.psum(es, "accD", [128, 512], F32) for _ in range(2)])
            pTr = Ring([f.sbuf(es, "pT", [128, 2, 512], BF16) for _ in range(4)])
            rcr = Ring([f.sbuf(es, "rc", [128, 512], F32) for _ in range(2)])
            ostr = Ring([f.sbuf(es, "ost", [128, 512], BF16) for _ in range(2)])
            daccr = Ring([f.sbuf(es, "dacc", [128, 2, 512], F32) for _ in range(2)])
            dsbr = Ring([f.sbuf(es, "dsb", [128, 512], BF16) for _ in range(2)])
            for seg in range(2):
                S = SEG_S[seg]; nkt = S // 128; R = S // NL
                ngrp = nkt // 2
                for kvh in range(2):
                    f.dma('sp', kT[:, 0:S].rearrange("p (r t) -> p r t", r=R),
                          akout[seg].h.ap().rearrange("(r hd) t -> hd r t", hd=256)[kvh * 128:(kvh + 1) * 128], writes=[kT])
                    vsrc = avout[seg].h.ap().rearrange("(j p) c -> p j c", p=128)
                    for j0 in range(0, nkt, 16):
                        f.dma('sp', vA[:, j0:j0 + 16, :], vsrc[:, j0:j0 + 16, kvh * 128:(kvh + 1) * 128], writes=[vA])
                    pending = None

                    def stage2(p):
                        pT, kg, aN, aD, h, q0, dacc = p
                        for j in range(2):
                            kt = 2 * kg + j
                            f.op('pe', lambda e: e.matmul(aN[:], vA[:, kt, :], pT[:, j, :], start=(kt == 0), stop=(kt == nkt - 1)),
                                 reads=[pT, vA], writes=[aN])
                            f.op('pe', lambda e: e.matmul(aD[:], onesA[:], pT[:, j, :], start=(kt == 0), stop=(kt == nkt - 1)),
                                 reads=[pT, onesA], writes=[aD])
                        if kg == ngrp - 1:
                            rc = rcr.next(); ost = ostr.next()
                            f.op('dve', lambda e: e.reciprocal(rc[:], aD[:]), reads=[aD], writes=[rc])
                            f.op('dve', lambda e: e.tensor_tensor(ost[:], aN[:], rc[:], ALU.mult), reads=[aN, rc], writes=[ost])
                            f.dma('act', oAT[h * 128:(h + 1) * 128, q0:q0 + 512], ost[:], reads=[ost])

                    for g in range(4):
                        h = kvh * 4 + g
                        for qb in range(4):
                            qt = qtr.next()
                            q0 = seg * NL + qb * 512
                            f.dma('sp', qt[:], qTA[h * 128:(h + 1) * 128, q0:q0 + 512], writes=[qt])
                            aN = accNr.next(); aD = accDr.next(); dacc = daccr.next()
                            for kg in range(ngrp):
                                ps = psS.next(); pT = pTr.next()
                                for j in range(2):
                                    kt = 2 * kg + j
                                    f.op('pe', lambda e: e.matmul(ps[:, j, :], kT[:, kt * 128:(kt + 1) * 128], qt[:], start=True, stop=True),
                                         reads=[kT, qt], writes=[ps])
                                f.op('act', lambda e: e.activation(out=pT[:], in_=ps[:], func=AF.Exp), reads=[ps], writes=[pT])
                                if pending is not None:
                                    stage2(pending)
                                pending = (pT, kg, aN, aD, h, q0, dacc)
                    stage2(pending)
            f.barrier()
        if layer == 0 and BUILD_UPTO == 4:
            break

        with ExitStack() as es:
            selI = f.sbuf(es, "selI", [128, 32, 128], BF16)
            f.dma('sp', selI[:], selI_in.h.ap().rearrange("p (i m) -> p i m", m=128), writes=[selI])
            candK = Ring([f.sbuf(es, "candK", [128, 1024], BF16) for _ in range(4)])
            candV = Ring([f.sbuf(es, "candV", [128, 768], BF16) for _ in range(4)])
            psE = Ring([f.psum(es, "psE", [128, 2, 512], F32) for _ in range(3)])
            steK = Ring([f.sbuf(es, "steK", [128, 2, 512], BF16) for _ in range(2)])
            steV = Ring([f.sbuf(es, "steV", [128, 2, 384], BF16) for _ in range(2)])
            for seg in range(2):
                f.dma('sp', extK[seg * 768:(seg + 1) * 768, 1024:3072], bkin[seg * 768:(seg + 1) * 768, :], reads=[bkin], writes=[extK])
                f.dma('sp', extV[seg * 4096 + 1024: seg * 4096 + 3072, :], bvin[seg * NL:(seg + 1) * NL, :], reads=[bvin], writes=[extV])
                for side in range(2):
                    scol = 1024 if side == 0 else 0
                    ecol = 0 if side == 0 else 3072
                    si = (seg * 2 + side) * 8
                    for hp in range(6):
                        ps = psE.next()
                        for r in range(8):
                            cd = candK.next()
                            r0 = r * 1536 + seg * 768 + hp * 128
                            f.dma('sp', cd[:], bkout[r0:r0 + 128, scol:scol + 1024], writes=[cd])
                            for j in range(2):
                                f.op('pe', lambda e: e.matmul(ps[:, j, :], selI[:, si + r, :], cd[:, j * 512:(j + 1) * 512], start=(r == 0), stop=(r == 7)),
                                     reads=[selI, cd], writes=[ps])
                        st = steK.next()
                        f.op('act', lambda e: e.copy(st[:], ps[:]), reads=[ps], writes=[st])
                        f.dma('act', extK[seg * 768 + hp * 128: seg * 768 + (hp + 1) * 128, ecol:ecol + 1024].rearrange("p (j n) -> p j n", j=2), st[:], reads=[st])
                    for kt in range(8):
                        ps = psE.next()
                        for r in range(8):
                            cv = candV.next()
                            r0 = r * 4096 + seg * 2048 + scol + kt * 128
                            f.dma('sp', cv[:], bvout[r0:r0 + 128, :], writes=[cv])
                            for j in range(2):
                                f.op('pe', lambda e: e.matmul(ps[:, j, 0:384], selI[:, si + r, :], cv[:, j * 384:(j + 1) * 384], start=(r == 0), stop=(r == 7)),
                                     reads=[selI, cv], writes=[ps])
                        st = steV.next()
                        f.op('dve', lambda e: e.tensor_copy(st[:], ps[:, :, 0:384]), reads=[ps], writes=[st])
                        e0 = seg * 4096 + ecol + kt * 128
                        f.dma('act', extV[e0:e0 + 128, :].rearrange("p (j n) -> p j n", j=2), st[:], reads=[st])
            f.barrier()
        with ExitStack() as es:
            bmask = f.sbuf(es, "bmask", [128, 34, 512], BF16)
            f.dma('sp', bmask[:], bmask_in.h.ap().rearrange("p (i m) -> p i m", m=512), writes=[bmask])
            bval = f.sbuf(es, "bval", [128, 64], F32)
            f.dma('sp', bval[:], bvalid_in[:, :], writes=[bval])
            kTe = f.sbuf(es, "kTe", [128, 6, 4096], BF16)
            vext = f.sbuf(es, "vext", [128, 32, 768], BF16)
            vones = f.sbuf(es, "vones", [128, 32, 64], BF16)
            qtbr = Ring([f.sbuf(es, "qtb", [128, 6, 512], BF16) for _ in range(2)])
            psB = Ring([f.psum(es, "psB", [128, 512], F32) for _ in range(4)])
            accBNr = Ring([f.psum(es, "accBN", [64, 512], F32) for _ in range(2)])
            accBDr = Ring([f.psum(es, "accBD", [64, 512], F32) for _ in range(2)])
            pTb = Ring([f.sbuf(es, "pTb", [128, 512], BF16) for _ in range(4)])
            pmb = Ring([f.sbuf(es, "pmb", [128, 512], BF16) for _ in range(5)])
            rcb = Ring([f.sbuf(es, "rcb", [64, 512], F32) for _ in range(2)])
            ostb = Ring([f.sbuf(es, "ostb", [64, 512], BF16) for _ in range(2)])
            for seg in range(2):
                f.dma('sp', kTe[:], extK.h.ap()[seg * 768:(seg + 1) * 768].rearrange("(j p) t -> p j t", p=128), writes=[kTe])
                for ch in range(4):
                    f.dma('sp', vext[:, ch * 8:(ch + 1) * 8, :],
                          extV.h.ap()[seg * 4096 + ch * 1024: seg * 4096 + (ch + 1) * 1024].rearrange("(j p) c -> p j c", p=128), writes=[vext])
                f.op('dve', lambda e: e.tensor_copy(vones[:], bval[:, seg * 32:(seg + 1) * 32].unsqueeze(2).broadcast_to([128, 32, 64])),
                     reads=[bval], writes=[vones])
                pend = []

                def stage2b(p):
                    pm, kt, hh, first, last, aN, aD, h, q0 = p
                    f.op('pe', lambda e: e.matmul(aN[:], vext[:, kt, hh * 64:(hh + 1) * 64], pm[:], start=first, stop=last),
                         reads=[pm, vext], writes=[aN])
                    f.op('pe', lambda e: e.matmul(aD[:], vones[:, kt, :], pm[:], start=first, stop=last),
                         reads=[pm, vones], writes=[aD])
                    if last:
                        rc = rcb.next(); ost = ostb.next()
                        f.op('dve', lambda e: e.reciprocal(rc[:], aD[:]), reads=[aD], writes=[rc])
                        f.op('dve', lambda e: e.tensor_tensor(ost[:], aN[:], rc[:], ALU.mult), reads=[aN, rc], writes=[ost])
                        f.dma('act', oBT[h * 64:(h + 1) * 64, q0:q0 + 512], ost[:], reads=[ost])

                for qb in range(4):
                    q0 = seg * NL + qb * 512
                    qtb = qtbr.next()
                    f.dma('sp', qtb[:], qTB.h.ap().rearrange("(j p) t -> p j t", p=128)[:, :, q0:q0 + 512], writes=[qtb])
                    for h in range(4):
                        tl = [(0, kt, kt - (7 + 4 * qb)) for kt in range(7 + 4 * qb, 13 + 4 * qb)]
                        tl += [(1, kt, 6 + kt - (6 + 4 * qb)) for kt in range(6 + 4 * qb, 14 + 4 * qb)]
                        tl += [(2, kt, 14 + kt - 4 * qb) for kt in range(4 * qb, 4 * qb + 20)]
                        aN = accBNr.next(); aD = accBDr.next()
                        for idx, (g, kt, mi) in enumerate(tl):
                            hh = g * 4 + h; hp = hh // 2; b0 = (hh % 2) * 64
                            ps = psB.next(); pT = pTb.next(); pm = pmb.next()
                            f.op('pe', lambda e: e.matmul(ps[:], kTe[b0:b0 + 64, hp, kt * 128:(kt + 1) * 128], qtb[b0:b0 + 64, hp, :], start=True, stop=True),
                                 reads=[kTe, qtb], writes=[ps])
                            f.op('act', lambda e: e.activation(out=pT[:], in_=ps[:], func=AF.Exp), reads=[ps], writes=[pT])
                            f.op('dve', lambda e: e.tensor_tensor(pm[:], pT[:], bmask[:, mi, :], ALU.mult), reads=[pT, bmask], writes=[pm])
                            pend.append((pm, kt, hh, idx == 0, idx == len(tl) - 1, aN, aD, h, q0))
                            if len(pend) > 3:
                                stage2b(pend.pop(0))
                while pend:
                    stage2b(pend.pop(0))
            f.barrier()
        if layer == 0 and BUILD_UPTO == 5:
            break

        with ExitStack() as es:
            def load_w(name, src2d, kch, ncol, es_=es):
                wt = f.sbuf(es_, name, [128, kch, ncol], BF16)
                with ExitStack() as est:
                    tmpw = Ring([f.sbuf(est, "tmpw", [128, 512], F32) for _ in range(3)])
                    sv = src2d.rearrange("(kc p) n -> p kc n", p=128)
                    i = 0
                    for kc in range(kch):
                        for c0 in range(0, ncol, 512):
                            w = min(512, ncol - c0)
                            tw = tmpw.next()
                            f.dma('sp', tw[:, 0:w], sv[:, kc, c0:c0 + w], writes=[tw])
                            E = ('dve', 'pool')[i % 2]; i += 1
                            f.op(E, lambda e: e.tensor_copy(wt[:, kc, c0:c0 + w], tw[:, 0:w]), reads=[tw], writes=[wt])
                    f.barrier()
                return wt
            wA = load_w("wA", I["w_br_a"].h.ap()[layer], 8, D)
            wBb = load_w("wBb", I["w_br_b"].h.ap()[layer], 2, D)
            wC = load_w("wC", I["w_br_c"].h.ap()[layer], 8, D)
            wMo = load_w("wMo", I["w_mix_out"].h.ap()[layer], 8, D)
            wQ = load_w("wQ", I["x_wq"].h.ap()[layer], 8, D)
            wO = load_w("wO", I["x_wo"].h.ap()[layer], 8, D)
            gcn = load_bcast_row(es, "gcn", I["c_gnorm"][layer:layer + 1, :], D)
            gcr = load_bcast_row(es, "gcr", I["g_cross"][layer:layer + 1, :], D)
            gff = load_bcast_row(es, "gff", I["g_ffn"][layer:layer + 1, :], D)
            gxq = f.sbuf(es, "gxq", [128, 4, 256], F32)
            for h in range(4):
                f.dma('sp', gxq[:, h, :], I["x_gq"][layer:layer + 1, :].partition_broadcast(128), writes=[gxq])
            f.op('dve', lambda e: e.tensor_scalar(gxq[:], gxq[:], 256.0 ** -0.5, None, ALU.mult), reads=[gxq], writes=[gxq])
            kTm = f.sbuf(es, "kTm", [128, 2, 8, 256], BF16)
            vmem = f.sbuf(es, "vmem", [128, 2, 2, 4, 257], BF16)
            f.op('dve', lambda e: e.memset(vmem[:, :, :, :, 256:257], 1.0), writes=[vmem])
            pT4 = Ring([f.psum(es, "pT4", [128, 4, 128], F32) for _ in range(2)])
            pM = Ring([f.psum(es, "pM", [128, 512], F32) for _ in range(2)])
            with ExitStack() as es2:
                gmm = load_bcast_row(es2, "gmm", I["g_mem"][layer:layer + 1, :], D)
                gxk = f.sbuf(es2, "gxk", [128, 4, 256], F32)
                for h in range(4):
                    f.dma('sp', gxk[:, h, :], I["x_gk"][layer:layer + 1, :].partition_broadcast(128), writes=[gxk])
                wKV = load_w("wKV", I["x_wkv"].h.ap()[layer], 8, 2 * D, es_=es2)
                mt = f.sbuf(es2, "mt", [128, D], F32); sqm = f.sbuf(es2, "sqm", [128, D], F32)
                rsm = f.sbuf(es2, "rsm", [128, 4], F32); mnb = f.sbuf(es2, "mnb", [128, D], BF16)
                mT = f.sbuf(es2, "mT", [128, 8, 128], BF16)
                kf = f.sbuf(es2, "kf", [128, 4, 256], F32); kb = f.sbuf(es2, "kb", [128, D], BF16)
                for seg in range(2):
                    for mtile in range(2):
                        f.dma('sp', mt[:], mem_in[seg * 256 + mtile * 128: seg * 256 + (mtile + 1) * 128, :], writes=[mt])
                        rms_rows(es2, mt, D, rsm, sqm)
                        f.op('dve', lambda e: e.scalar_tensor_tensor(mnb[:], mt[:], rsm[:, 0:1], gmm[:], ALU.mult, ALU.mult), reads=[mt, rsm, gmm], writes=[mnb])
                        transpose_blocks(mnb, 8, pT4, lambda k0, n: mT[:, k0:k0 + n, :], [mT])
                        for cb in range(4):
                            ps = pM.next()
                            for kc in range(8):
                                f.op('pe', lambda e: e.matmul(ps[:], mT[:, kc, :], wKV[:, kc, cb * 512:(cb + 1) * 512], start=(kc == 0), stop=(kc == 7)),
                                     reads=[mT, wKV], writes=[ps])
                            if cb < 2:
                                f.op('act', lambda e: e.copy(kf[:, cb * 2:(cb + 1) * 2, :], ps[:].rearrange("p (h d) -> p h d", d=256)), reads=[ps], writes=[kf])
                            else:
                                f.op('act', lambda e: e.copy(vmem[:, seg, mtile, (cb - 2) * 2:(cb - 1) * 2, 0:256], ps[:].rearrange("p (h d) -> p h d", d=256)),
                                     reads=[ps], writes=[vmem])
                        f.op('dve', lambda e: e.tensor_tensor(sqm[:], kf[:].rearrange("p h d -> p (h d)"), kf[:].rearrange("p h d -> p (h d)"), ALU.mult), reads=[kf], writes=[sqm])
                        f.op('dve', lambda e: e.reduce_sum(rsm[:], sqm[:].rearrange("p (h d) -> p h d", d=256), AX.X), reads=[sqm], writes=[rsm])
                        f.op('dve', lambda e: e.tensor_scalar(rsm[:], rsm[:], 1.0 / 256, EPS, ALU.mult, ALU.add), reads=[rsm], writes=[rsm])
                        f.op('act', lambda e: e.activation(out=rsm[:], in_=rsm[:], func=AF.Sqrt), reads=[rsm], writes=[rsm])
                        f.op('dve', lambda e: e.reciprocal(rsm[:], rsm[:]), reads=[rsm], writes=[rsm])
                        f.op('dve', lambda e: e.tensor_tensor(kf[:], kf[:], rsm[:].unsqueeze(2).broadcast_to([128, 4, 256]), ALU.mult), reads=[kf, rsm], writes=[kf])
                        f.op('dve', lambda e: e.tensor_tensor(kb[:].rearrange("p (h d) -> p h d", d=256), kf[:], gxk[:], ALU.mult), reads=[kf, gxk], writes=[kb])
                        transpose_blocks(kb, 8, pT4, lambda k0, n: kTm[:, seg, k0:k0 + n, mtile * 128:(mtile + 1) * 128], [kTm])
                f.barrier()
            with ExitStack() as es2:
                oAt = Ring([f.sbuf(es2, "oAt", [128, D], BF16) for _ in range(1)])
                oBt = Ring([f.sbuf(es2, "oBt", [128, 256], BF16) for _ in range(1)])
                cft = Ring([f.sbuf(es2, "cft", [128, D], F32) for _ in range(1)])
                cbt = Ring([f.sbuf(es2, "cbt", [128, D], F32) for _ in range(1)])
                cgt = Ring([f.sbuf(es2, "cgt", [128, 4096], BF16) for _ in range(1)])
                xt_ = Ring([f.sbuf(es2, "xt7", [128, D], F32) for _ in range(1)])
                sq7 = f.sbuf(es2, "sq7", [128, D], F32)
                rs7 = f.sbuf(es2, "rs7", [128, 4], F32)
                slt = f.sbuf(es2, "slt", [128, D], F32)
                ocb = f.sbuf(es2, "ocb", [128, D], BF16)
                aT = f.sbuf(es2, "aT", [128, 18, 128], BF16)
                gsg = f.sbuf(es2, "gsg", [128, 3, D], BF16)
                mrg = f.sbuf(es2, "mrg", [128, D], F32); mtmp = f.sbuf(es2, "mtmp", [128, 512], F32)
                mrb = f.sbuf(es2, "mrb", [128, D], BF16)
                mTt = f.sbuf(es2, "mTt", [128, 8, 128], BF16)
                x1 = f.sbuf(es2, "x1", [128, D], F32)
                u2 = f.sbuf(es2, "u2", [128, D], BF16); u2T = f.sbuf(es2, "u2T", [128, 8, 128], BF16)
                qf = f.sbuf(es2, "qf", [128, 4, 256], F32); qb_ = f.sbuf(es2, "qb7", [128, D], BF16)
                qTx = f.sbuf(es2, "qTx", [128, 8, 128], BF16)
                psS7 = f.psum(es2, "psS7", [128, 8, 128], F32)
                psO7 = Ring([f.psum(es2, "psO7", [128, 512], F32) for _ in range(2)])
                PT7 = f.sbuf(es2, "PT7", [128, 8, 128], BF16)
                rc7 = f.sbuf(es2, "rc7", [128, 4], F32)
                ox = f.sbuf(es2, "ox", [128, D], BF16); oxT = f.sbuf(es2, "oxT", [128, 8, 128], BF16)
                x2 = Ring([f.sbuf(es2, "x2", [128, D], F32) for _ in range(1)])
                u3 = Ring([f.sbuf(es2, "u3", [128, D], BF16) for _ in range(1)])
                u3s = Ring([f.sbuf(es2, "u3s", [128, 8, 128], BF16) for _ in range(1)])
                for tt in range(NT):
                    seg, lt = tt // 16, tt % 16
                    rows = slice(tt * 128, (tt + 1) * 128)
                    oa = oAt.next(); ob = oBt.next(); cf = cft.next(); cb_ = cbt.next(); cg = cgt.next(); xt = xt_.next()
                    f.dma('sp', aT[:, 0:8, :], oAT.h.ap().rearrange("(k p) t -> p k t", p=128)[:, :, rows], writes=[aT])
                    f.dma('sp', aT[:, 8:10, :], oBT.h.ap().rearrange("(k p) t -> p k t", p=128)[:, :, rows], writes=[aT])
                    f.dma('sp', cf[:], coC[tt * 128:(tt + 1) * 128, :], writes=[cf])
                    f.dma('sp', cb_[:], coC[NTOK + tt * 128: NTOK + (tt + 1) * 128, :], writes=[cb_])
                    f.dma('sp', cg[:], projTM[rows, TM_CG:TM_CG + 4096], writes=[cg])
                    f.dma('sp', xt[:], xsrc[rows, :], writes=[xt])
                    f.op('pool', lambda e: e.tensor_tensor(cf[:], cf[:], cb_[:], ALU.add), reads=[cf, cb_], writes=[cf])
                    rms_rows(es2, cf, D, rs7, sq7)
                    f.op('act', lambda e: e.activation(out=slt[:], in_=cg[:, 0:1024], func=AF.Silu), reads=[cg], writes=[slt])
                    f.op('dve', lambda e: e.scalar_tensor_tensor(cf[:], cf[:], rs7[:, 0:1], gcn[:], ALU.mult, ALU.mult), reads=[cf, rs7, gcn], writes=[cf])
                    f.op('dve', lambda e: e.tensor_tensor(ocb[:], cf[:], slt[:], ALU.mult), reads=[cf, slt], writes=[ocb])
                    f.op('act', lambda e: e.activation(out=gsg[:].rearrange("p a b -> p (a b)"), in_=cg[:, 1024:4096], func=AF.Sigmoid), reads=[cg], writes=[gsg])
                    transpose_blocks(ocb, 8, pT4, lambda k0, n: aT[:, 10 + k0:10 + k0 + n, :], [aT])
                    for cbk in range(2):
                        cs = slice(cbk * 512, (cbk + 1) * 512)
                        for bi, (wt, k0, nk) in enumerate(((wA, 0, 8), (wBb, 8, 2), (wC, 10, 8))):
                            ps = pM.next()
                            for kc in range(nk):
                                f.op('pe', lambda e: e.matmul(ps[:], aT[:, k0 + kc, :], wt[:, kc, cs], start=(kc == 0), stop=(kc == nk - 1)),
                                     reads=[aT, wt], writes=[ps])
                            if bi == 0:
                                f.op('dve', lambda e: e.tensor_tensor(mrg[:, cs], ps[:], gsg[:, 0, cs], ALU.mult), reads=[ps, gsg], writes=[mrg])
                            else:
                                f.op('dve', lambda e: e.tensor_tensor(mtmp[:], ps[:], gsg[:, bi, cs], ALU.mult), reads=[ps, gsg], writes=[mtmp])
                                f.op('pool', lambda e: e.tensor_tensor(mrg[:, cs], mrg[:, cs], mtmp[:], ALU.add), reads=[mrg, mtmp], writes=[mrg])
                    f.op('act', lambda e: e.copy(mrb[:], mrg[:]), reads=[mrg], writes=[mrb])
                    transpose_blocks(mrb, 8, pT4, lambda k0, n: mTt[:, k0:k0 + n, :], [mTt])
                    for cbk in range(2):
                        cs = slice(cbk * 512, (cbk + 1) * 512)
                        ps = pM.next()
                        for kc in range(8):
                            f.op('pe', lambda e: e.matmul(ps[:], mTt[:, kc, :], wMo[:, kc, cs], start=(kc == 0), stop=(kc == 7)), reads=[mTt, wMo], writes=[ps])
                        f.op('dve', lambda e: e.tensor_tensor(x1[:, cs], ps[:], xt[:, cs], ALU.add), reads=[ps, xt], writes=[x1])
                    rms_rows(es2, x1, D, rs7, sq7)
                    f.op('dve', lambda e: e.scalar_tensor_tensor(u2[:], x1[:], rs7[:, 0:1], gcr[:], ALU.mult, ALU.mult), reads=[x1, rs7, gcr], writes=[u2])
                    transpose_blocks(u2, 8, pT4, lambda k0, n: u2T[:, k0:k0 + n, :], [u2T])
                    for cbk in range(2):
                        ps = pM.next()
                        for kc in range(8):
                            f.op('pe', lambda e: e.matmul(ps[:], u2T[:, kc, :], wQ[:, kc, cbk * 512:(cbk + 1) * 512], start=(kc == 0), stop=(kc == 7)), reads=[u2T, wQ], writes=[ps])
                        f.op('act', lambda e: e.copy(qf[:, cbk * 2:(cbk + 1) * 2, :], ps[:].rearrange("p (h d) -> p h d", d=256)), reads=[ps], writes=[qf])
                    f.op('dve', lambda e: e.tensor_tensor(sq7[:], qf[:].rearrange("p h d -> p (h d)"), qf[:].rearrange("p h d -> p (h d)"), ALU.mult), reads=[qf], writes=[sq7])
                    f.op('dve', lambda e: e.reduce_sum(rs7[:], sq7[:].rearrange("p (h d) -> p h d", d=256), AX.X), reads=[sq7], writes=[rs7])
                    f.op('dve', lambda e: e.tensor_scalar(rs7[:], rs7[:], 1.0 / 256, EPS, ALU.mult, ALU.add), reads=[rs7], writes=[rs7])
                    f.op('act', lambda e: e.activation(out=rs7[:], in_=rs7[:], func=AF.Sqrt), reads=[rs7], writes=[rs7])
                    f.op('dve', lambda e: e.reciprocal(rs7[:], rs7[:]), reads=[rs7], writes=[rs7])
                    f.op('dve', lambda e: e.tensor_tensor(qf[:], qf[:], rs7[:].unsqueeze(2).broadcast_to([128, 4, 256]), ALU.mult), reads=[qf, rs7], writes=[qf])
                    f.op('pool', lambda e: e.tensor_tensor(qb_[:].rearrange("p (h d) -> p h d", d=256), qf[:], gxq[:], ALU.mult), reads=[qf, gxq], writes=[qb_])
                    transpose_blocks(qb_, 8, pT4, lambda k0, n: qTx[:, k0:k0 + n, :], [qTx])
                    for h in range(4):
                        for kt in range(2):
                            for dc in range(2):
                                f.op('pe', lambda e: e.matmul(psS7[:, h * 2 + kt, :], kTm[:, seg, h * 2 + dc, kt * 128:(kt + 1) * 128], qTx[:, h * 2 + dc, :],
                                                              start=(dc == 0), stop=(dc == 1)), reads=[kTm, qTx], writes=[psS7])
                    f.op('act', lambda e: e.activation(out=PT7[:], in_=psS7[:], func=AF.Exp), reads=[psS7], writes=[PT7])
                    for h in range(4):
                        ps = psO7.next()
                        for kt in range(2):
                            f.op('pe', lambda e: e.matmul(ps[:, 0:257], PT7[:, h * 2 + kt, :], vmem[:, seg, kt, h, :], start=(kt == 0), stop=(kt == 1)),
                                 reads=[PT7, vmem], writes=[ps])
                        f.op('dve', lambda e: e.reciprocal(rc7[:, h:h + 1], ps[:, 256:257]), reads=[ps], writes=[rc7])
                        f.op('dve', lambda e: e.tensor_scalar(ox[:, h * 256:(h + 1) * 256], ps[:, 0:256], rc7[:, h:h + 1], None, ALU.mult), reads=[ps, rc7], writes=[ox])
                    transpose_blocks(ox, 8, pT4, lambda k0, n: oxT[:, k0:k0 + n, :], [oxT])
                    xo = x2.next()
                    for cbk in range(2):
                        cs = slice(cbk * 512, (cbk + 1) * 512)
                        ps = pM.next()
                        for kc in range(8):
                            f.op('pe', lambda e: e.matmul(ps[:], oxT[:, kc, :], wO[:, kc, cs], start=(kc == 0), stop=(kc == 7)), reads=[oxT, wO], writes=[ps])
                        f.op('dve', lambda e: e.tensor_tensor(xo[:, cs], ps[:], x1[:, cs], ALU.add), reads=[ps, x1], writes=[xo])
                    f.dma('act', xmid[rows, :], xo[:], reads=[xo])
                    rms_rows(es2, xo, D, rs7, sq7)
                    u3_ = u3.next(); u3s_ = u3s.next()
                    f.op('dve', lambda e: e.scalar_tensor_tensor(u3_[:], xo[:], rs7[:, 0:1], gff[:], ALU.mult, ALU.mult), reads=[xo, rs7, gff], writes=[u3_])
                    transpose_blocks(u3_, 8, pT4, lambda k0, n: u3s_[:, k0:k0 + n, :], [u3s_])
                    f.dma('act', u3T.h.ap().rearrange("(kc p) t -> p kc t", p=128)[:, :, rows], u3s_[:], reads=[u3s_])
                    if lt == 0:
                        f.dma('act', fgin[seg * 2:seg * 2 + 1, :], u3_[0:1, :], reads=[u3_])
                    if lt == 15:
                        f.dma('act', fgin[seg * 2 + 1:seg * 2 + 2, :], u3_[127:128, :], reads=[u3_])
                f.barrier()
            f.barrier()
        f.collective("AllGather", ALL8, fgin, fgout)
        f.barrier()
        with ExitStack() as es:
            EXT = NL + 2
            uTe = f.sbuf(es, "uTe", [128, 8, 2, EXT], BF16)
            for seg in range(2):
                f.dma('sp', uTe[:, :, seg, 1:NL + 1], u3T.h.ap().rearrange("(kc p) t -> p kc t", p=128)[:, :, seg * NL:(seg + 1) * NL], writes=[uTe])
            gall = f.sbuf(es, "gall", [32, D], BF16)
            fsl = f.sbuf(es, "fsl", [32, 4], BF16)
            f.dma('sp', gall[:], fgout[:, :], writes=[gall])
            f.dma('sp', fsl[:], fsel_in[:, :], writes=[fsl])
            psh = f.psum(es, "psh", [128, 8, 4], F32)
            for kc in range(8):
                f.op('pe', lambda e: e.matmul(psh[:, kc, :], gall[:, kc * 128:(kc + 1) * 128], fsl[:], start=True, stop=True), reads=[gall, fsl], writes=[psh])
            for seg in range(2):
                f.op('dve', lambda e: e.tensor_copy(uTe[:, :, seg, 0:1], psh[:, :, seg * 2:seg * 2 + 1]), reads=[psh], writes=[uTe])
                f.op('dve', lambda e: e.tensor_copy(uTe[:, :, seg, NL + 1:NL + 2], psh[:, :, seg * 2 + 1:seg * 2 + 2]), reads=[psh], writes=[uTe])
            cw = f.sbuf(es, "cw", [128, 44, 3], F32)
            f.dma('sp', cw[:].rearrange("p j k -> p (j k)"), I["f_conv"].h.ap()[layer], writes=[cw])
            cbias = f.sbuf(es, "cbias", [128, 44], F32)
            f.dma('sp', cbias[:], I["f_conv_b"].h.ap()[layer], writes=[cbias])
            wu32 = Ring([f.sbuf(es, "wu32", [128, 8, 128], F32) for _ in range(3)])
            wub = Ring([f.sbuf(es, "wub", [128, 8, 128], BF16) for _ in range(4)])
            psU = Ring([f.psum(es, "psU", [128, 2, 512], F32) for _ in range(3)])
            hsr = Ring([f.sbuf(es, "hs", [128, 514], F32) for _ in range(4)])
            cvr = Ring([f.sbuf(es, "cv", [128, 512], F32) for _ in range(4)])
            sgt = Ring([f.sbuf(es, "sgt", [128, 512], F32) for _ in range(2)])
            actt = Ring([f.sbuf(es, "actt", [128, 512], BF16) for _ in range(3)])
            wupv = I["f_wup"].h.ap()[layer].rearrange("(kc p) n -> p kc n", p=128)
            for j in range(22):
                wbs = []
                for ag in range(2):
                    c0 = ag * DFF + j * 128
                    w32_ = wu32.next(); wb_ = wub.next()
                    f.dma('sp', w32_[:], wupv[:, :, c0:c0 + 128], writes=[w32_])
                    f.op('pool', lambda e: e.tensor_copy(wb_[:], w32_[:]), reads=[w32_], writes=[wb_])
                    wbs.append(wb_)
                for tg in range(8):
                    seg, lg = tg // 4, tg % 4
                    cvs = []
                    for ag in range(2):
                        ps = psU.next(); hs = hsr.next(); cv = cvr.next()
                        jj = ag * 22 + j
                        e0 = lg * 512
                        for hf in range(2):
                            for kc in range(8):
                                f.op('pe', lambda e: e.matmul(ps[:, hf, 0:257], wbs[ag][:, kc, :], uTe[:, kc, seg, e0 + hf * 257:e0 + hf * 257 + 257], start=(kc == 0), stop=(kc == 7)),
                                     reads=[wbs[ag], uTe], writes=[ps])
                        f.op('act', lambda e: e.copy(hs[:].rearrange("p (a b) -> p a b", a=2), ps[:, :, 0:257]), reads=[ps], writes=[hs])
                        E = 'dve'
                        f.op(E, lambda e: e.tensor_scalar(cv[:], hs[:, 0:512], cw[:, jj, 0:1], cbias[:, jj:jj + 1], ALU.mult, ALU.add),
                             reads=[hs, cw, cbias], writes=[cv])
                        f.op(E, lambda e: e.scalar_tensor_tensor(cv[:], hs[:, 1:513], cw[:, jj, 1:2], cv[:], ALU.mult, ALU.add),
                             reads=[hs, cw, cv], writes=[cv])
                        f.op(E, lambda e: e.scalar_tensor_tensor(cv[:], hs[:, 2:514], cw[:, jj, 2:3], cv[:], ALU.mult, ALU.add),
                             reads=[hs, cw, cv], writes=[cv])
                        cvs.append(cv)
                    sg_ = sgt.next(); ac_ = actt.next()
                    f.op('act', lambda e: e.activation(out=sg_[:], in_=cvs[1][:], func=AF.Silu), reads=[cvs[1]], writes=[sg_])
                    f.op('dve', lambda e: e.tensor_tensor(ac_[:], cvs[0][:], sg_[:], ALU.mult), reads=[cvs[0], sg_], writes=[ac_])
                    f.dma('act', hact[j * 128:(j + 1) * 128, tg * 512:(tg + 1) * 512], ac_[:], reads=[ac_])
            f.barrier()
        with ExitStack() as es:
            wD = f.sbuf(es, "wD", [128, 22, D], BF16)
            with ExitStack() as est:
                tmpw = Ring([f.sbuf(est, "tmpwd", [128, 512], F32) for _ in range(3)])
                sv = I["f_wdown"].h.ap()[layer].rearrange("(kc p) n -> p kc n", p=128)
                i = 0
                for kc in range(22):
                    for c0 in (0, 512):
                        tw = tmpw.next()
                        f.dma('sp', tw[:], sv[:, kc, c0:c0 + 512], writes=[tw])
                        E = ('dve', 'pool')[i % 2]; i += 1
                        f.op(E, lambda e: e.tensor_copy(wD[:, kc, c0:c0 + 512], tw[:]), reads=[tw], writes=[wD])
                f.barrier()
            hat = Ring([f.sbuf(es, "hat", [128, 22, 128], BF16) for _ in range(2)])
            xm = Ring([f.sbuf(es, "xm", [128, D], F32) for _ in range(2)])
            xo3 = Ring([f.sbuf(es, "xo3", [128, D], F32) for _ in range(2)])
            pD = Ring([f.psum(es, "pD", [128, 512], F32) for _ in range(4)])
            dst = xres if layer == 0 else y_out
            for tt in range(NT):
                rows = slice(tt * 128, (tt + 1) * 128)
                ha = hat.next(); xm_ = xm.next(); xo_ = xo3.next()
                f.dma('sp', ha[:], hact.h.ap().rearrange("(j p) t -> p j t", p=128)[:, :, rows], writes=[ha])
                f.dma('sp', xm_[:], xmid[rows, :], writes=[xm_])
                for cbk in range(2):
                    cs = slice(cbk * 512, (cbk + 1) * 512)
                    ps = pD.next()
                    for j in range(22):
                        f.op('pe', lambda e: e.matmul(ps[:], ha[:, j, :], wD[:, j, cs], start=(j == 0), stop=(j == 21)), reads=[ha, wD], writes=[ps])
                    f.op('dve', lambda e: e.tensor_tensor(xo_[:, cs], ps[:], xm_[:, cs], ALU.add), reads=[ps, xm_], writes=[xo_])
                f.dma('act', dst[rows, :], xo_[:], reads=[xo_])
            f.barrier()
    f.barrier()
    print("ninst", f.ninst, "nwaits", f.nwaits)
    top.close()
    return nc


BUILD_UPTO = 99
DEBUG_OUT = set()


def _consts(c):
    bf = ml_dtypes.bfloat16
    half = c % 2
    out = {}
    inv = np.power(np.float32(10000.0), -(np.arange(0, 64, 2, dtype=np.float32) / np.float32(64))).astype(np.float32)
    tabs = np.zeros((NTOK, 384), np.float32)
    for seg in range(2):
        t = (half * NL if seg == 0 else c * NL) + np.arange(NL)
        row = (t // 64).astype(np.float32); col = (t % 64).astype(np.float32)
        ar = row[:, None] * inv[None, :]; ac = col[:, None] * inv[None, :]; at = t.astype(np.float32)[:, None] * inv[None, :]
        cr, sr, cc, sc, ct, st = np.cos(ar), np.sin(ar), np.cos(ac), np.sin(ac), np.cos(at), np.sin(at)
        sl = slice(seg * NL, (seg + 1) * NL)
        tabs[sl, 0:128] = np.concatenate([cr, cr, cc, cc], 1)
        tabs[sl, 128:256] = np.concatenate([-sr, sr, -sc, sc], 1)
        tabs[sl, 256:320] = np.concatenate([ct, ct], 1)
        tabs[sl, 320:384] = np.concatenate([-st, st], 1)
    out["tabs"] = tabs
    out["ident"] = np.eye(128, dtype=np.float32).astype(bf)
    src = {}
    src[(0, 0)] = c - 1 if half == 1 else None
    src[(0, 1)] = c + 1 if half == 0 else None
    src[(1, 0)] = c - 1 if c > 0 else None
    src[(1, 1)] = c + 1 if c < 7 else None
    selI = np.zeros((128, 32, 128), np.float32)
    for (seg, side), r in src.items():
        if r is not None:
            selI[:, (seg * 2 + side) * 8 + r, :] = np.eye(128, dtype=np.float32)
    out["selI"] = selI.reshape(128, 32 * 128).astype(bf)
    bm = np.zeros((128, 34, 512), np.float32)
    p = np.arange(128)[:, None]; qi = np.arange(512)[None, :]
    ti = 0
    for g, (ntile, off, dil) in enumerate(((6, -128, 1), (8, -256, 4), (20, -1024, 16))):
        for rel in range(ntile):
            dlt = rel * 128 + p - qi + off
            bm[:, ti, :] = ((np.abs(dlt) <= 64 * dil) & (dlt % dil == 0)).astype(np.float32)
            ti += 1
    out["bmask"] = bm.reshape(128, 34 * 512).astype(bf)
    bv = np.zeros((128, 2, 32), np.float32)
    for seg in range(2):
        t0 = half * NL if seg == 0 else c * NL
        e = np.arange(32)[None, :] * 128 + np.arange(128)[:, None]
        pos = t0 + e - 1024
        bv[:, seg, :] = ((pos >= 0) & (pos < SEG_S[seg])).astype(np.float32)
    out["bvalid"] = bv.reshape(128, 64)
    cm = np.zeros((128, 2, 128), np.float32)
    s = np.arange(128)[:, None]; t = np.arange(128)[None, :]
    same = (s // 64) == (t // 64)
    cm[:, 0, :] = (same & (s <= t)).astype(np.float32)
    cm[:, 1, :] = (same & (s >= t)).astype(np.float32)
    out["cmask"] = cm.reshape(128, 256).astype(bf)
    cs = np.zeros((128, 16), np.float32)
    cs[:, c] = 1.0; cs[:, 8 + c] = 1.0
    out["cscan"] = cs
    chm = np.ones((128, NL), np.float32); chm[:, ::64] = 0.0
    out["chunkm"] = chm
    fs = np.zeros((32, 4), np.float32)
    for (seg, side), r in src.items():
        if r is not None:
            fs[r * 4 + seg * 2 + (1 if side == 0 else 0), seg * 2 + side] = 1.0
    out["fsel"] = fs.astype(bf)
    return out


def make_in_maps(inputs):
    maps = []
    wnames = ["g_mix", "w_in", "a_gq", "a_gk", "b_gq", "b_gk", "c_lb_fwd", "c_lb_bwd", "c_gnorm", "w_br_a", "w_br_b",
              "w_br_c", "w_mix_out", "g_cross", "g_mem", "x_wq", "x_wkv", "x_gq", "x_gk", "x_wo", "g_ffn", "f_wup",
              "f_conv", "f_conv_b", "f_wdown"]
    shared = {}
    for n in wnames:
        a = np.ascontiguousarray(inputs[n], dtype=np.float32)
        if n in ("b_gq", "b_gk"):
            a = a.reshape(2, 192)
        if n in ("c_lb_fwd", "c_lb_bwd"):
            a = np.ascontiguousarray(a.reshape(2, 8, 128).transpose(0, 2, 1))
        if n == "f_conv":
            a = np.ascontiguousarray(a.reshape(2, 3, 44, 128).transpose(0, 3, 2, 1).reshape(2, 128, 132))
        if n == "f_conv_b":
            a = np.ascontiguousarray(a.reshape(2, 44, 128).transpose(0, 2, 1))
        shared[n] = a
    xp = inputs["x_prompt"]; xs = inputs["x_sample"]
    for c in range(NCORES):
        b, half = c // 2, c % 2
        m = dict(shared)
        m["x"] = np.ascontiguousarray(np.concatenate([xp[b, half * NL:(half + 1) * NL], xs[0, c * NL:(c + 1) * NL]], 0), dtype=np.float32)
        m["mem"] = np.ascontiguousarray(np.concatenate([inputs["mem_prompt"][b], inputs["mem_sample"][0]], 0), dtype=np.float32)
        m.update(_consts(c))
        maps.append(m)
    return maps


_NC = None


def kernel(**inputs):
    global _NC
    if _NC is None:
        _NC = build()
    maps = make_in_maps(inputs)
    res = run_bass_kernel_spmd(_NC, maps, core_ids=list(range(NCORES)))
    yp = np.zeros((4, 4096, D), np.float32)
    ys = np.zeros((1, 16384, D), np.float32)
    for c in range(NCORES):
        y = res.results[c]["y"]
        b, half = c // 2, c % 2
        yp[b, half * NL:(half + 1) * NL] = y[0:NL]
        ys[0, c * NL:(c + 1) * NL] = y[NL:]
    return (yp, ys)
```
